# Optimizing a Trainium2 kernel written in Bass

```python
import math
import jax, jax.numpy as jnp
from jax import lax
import numpy as np

D_MODEL = 1024
BATCH = 16
SEQ = 4096
DEPTH = 4
DEC_BATCH = 16
DEC_SEQ = 2048
PAST_LEN = 128

D_MIX = D_MODEL
D_ATTN = D_MIX // 2
D_HY = D_MIX - D_ATTN
HEAD_DIM = 64
N_Q_HEADS = D_ATTN // HEAD_DIM
N_KV_HEADS = 2
GQA_GROUP = N_Q_HEADS // N_KV_HEADS
KV_WIDTH = N_KV_HEADS * HEAD_DIM
ROT_DIM = HEAD_DIM // 4
ROPE_THETA = 500000.0
WINDOW = 128
BLOCK = 128
HY_ORDER = 2
SHORT_K = 3
FILTER_EMB = 33
FILTER_HID = 64
DECAY_TARGET = 1e-2
FAST_DECAY_PCT = 0.3
SLOW_DECAY_PCT = 1.5
D_FF = 2816
N_IN = D_ATTN + 2 * KV_WIDTH + (HY_ORDER + 1) * D_HY
N_MOD = 9
ALPHA = float((2 * DEPTH) ** 0.25)
BETA = float((8 * DEPTH) ** -0.25)
LN_EPS = 1e-5
RMS_EPS = 1e-6

kernel_name = "hymba_attn_hyena_macaron_deepnorm_encoder"


def _layer_norm(x, g, b):
    xf = x.astype(jnp.float32)
    mu = xf.mean(-1, keepdims=True)
    var = jnp.square(xf - mu).mean(-1, keepdims=True)
    return ((xf - mu) * lax.rsqrt(var + LN_EPS) * g.astype(jnp.float32) + b.astype(jnp.float32)).astype(x.dtype)


def _rms_norm(x, g):
    xf = x.astype(jnp.float32)
    ms = jnp.square(xf).mean(-1, keepdims=True)
    return (xf * lax.rsqrt(ms + RMS_EPS) * g.astype(jnp.float32)).astype(x.dtype)


def _swiglu(h, wi, wo):
    g, u = jnp.split(h @ wi, 2, axis=-1)
    return (jax.nn.silu(g) * u) @ wo


def _partial_rope(x, L):
    inv = ROPE_THETA ** (-jnp.arange(0, ROT_DIM, 2, dtype=jnp.float32) / ROT_DIM)
    ang = jnp.arange(L, dtype=jnp.float32)[:, None] * inv[None]
    cos = jnp.cos(ang)[None, :, None, :]
    sin = jnp.sin(ang)[None, :, None, :]
    xr = x[..., :ROT_DIM].astype(jnp.float32)
    x1, x2 = xr[..., : ROT_DIM // 2], xr[..., ROT_DIM // 2:]
    rot = jnp.concatenate([x1 * cos - x2 * sin, x2 * cos + x1 * sin], axis=-1)
    return jnp.concatenate([rot.astype(x.dtype), x[..., ROT_DIM:]], axis=-1)


def _window_attention(q, k, v, sink):
    B, L = q.shape[0], q.shape[1]
    nb = L // BLOCK
    pad = ((0, 0), (BLOCK, BLOCK), (0, 0), (0, 0))

    def bands(t):
        tb = jnp.pad(t, pad).reshape(B, nb + 2, BLOCK, N_KV_HEADS, HEAD_DIM)
        return jnp.concatenate([tb[:, :-2], tb[:, 1:-1], tb[:, 2:]], axis=2)

    kb, vb = bands(k), bands(v)
    qb = q.reshape(B, nb, BLOCK, N_KV_HEADS, GQA_GROUP, HEAD_DIM)
    s = jnp.einsum('bnqkgd,bnskd->bnkgqs', qb, kb, preferred_element_type=jnp.float32) * (HEAD_DIM ** -0.5)
    qpos = jnp.arange(nb)[:, None] * BLOCK + jnp.arange(BLOCK)[None]
    kpos = (jnp.arange(nb)[:, None] - 1) * BLOCK + jnp.arange(3 * BLOCK)[None]
    valid = ((jnp.abs(qpos[:, :, None] - kpos[:, None, :]) <= WINDOW)
             & (kpos[:, None, :] >= 0) & (kpos[:, None, :] < L))
    s = jnp.where(valid[None, :, None, None], s, -jnp.inf)
    sink_f = sink.astype(jnp.float32).reshape(N_KV_HEADS, GQA_GROUP)[None, None, :, :, None, None]
    m = jnp.maximum(s.max(-1, keepdims=True), sink_f)
    p = jnp.exp(s - m)
    p = p / (p.sum(-1, keepdims=True) + jnp.exp(sink_f - m))
    o = jnp.einsum('bnkgqs,bnskd->bnqkgd', p.astype(v.dtype), vb)
    return o.reshape(B, L, N_Q_HEADS * HEAD_DIM)


def _short_conv(u, w, b):
    up = jnp.pad(u, ((0, 0), (1, 1), (0, 0)))
    return up[:, :-2] * w[0] + up[:, 1:-1] * w[1] + up[:, 2:] * w[2] + b


def _hyena_filters(L, w1, b1, w2, b2, w3, b3, w4, freq, decay):
    f32 = jnp.float32
    t = jnp.linspace(0.0, 1.0, L, dtype=f32)[:, None]
    n_bands = (FILTER_EMB - 1) // 2
    w = 2.0 * math.pi * jnp.arange(L, dtype=f32)[:, None] / L
    fb = jnp.linspace(1e-4, n_bands - 1, n_bands, dtype=f32)[None]
    z = jnp.concatenate([t, jnp.cos(fb * w), -jnp.sin(fb * w)], axis=-1)
    fr = freq.astype(f32)
    h = jnp.sin(fr * (z @ w1.astype(f32) + b1.astype(f32)))
    h = jnp.sin(fr * (h @ w2.astype(f32) + b2.astype(f32)))
    h = jnp.sin(fr * (h @ w3.astype(f32) + b3.astype(f32)))
    h = (h @ w4.astype(f32)).reshape(L, 2, D_HY)
    window = jnp.exp(-t[:, :, None] * jnp.abs(decay.astype(f32))[None])
    return h * window


def _bidir_long_conv(u, filt, bias):
    L = u.shape[1]
    hf, hb = filt[:, 0], filt[:, 1]
    kern = jnp.concatenate([hf[:1] + hb[:1], hf[1:], jnp.zeros((1, D_HY), jnp.float32), hb[:0:-1]], axis=0)
    uf = u.astype(jnp.float32)
    U = jnp.fft.rfft(uf, n=2 * L, axis=1)
    K = jnp.fft.rfft(kern, n=2 * L, axis=0)
    y = jnp.fft.irfft(U * K[None], n=2 * L, axis=1)[:, :L]
    return (y + uf * bias.astype(jnp.float32)).astype(u.dtype)


def _modulate(x, shift, scale):
    return x * (1.0 + scale[:, None, :]) + shift[:, None, :]


def _trunk(x, c, ada_w, ada_b, ffn1_wi, ffn1_wo, ffn2_wi, ffn2_wo, ln_g, ln_b, w_in, w_out, sink,
           grp_norm_g, hy_conv_w, hy_conv_b, hy_w1, hy_b1, hy_w2, hy_b2, hy_w3, hy_b3, hy_w4,
           hy_freq, hy_decay, hy_bias):
    B, L = x.shape[0], x.shape[1]
    for l in range(DEPTH):
        mod = (jax.nn.silu(c) @ ada_w[l] + ada_b[l]).reshape(B, N_MOD, D_MODEL)
        h = _modulate(x, mod[:, 0], mod[:, 1])
        f = _swiglu(h, ffn1_wi[l], ffn1_wo[l])
        x = _layer_norm(ALPHA * x + 0.5 * (1.0 + mod[:, 2][:, None, :]) * f, ln_g[l, 0], ln_b[l, 0])
        h = _modulate(x, mod[:, 3], mod[:, 4])
        z = h @ w_in[l]
        q, k, v, hz = jnp.split(z, [D_ATTN, D_ATTN + KV_WIDTH, D_ATTN + 2 * KV_WIDTH], axis=-1)
        q = _partial_rope(q.reshape(B, L, N_Q_HEADS, HEAD_DIM), L)
        k = _partial_rope(k.reshape(B, L, N_KV_HEADS, HEAD_DIM), L)
        v = v.reshape(B, L, N_KV_HEADS, HEAD_DIM)
        o_attn = _window_attention(q, k, v, sink[l])
        hz = _short_conv(hz, hy_conv_w[l], hy_conv_b[l])
        x0, x1, hv = jnp.split(hz, 3, axis=-1)
        filt = _hyena_filters(L, hy_w1[l], hy_b1[l], hy_w2[l], hy_b2[l], hy_w3[l], hy_b3[l], hy_w4[l],
                              hy_freq[l], hy_decay[l])
        o_hy = _bidir_long_conv(hv * x1, filt, hy_bias[l]) * x0
        o = jnp.concatenate([_rms_norm(o_attn, grp_norm_g[l, :D_ATTN]),
                             _rms_norm(o_hy, grp_norm_g[l, D_ATTN:])], axis=-1) @ w_out[l]
        x = _layer_norm(ALPHA * x + (1.0 + mod[:, 5][:, None, :]) * o, ln_g[l, 1], ln_b[l, 1])
        h = _modulate(x, mod[:, 6], mod[:, 7])
        f = _swiglu(h, ffn2_wi[l], ffn2_wo[l])
        x = _layer_norm(ALPHA * x + 0.5 * (1.0 + mod[:, 8][:, None, :]) * f, ln_g[l, 2], ln_b[l, 2])
    return x


def setup_inputs(seed: int = 0) -> dict:
    key = jax.random.key(seed)
    ks = jax.random.split(key, 32)
    f32 = jnp.float32
    n = lambda i, shape, s: jax.random.normal(ks[i], shape, f32) * s
    base_decay = jnp.abs(jnp.linspace(math.log(DECAY_TARGET) / FAST_DECAY_PCT,
                                      math.log(DECAY_TARGET) / SLOW_DECAY_PCT, D_HY, dtype=f32))
    return {
        "x_prompt": n(0, (BATCH, SEQ, D_MODEL), 1.0),
        "x_sample": n(1, (DEC_BATCH, DEC_SEQ, D_MODEL), 1.0),
        "c_prompt": n(2, (BATCH, D_MODEL), 1.0),
        "c_sample": n(3, (DEC_BATCH, D_MODEL), 1.0),
        "ada_w": n(4, (DEPTH, D_MODEL, N_MOD * D_MODEL), 0.5 * D_MODEL ** -0.5),
        "ada_b": n(5, (DEPTH, N_MOD * D_MODEL), 0.01),
        "ffn1_wi": n(6, (DEPTH, D_MODEL, 2 * D_FF), D_MODEL ** -0.5),
        "ffn1_wo": n(7, (DEPTH, D_FF, D_MODEL), BETA * D_FF ** -0.5),
        "ffn2_wi": n(8, (DEPTH, D_MODEL, 2 * D_FF), D_MODEL ** -0.5),
        "ffn2_wo": n(9, (DEPTH, D_FF, D_MODEL), BETA * D_FF ** -0.5),
        "ln_g": 1.0 + n(10, (DEPTH, 3, D_MODEL), 0.02),
        "ln_b": n(11, (DEPTH, 3, D_MODEL), 0.02),
        "w_in": n(12, (DEPTH, D_MODEL, N_IN), D_MODEL ** -0.5),
        "w_out": n(13, (DEPTH, D_MIX, D_MODEL), BETA * D_MIX ** -0.5),
        "sink": n(14, (DEPTH, N_Q_HEADS), 1.0),
        "grp_norm_g": 1.0 + n(15, (DEPTH, D_MIX), 0.02),
        "hy_conv_w": n(16, (DEPTH, SHORT_K, (HY_ORDER + 1) * D_HY), SHORT_K ** -0.5),
        "hy_conv_b": n(17, (DEPTH, (HY_ORDER + 1) * D_HY), 0.02),
        "hy_w1": n(18, (DEPTH, FILTER_EMB, FILTER_HID), FILTER_EMB ** -0.5),
        "hy_b1": n(19, (DEPTH, FILTER_HID), 0.02),
        "hy_w2": n(20, (DEPTH, FILTER_HID, FILTER_HID), FILTER_HID ** -0.5),
        "hy_b2": n(21, (DEPTH, FILTER_HID), 0.02),
        "hy_w3": n(22, (DEPTH, FILTER_HID, FILTER_HID), FILTER_HID ** -0.5),
        "hy_b3": n(23, (DEPTH, FILTER_HID), 0.02),
        "hy_w4": n(24, (DEPTH, FILTER_HID, 2 * D_HY), FILTER_HID ** -0.5),
        "hy_freq": 1.0 + n(25, (DEPTH, FILTER_HID), 0.05),
        "hy_decay": base_decay[None, None, :] * (1.0 + n(26, (DEPTH, 2, D_HY), 0.05)),
        "hy_bias": n(27, (DEPTH, D_HY), 0.5),
    }


def reference(x_prompt, x_sample, c_prompt, c_sample, ada_w, ada_b, ffn1_wi, ffn1_wo, ffn2_wi, ffn2_wo,
              ln_g, ln_b, w_in, w_out, sink, grp_norm_g, hy_conv_w, hy_conv_b, hy_w1, hy_b1, hy_w2, hy_b2,
              hy_w3, hy_b3, hy_w4, hy_freq, hy_decay, hy_bias):
    y_prompt = _trunk(x_prompt, c_prompt, ada_w, ada_b, ffn1_wi, ffn1_wo, ffn2_wi, ffn2_wo, ln_g, ln_b,
                      w_in, w_out, sink, grp_norm_g, hy_conv_w, hy_conv_b, hy_w1, hy_b1, hy_w2, hy_b2,
                      hy_w3, hy_b3, hy_w4, hy_freq, hy_decay, hy_bias)
    y_sample = _trunk(x_sample, c_sample, ada_w, ada_b, ffn1_wi, ffn1_wo, ffn2_wi, ffn2_wo, ln_g, ln_b,
                      w_in, w_out, sink, grp_norm_g, hy_conv_w, hy_conv_b, hy_w1, hy_b1, hy_w2, hy_b2,
                      hy_w3, hy_b3, hy_w4, hy_freq, hy_decay, hy_bias)
    return (y_prompt, y_sample)
```

```python
import math
from contextlib import ExitStack
import numpy as np
import ml_dtypes
import concourse.bass as bass
import concourse.mybir as mybir
from concourse.bass_utils import run_bass_kernel_spmd

F32 = mybir.dt.float32
BF16 = mybir.dt.bfloat16
AF = mybir.ActivationFunctionType
ALU = mybir.AluOpType
AX = mybir.AxisListType

D = 1024
DEPTH = 4
DFF = 2816
NIN = 2304
ALPHA = float((2 * DEPTH) ** 0.25)
LN_EPS = 1e-5 / (ALPHA * ALPHA)
RMS_EPS = 1e-6
NCORES = 8
SEQS = [(0, 4096), (4096, 4096), (8192, 2048), (10240, 2048)]
T = 12288
LS = [4096, 2048]
PI = math.pi

RUN_LAYERS = DEPTH
FFN_DEBUG = 4
STOP_AFTER = None


def seq_of(tok):
    for i, (s0, ln) in enumerate(SEQS):
        if s0 <= tok < s0 + ln:
            return i, s0, ln
    raise ValueError


class Sched:
    NDMA = 8

    def __init__(self, nc, stack):
        self.nc = nc
        self.E = {'pe': nc.tensor, 'act': nc.scalar, 'dve': nc.vector, 'pool': nc.gpsimd, 'sp': nc.sync}
        self.sem = {e: stack.enter_context(nc.semaphore(f"s_{e}")) for e in ('pe', 'act', 'dve', 'pool')}
        self.cnt = {e: 0 for e in self.sem}
        self.nd = {'sp': 3, 'pool': 2}
        self.dsem = {q: [stack.enter_context(nc.semaphore(f"d_{q}{i}")) for i in range(self.nd[q])]
                     for q in ('sp', 'pool')}
        self.dcnt = {q: 0 for q in self.dsem}
        self.known = {e: {} for e in self.E}
        self.res = {}
        self.nwait = 0

    PS_NAMES = frozenset(['pt', 'pm', 'pst', 'gu', 'pf', 'psq', 'psr', 'psv', 'psz', 'ptr', 'pS', 'pO', 'pa', 'px',
                          'pb', 'py'])

    def _isp(self, k):
        if isinstance(k, tuple):
            return any(isinstance(x, str) and x in self.PS_NAMES for x in k)
        return k in self.PS_NAMES

    def _split(self, reads, writes):
        r2 = [k for k in reads if not self._isp(k)]
        w2 = list(writes) + [k for k in reads if self._isp(k)]
        return r2, w2

    def _wait(self, e, tok):
        if tok is None:
            return
        sem, val = tok
        k = self.known[e]
        sid = id(sem)
        if k.get(sid, 0) >= val:
            return
        self.E[e].wait_ge(sem, val)
        self.nwait += 1
        k[sid] = val

    def deps(self, e, reads, writes):
        for r in reads:
            st = self.res.get(r)
            if st is not None:
                self._wait(e, st[0])
        for w in writes:
            st = self.res.get(w)
            if st is not None:
                self._wait(e, st[0])
                for t in st[1]:
                    self._wait(e, t)

    def commit(self, tok, reads, writes):
        for r in reads:
            st = self.res.setdefault(r, [None, []])
            st[1].append(tok)
            if len(st[1]) > 10:
                d = {}
                for s, v in st[1]:
                    if id(s) not in d or d[id(s)][1] < v:
                        d[id(s)] = (s, v)
                st[1] = list(d.values())
        for w in writes:
            self.res[w] = [tok, []]

    def op(self, e, fn, reads=(), writes=()):
        reads, writes = self._split(reads, writes)
        self.deps(e, reads, writes)
        ins = fn()
        self.cnt[e] += 1
        ins.then_inc(self.sem[e], 1)
        tok = (self.sem[e], self.cnt[e])
        if e == 'pe':
            self.known[e][id(self.sem[e])] = self.cnt[e]
        self.commit(tok, reads, writes)
        return tok

    def pe(self, fns, reads=(), writes=()):
        reads, writes = self._split(reads, writes)
        self.deps('pe', reads, writes)
        ins = None
        for fn in fns:
            ins = fn()
        self.cnt['pe'] += 1
        ins.then_inc(self.sem['pe'], 1)
        tok = (self.sem['pe'], self.cnt['pe'])
        self.known['pe'][id(self.sem['pe'])] = self.cnt['pe']
        self.commit(tok, reads, writes)
        return tok

    def act(self, fn, r=(), w=()):
        return self.op('act', fn, r, w)

    def dve(self, fn, r=(), w=()):
        return self.op('dve', fn, r, w)

    def pool(self, fn, r=(), w=()):
        return self.op('pool', fn, r, w)

    def dma(self, q, out, in_, reads=(), writes=()):
        self.deps(q, reads, writes)
        j = self.dcnt[q]
        self.dcnt[q] += 1
        sem = self.dsem[q][j % self.nd[q]]
        rnd = j // self.nd[q]
        if rnd > 0:
            self._wait(q, (sem, 16 * rnd))
        self.E[q].dma_start(out=out, in_=in_).then_inc(sem, 16)
        tok = (sem, 16 * (rnd + 1))
        self.commit(tok, reads, writes)
        return tok

    def _all_tokens(self):
        toks = []
        for q in self.dsem:
            j = self.dcnt[q]
            for i in range(min(j, self.nd[q])):
                last = j - 1 - i
                toks.append((self.dsem[q][last % self.nd[q]], 16 * (last // self.nd[q] + 1)))
        for x in self.sem:
            if self.cnt[x] > 0:
                toks.append((self.sem[x], self.cnt[x]))
        return toks

    def barrier(self):
        toks = self._all_tokens()
        for e in self.E:
            for t in toks:
                self._wait(e, t)
        self.res = {}

    def finish(self):
        for t in self._all_tokens():
            self._wait('sp', t)


def fft_consts(L):
    N = 2 * L
    N1 = N // 128
    H1 = N1 // 2
    nk1 = H1 + 1
    n1 = np.arange(H1)[:, None]
    k1 = np.arange(nk1)[None, :]
    ang = 2 * np.pi * n1 * k1 / N1
    FA = np.concatenate([np.cos(ang), -np.sin(ang)], axis=1)
    w = np.full(nk1, 2.0)
    w[0] = 1
    w[-1] = 1
    GA = np.concatenate([(w[None, :] * np.cos(ang)).T, (-w[None, :] * np.sin(ang)).T], axis=0)
    n2 = np.arange(128)[:, None]
    k2 = np.arange(128)[None, :]
    Ms = []
    for kk in range(nk1):
        th = 2 * np.pi * ((n2 * (kk + N1 * k2)) % N) / N
        Mre = np.cos(th)
        Mim = -np.sin(th)
        Ms.append(np.stack([Mre, Mim, -Mim, Mre.T, Mim.T, -Mim.T], axis=1))
    M = np.stack(Ms, axis=0)
    return FA, GA, M


def host_consts():
    bf = ml_dtypes.bfloat16
    c = {}
    c["ident_f"] = np.eye(128, dtype=np.float32)
    c["ident_b"] = np.eye(128, dtype=np.float32).astype(bf)
    c["ones_b"] = np.full((128, 128), 1.0 / D, dtype=np.float32).astype(bf)
    rt = np.zeros((64, 16), np.float32)
    for i in range(8):
        rt[i + 8, i] = -1.0
        rt[i, i + 8] = 1.0
    c["rt"] = rt.astype(bf)
    inv = 500000.0 ** (-np.arange(0, 16, 2, dtype=np.float64) / 16)
    ang = np.arange(4096, dtype=np.float64)[None, :] * np.concatenate([inv, inv])[:, None]
    c["cos_t"] = np.cos(ang).astype(np.float32)
    c["sin_t"] = np.sin(ang).astype(np.float32)
    s = np.arange(128)[:, None]
    q = np.arange(128)[None, :]
    c["maskp"] = (s >= q).astype(np.float32).astype(bf)
    c["maskn"] = (s <= q).astype(np.float32).astype(bf)
    for i, L in enumerate(LS):
        FA, GA, M = fft_consts(L)
        c[f"fa{i}"] = FA.astype(np.float32).astype(bf)
        c[f"ga{i}"] = GA.astype(np.float32).astype(bf)
        c[f"mm{i}"] = M.astype(np.float32).astype(bf)
        t = np.linspace(0.0, 1.0, L, dtype=np.float32).astype(np.float64)
        wv = 2.0 * np.pi * np.arange(L, dtype=np.float64) / L
        fb = np.linspace(1e-4, 15, 16, dtype=np.float32).astype(np.float64)[None]
        z = np.concatenate([t[:, None], np.cos(fb * wv[:, None]), -np.sin(fb * wv[:, None])], axis=-1)
        c[f"zt{i}"] = np.ascontiguousarray(z.T).astype(np.float32)
        c[f"negt{i}"] = np.ascontiguousarray((-t).reshape(L // 128, 128).T).astype(np.float32)
    return c


def build():
    nc = bass.Bass("TRN2", target_bir_lowering=False)
    DEPTH = RUN_LAYERS
    consts = host_consts()

    def din(name, shape, dt=F32):
        return nc.dram_tensor(name, list(shape), dt, kind="ExternalInput").ap()

    def dscr(name, shape, dt=F32):
        return nc.dram_tensor(name, list(shape), dt).ap()

    xin = din("xin", [T, D])
    cin = din("cin", [4, D])
    ada_w = din("ada_w", [DEPTH, D, 9 * D])
    ada_b = din("ada_b", [DEPTH, 9 * D])
    ffn_wi = [din("ffn1_wi", [DEPTH, D, 2 * DFF]), din("ffn2_wi", [DEPTH, D, 2 * DFF])]
    ffn_wo = [din("ffn1_wo", [DEPTH, DFF, D]), din("ffn2_wo", [DEPTH, DFF, D])]
    ln_g = din("ln_g", [DEPTH, 3, D])
    ln_b = din("ln_b", [DEPTH, 3, D])
    w_in = din("w_in", [DEPTH, D, NIN])
    w_out = din("w_out", [DEPTH, D, D])
    sink = din("sink", [DEPTH, 8])
    grp_g = din("grp_norm_g", [DEPTH, D])
    hy_cw = din("hy_conv_w", [DEPTH, 3, 1536])
    hy_cb = din("hy_conv_b", [DEPTH, 1536])
    hy_w1 = din("hy_w1", [DEPTH, 33, 64])
    hy_b1 = din("hy_b1", [DEPTH, 64])
    hy_w2 = din("hy_w2", [DEPTH, 64, 64])
    hy_b2 = din("hy_b2", [DEPTH, 64])
    hy_w3 = din("hy_w3", [DEPTH, 64, 64])
    hy_b3 = din("hy_b3", [DEPTH, 64])
    hy_w4 = din("hy_w4", [DEPTH, 64, 1024])
    hy_freq = din("hy_freq", [DEPTH, 64])
    hy_decay = din("hy_decay", [DEPTH, 2, 512])
    hy_bias = din("hy_bias", [DEPTH, 512])
    cd = {}
    for k, v in consts.items():
        cd[k] = din("c_" + k, v.shape, BF16 if v.dtype == ml_dtypes.bfloat16 else F32)
    yout = nc.dram_tensor("yout", [T, D], F32, kind="ExternalOutput").ap()

    xT_d = dscr("xT_d", [D, T])
    q_d = dscr("q_d", [10, 64, T], BF16)
    v_d = dscr("v_d", [T, 130], BF16)
    u_d = dscr("u_d", [T, 512], BF16)
    x0_d = dscr("x0_d", [T, 512], BF16)
    y_d = dscr("y_d", [T, 512], F32)
    oT_d = dscr("oT_d", [D, T], BF16)
    filt_d = [dscr(f"filt_d{i}", [2, L, 512], BF16) for i, L in enumerate(LS)]
    NK1 = [L // 128 + 1 for L in LS]
    A_d = [dscr(f"A_d{i}", [2, 2 * NK1[i], 128, 512], BF16) for i in range(2)]
    B_d = [dscr(f"B_d{i}", [2, 2 * NK1[i], 128, 512], BF16) for i in range(2)]
    Kf_d = [dscr(f"Kf_d{i}", [NK1[i], 128, 2, 512], F32) for i in range(2)]

    xT_v = xT_d.rearrange("(k p) t -> p k t", p=128)
    oT_v = oT_d.rearrange("(k p) t -> p k t", p=128)

    top = ExitStack()
    with top:
        S = Sched(nc, top)
        uid = [0]

        def sb(st, shape, dt, name=None):
            uid[0] += 1
            return st.enter_context(nc.sbuf_tensor(f"{name or 't'}_{uid[0]}", list(shape), dt))

        def ps(st, shape, dt=F32, name=None):
            uid[0] += 1
            return st.enter_context(nc.psum_tensor(f"{name or 'p'}_{uid[0]}", list(shape), dt))

        V, G, A, P = nc.vector, nc.gpsimd, nc.scalar, nc.tensor

        ident_f = sb(top, [128, 128], F32, "identf")
        ident_b = sb(top, [128, 128], BF16, "identb")
        ones_b = sb(top, [128, 128], BF16, "onesb")
        rt_sb = sb(top, [64, 16], BF16, "rt")
        maskp = sb(top, [128, 128], BF16, "maskp")
        maskn = sb(top, [128, 128], BF16, "maskn")
        scT = sb(top, [128, 8, 4], F32, "scT")
        scT_b = sb(top, [128, 8, 4], BF16, "scTb")
        modsb = sb(top, [128, 72, 4], F32, "modsb")
        adab = sb(top, [128, DEPTH, 72], F32, "adab")
        lng = sb(top, [128, DEPTH * 3, 8], F32, "lng")
        nlng = sb(top, [128, DEPTH * 3, 8], F32, "nlng")
        lnb = sb(top, [128, DEPTH * 3, 8], F32, "lnb")
        cw = sb(top, [128, DEPTH * 3, 12], F32, "cw")
        cbs = sb(top, [128, DEPTH, 12], F32, "cbs")
        nsink = sb(top, [128, DEPTH * 8], F32, "nsink")
        eps_ln = sb(top, [128, 1], F32, "epsln")
        S.dve(lambda: V.memset(eps_ln[:], LN_EPS), [], ['epsln'])

        with nc.allow_non_contiguous_dma(reason="tiny one-time parameter loads"):
            S.dma('sp', ident_f[:], cd["ident_f"], writes=['c0'])
            S.dma('sp', ident_b[:], cd["ident_b"], writes=['c1'])
            S.dma('sp', ones_b[:], cd["ones_b"], writes=['c2'])
            S.dma('sp', rt_sb[:], cd["rt"], writes=['c3'])
            S.dma('sp', maskp[:], cd["maskp"], writes=['c4'])
            S.dma('sp', maskn[:], cd["maskn"], writes=['c5'])
            for s_ in range(4):
                S.dma('sp', scT[:, :, s_], cin[s_].rearrange("(k p) -> p k", p=128), writes=[('scT', s_)])
            for l_ in range(DEPTH):
                S.dma('sp', adab[:, l_, :], ada_b[l_].rearrange("(j p) -> p j", p=128), writes=[('adab', l_)])
                S.dma('sp', cbs[:, l_, :], hy_cb[l_].rearrange("(c p) -> p c", p=128), writes=[('cbs', l_)])
                for m_ in range(3):
                    S.dma('sp', lng[:, l_ * 3 + m_, :], ln_g[l_, m_].rearrange("(k p) -> p k", p=128),
                          writes=[('lng', l_, m_)])
                    S.dma('sp', lnb[:, l_ * 3 + m_, :], ln_b[l_, m_].rearrange("(k p) -> p k", p=128),
                          writes=[('lnb', l_, m_)])
                    S.dma('sp', cw[:, l_ * 3 + m_, :], hy_cw[l_, m_].rearrange("(c p) -> p c", p=128),
                          writes=[('cw', l_, m_)])
            S.dma('sp', nsink[:], sink.rearrange("l h -> (l h)").partition_broadcast(128), writes=['nsink'])
        S.barrier()
        S.act(lambda: A.activation(out=scT_b[:], in_=scT[:], func=AF.Silu), ['scT'], ['scTb'])
        S.dve(lambda: V.tensor_scalar(out=nlng[:], in0=lng[:], scalar1=-1.0, scalar2=None, op0=ALU.mult),
              ['lng'], ['nlng'])
        S.dve(lambda: V.tensor_scalar(out=nsink[:], in0=nsink[:], scalar1=-1.0, scalar2=None, op0=ALU.mult),
              ['nsink'], ['nsink'])
        S.barrier()

        def phase_tin():
            with ExitStack() as st:
                xtok = [sb(st, [128, 4, D], F32, "xtok") for _ in range(2)]
                xo = [sb(st, [128, 8, 512], F32, "xo") for _ in range(2)]
                pt = [ps(st, [128, 512], F32, "pt") for _ in range(3)]
                n = 0
                for g in range(T // 512):
                    b = g % 2
                    S.dma('sp', xtok[b][:], xin[g * 512:(g + 1) * 512, :].rearrange("(j p) d -> p j d", p=128),
                          writes=[('xtok', b)])
                    for k in range(8):
                        pb = n % 3
                        n += 1
                        S.pe([(lambda j=j: P.transpose(pt[pb][:, j * 128:(j + 1) * 128],
                                                        xtok[b][:, j, k * 128:(k + 1) * 128], ident_f[:]))
                              for j in range(4)], [('xtok', b)], [('pt', pb)])
                        if k % 2 == 0:
                            S.act(lambda: A.copy(out=xo[b][:, k, :], in_=pt[pb][:]), [('pt', pb)], [('xo', b, k)])
                        else:
                            S.dve(lambda: V.tensor_copy(out=xo[b][:, k, :], in_=pt[pb][:]), [('pt', pb)], [('xo', b, k)])
                    S.dma('sp', xT_v[:, :, g * 512:(g + 1) * 512], xo[b][:],
                          reads=[('xo', b, k) for k in range(8)])
            S.barrier()

        def phase_tout():
            with ExitStack() as st:
                xo = [sb(st, [128, 8, 512], F32, "xo") for _ in range(2)]
                ytok = [sb(st, [128, 4, D], F32, "ytok") for _ in range(2)]
                pt = [ps(st, [128, 512], F32, "pt") for _ in range(4)]
                n = 0
                for g in range(T // 512):
                    b = g % 2
                    S.dma('sp', xo[b][:], xT_v[:, :, g * 512:(g + 1) * 512], writes=[('xo', b)])
                    for j in range(4):
                        for hf in range(2):
                            pb = n % 4
                            n += 1
                            S.pe([(lambda kk=kk: P.transpose(pt[pb][:, kk * 128:(kk + 1) * 128],
                                                              xo[b][:, hf * 4 + kk, j * 128:(j + 1) * 128], ident_f[:]))
                                  for kk in range(4)], [('xo', b)], [('pt', pb)])
                            if hf == 0:
                                S.act(lambda: A.copy(out=ytok[b][:, j, 0:512], in_=pt[pb][:]),
                                      [('pt', pb)], [('ytok', b, j, 0)])
                            else:
                                S.dve(lambda: V.tensor_copy(out=ytok[b][:, j, 512:1024], in_=pt[pb][:]),
                                      [('pt', pb)], [('ytok', b, j, 1)])
                    S.dma('sp', yout[g * 512:(g + 1) * 512, :].rearrange("(j p) d -> p j d", p=128), ytok[b][:],
                          reads=[('ytok', b, j, h) for j in range(4) for h in range(2)])
            S.barrier()

        def phase_mod(l):
            with ExitStack() as st:
                wt = [sb(st, [128, 8, 1152], BF16, "adaw") for _ in range(2)]
                pm = ps(st, [128, 72, 4], F32, "pm")
                src = ada_w[l].rearrange("(k p) f -> p k f", p=128)
                for cg in range(8):
                    b = cg % 2
                    S.dma('pool', wt[b][:], src[:, :, cg * 1152:(cg + 1) * 1152], writes=[('adaw', b)])
                    for jj in range(9):
                        j = cg * 9 + jj
                        S.pe([(lambda k=k: P.matmul(pm[:, j, :], wt[b][:, k, jj * 128:(jj + 1) * 128], scT_b[:, k, :],
                                                    start=(k == 0), stop=(k == 7))) for k in range(8)],
                             [('adaw', b), 'scTb'], ['pm'])
                allpm = ['pm']
                S.dve(lambda: V.tensor_tensor(out=modsb[:], in0=pm[:],
                                              in1=adab[:, l, :].unsqueeze(2).to_broadcast([128, 72, 4]), op=ALU.add),
                      allpm + ['modsb'], ['modsb'])
                for m, cf in ((1, None), (4, None), (7, None), (2, 0.5 / ALPHA), (5, 1.0 / ALPHA), (8, 0.5 / ALPHA)):
                    sl = modsb[:, m * 8:(m + 1) * 8, :]
                    if cf is None:
                        S.dve(lambda: V.tensor_scalar(out=sl, in0=sl, scalar1=1.0, scalar2=None, op0=ALU.add),
                              ['modsb'], ['modsb'])
                    else:
                        S.dve(lambda: V.tensor_scalar(out=sl, in0=sl, scalar1=1.0, scalar2=cf, op0=ALU.add,
                                                      op1=ALU.mult), ['modsb'], ['modsb'])
            S.barrier()

        def ln_tail(st_objs, l, m, b, xt, rb, rq, pst, tmps, TT, rbk='rb', rqk='rq'):
            mean_sb, var, mr = tmps[0:3]
            S.pe([(lambda k=k: P.matmul(pst[:, 0:TT], ones_b[:], rb[:, k, :], start=(k == 0), stop=(k == 7)))
                  for k in range(8)] +
                 [(lambda k=k: P.matmul(pst[:, TT:2 * TT], ones_b[:], rq[:, k, :], start=(k == 0), stop=(k == 7)))
                  for k in range(8)], [rbk, rqk], ['pst'])
            S.dve(lambda: V.tensor_copy(out=mean_sb[:], in_=pst[:, 0:TT]), ['pst'], ['mean'])
            S.dve(lambda: V.tensor_tensor(out=var[:], in0=mean_sb[:], in1=mean_sb[:], op=ALU.mult), ['mean'], ['var'])
            S.dve(lambda: V.tensor_tensor(out=var[:], in0=pst[:, TT:2 * TT], in1=var[:], op=ALU.subtract),
                  ['pst', 'var'], ['var'])
            S.act(lambda: A.activation(out=var[:], in_=var[:], func=AF.Sqrt, bias=eps_ln[:, 0:1], scale=1.0),
                  ['var'], ['var'])
            S.dve(lambda: V.reciprocal(out=var[:], in_=var[:]), ['var'], ['var'])
            S.dve(lambda: V.tensor_tensor(out=mr[:], in0=mean_sb[:], in1=var[:], op=ALU.mult), ['mean', 'var'], ['mr'])
            li = l * 3 + m
            for dk in range(8):
                x = xt[b][:, dk, :]
                S.dve(lambda: V.scalar_tensor_tensor(out=x, in0=x, scalar=lng[:, li, dk:dk + 1], in1=var[:],
                                                     op0=ALU.mult, op1=ALU.mult), [('xt', b, dk), 'var'], [('xt', b, dk)])
                tq = tmps[3 + dk % 2]
                S.pool(lambda: G.tensor_scalar(out=tq[:], in0=mr[:], scalar1=nlng[:, li, dk:dk + 1],
                                               scalar2=lnb[:, li, dk:dk + 1], op0=ALU.mult, op1=ALU.add),
                       ['mr'], [('tq', dk % 2)])
                S.pool(lambda: G.tensor_tensor(out=x, in0=x, in1=tq[:], op=ALU.add), [('xt', b, dk), ('tq', dk % 2)],
                       [('xt', b, dk)])

        def phase_ffn(l, which):
            TT = 256
            NT = T // TT
            jS, jB, jG = (1, 0, 2) if which == 0 else (7, 6, 8)
            lnm = 0 if which == 0 else 2
            with ExitStack() as st:
                wi = sb(st, [128, 8, 2 * DFF], BF16, "wi")
                wo = sb(st, [128, 22, D], BF16, "wo")
                xt = [sb(st, [128, 8, TT], F32, "xt") for _ in range(3)]
                hT = [sb(st, [128, 8, TT], BF16, "hT") for _ in range(2)]
                aT = sb(st, [128, 22, TT], BF16, "aT")
                rb = sb(st, [128, 8, TT], BF16, "rb")
                rq = sb(st, [128, 8, TT], BF16, "rq")
                sg = [sb(st, [128, TT], BF16, "sg") for _ in range(2)]
                tmps = [sb(st, [128, TT], F32, "lnt") for _ in range(5)]
                gu = [ps(st, [128, 512], F32, "gu") for _ in range(3)]
                pf = [ps(st, [128, 512], F32, "pf") for _ in range(4)]
                pst = ps(st, [128, 512], F32, "pst")
                wsrc = ffn_wi[which][l].rearrange("(k p) f -> p k f", p=128)
                osrc = ffn_wo[which][l].rearrange("(c p) d -> p c d", p=128)
                for cg in range(11):
                    S.dma('pool', wi[:, :, cg * 256:(cg + 1) * 256], wsrc[:, :, cg * 256:(cg + 1) * 256],
                          writes=[('wig', cg)])
                    S.dma('pool', wi[:, :, DFF + cg * 256:DFF + (cg + 1) * 256],
                          wsrc[:, :, DFF + cg * 256:DFF + (cg + 1) * 256], writes=[('wiu', cg)])
                for cg in range(11):
                    S.dma('pool', wo[:, cg * 2:cg * 2 + 2, :], osrc[:, cg * 2:cg * 2 + 2, :], writes=[('wo', cg)])

                def load(t):
                    S.dma('sp', xt[t % 3][:], xT_v[:, :, t * TT:(t + 1) * TT], writes=[('xt', t % 3, k_) for k_ in range(8)])

                def modulate(t):
                    b = t % 2
                    xb = t % 3
                    s = seq_of(t * TT)[0]
                    for k in range(8):
                        S.dve(lambda: V.tensor_scalar(out=hT[b][:, k, :], in0=xt[xb][:, k, :],
                                                      scalar1=modsb[:, jS * 8 + k, s:s + 1],
                                                      scalar2=modsb[:, jB * 8 + k, s:s + 1],
                                                      op0=ALU.mult, op1=ALU.add), [('xt', xb, k)], [('hT', b)])

                def gu_part(t):
                    b = t % 2
                    for c in range(22):
                        g = gu[c % 3]
                        S.pe([(lambda k=k: P.matmul(g[:, 0:TT], wi[:, k, c * 128:(c + 1) * 128], hT[b][:, k, :],
                                                    start=(k == 0), stop=(k == 7))) for k in range(8)] +
                             [(lambda k=k: P.matmul(g[:, TT:2 * TT], wi[:, k, DFF + c * 128:DFF + (c + 1) * 128],
                                                    hT[b][:, k, :], start=(k == 0), stop=(k == 7))) for k in range(8)],
                             [('hT', b), ('wig', c // 2), ('wiu', c // 2)], [('gu', c % 3)])
                        S.act(lambda: A.activation(out=sg[c % 2][:], in_=g[:, 0:TT], func=AF.Silu),
                              [('gu', c % 3)], [('sg', c % 2)])
                        S.dve(lambda: V.tensor_tensor(out=aT[:, c, :], in0=g[:, TT:2 * TT], in1=sg[c % 2][:],
                                                      op=ALU.mult), [('gu', c % 3), ('sg', c % 2)], [('aT', c)])

                def down_part(t):
                    b = t % 3
                    s = seq_of(t * TT)[0]
                    for dp in range(4):
                        bank = pf[dp]
                        S.pe([(lambda c=c, h_=h_: P.matmul(bank[:, h_ * TT:(h_ + 1) * TT],
                                                           wo[:, c, (2 * dp + h_) * 128:(2 * dp + h_ + 1) * 128],
                                                           aT[:, c, :], start=(c == 0), stop=(c == 21)))
                              for h_ in range(2) for c in range(22)],
                             [('aT', c) for c in range(22)] + [('wo', cg) for cg in range(11)], [('pf', dp)])
                        for h_ in range(2):
                            dk = 2 * dp + h_
                            pp = bank[:, h_ * TT:(h_ + 1) * TT]
                            x = xt[b][:, dk, :]
                            S.dve(lambda: V.scalar_tensor_tensor(out=x, in0=pp, scalar=modsb[:, jG * 8 + dk, s:s + 1],
                                                                 in1=x, op0=ALU.mult, op1=ALU.add),
                                  [('pf', dp), ('xt', b, dk)], [('xt', b, dk)])
                            S.pool(lambda: G.tensor_copy(out=rb[:, dk, :], in_=x), [('xt', b, dk)], ['rb'])
                            S.pool(lambda: G.tensor_tensor(out=rq[:, dk, :], in0=x, in1=x, op=ALU.mult),
                                   [('xt', b, dk)], ['rq'])

                def tail_part(t):
                    b = t % 3
                    ln_tail(None, l, lnm, b, xt, rb, rq, pst, tmps, TT)
                    S.dma('sp', xT_v[:, :, t * TT:(t + 1) * TT], xt[b][:], reads=[('xt', b, k_) for k_ in range(8)])

                for t in range(min(3, NT)):
                    load(t)
                modulate(0)
                gu_part(0)
                if NT > 1:
                    modulate(1)
                down_part(0)
                for t in range(NT):
                    if t + 1 < NT:
                        gu_part(t + 1)
                    if t + 2 < NT:
                        modulate(t + 2)
                    tail_part(t)
                    if t + 3 < NT:
                        load(t + 3)
                    if t + 1 < NT:
                        down_part(t + 1)
            S.barrier()

        def phase_wout(l):
            TT = 256
            NT = T // TT
            with ExitStack() as st:
                wo = sb(st, [128, 8, D], BF16, "wout")
                xt = [sb(st, [128, 8, TT], F32, "xt") for _ in range(3)]
                oT = [sb(st, [128, 8, TT], BF16, "oT") for _ in range(3)]
                rb = [sb(st, [128, 8, TT], BF16, "rb") for _ in range(2)]
                rq = [sb(st, [128, 8, TT], BF16, "rq") for _ in range(2)]
                tmps = [sb(st, [128, TT], F32, "lnt") for _ in range(5)]
                pf = [ps(st, [128, 512], F32, "pf") for _ in range(4)]
                pst = ps(st, [128, 512], F32, "pst")
                S.dma('pool', wo[:], w_out[l].rearrange("(k p) d -> p k d", p=128), writes=['wout'])

                def load(t):
                    S.dma('sp', xt[t % 3][:], xT_v[:, :, t * TT:(t + 1) * TT], writes=[('xt', t % 3, k_) for k_ in range(8)])
                    S.dma('sp', oT[t % 3][:], oT_v[:, :, t * TT:(t + 1) * TT], writes=[('oT', t % 3)])

                def mm(t):
                    b = t % 3
                    b2 = t % 2
                    s = seq_of(t * TT)[0]
                    for dp in range(4):
                        bank = pf[dp]
                        S.pe([(lambda k=k, h_=h_: P.matmul(bank[:, h_ * TT:(h_ + 1) * TT],
                                                           wo[:, k, (2 * dp + h_) * 128:(2 * dp + h_ + 1) * 128],
                                                           oT[b][:, k, :], start=(k == 0), stop=(k == 7)))
                              for h_ in range(2) for k in range(8)], [('oT', b), 'wout'], [('pf', dp)])
                        for h_ in range(2):
                            dk = 2 * dp + h_
                            pp = bank[:, h_ * TT:(h_ + 1) * TT]
                            x = xt[b][:, dk, :]
                            S.dve(lambda: V.scalar_tensor_tensor(out=x, in0=pp, scalar=modsb[:, 5 * 8 + dk, s:s + 1],
                                                                 in1=x, op0=ALU.mult, op1=ALU.add),
                                  [('pf', dp), ('xt', b, dk)], [('xt', b, dk)])
                            S.pool(lambda: G.tensor_copy(out=rb[b2][:, dk, :], in_=x), [('xt', b, dk)], [('rb', b2)])
                            S.pool(lambda: G.tensor_tensor(out=rq[b2][:, dk, :], in0=x, in1=x, op=ALU.mult),
                                   [('xt', b, dk)], [('rq', b2)])

                def tail(t):
                    b = t % 3
                    b2 = t % 2
                    ln_tail(None, l, 1, b, xt, rb[b2], rq[b2], pst, tmps, TT, rbk=('rb', b2), rqk=('rq', b2))
                    S.dma('sp', xT_v[:, :, t * TT:(t + 1) * TT], xt[b][:], reads=[('xt', b, k_) for k_ in range(8)])

                for t in range(min(3, NT)):
                    load(t)
                mm(0)
                for t in range(NT):
                    if t + 1 < NT:
                        mm(t + 1)
                    tail(t)
                    if t + 3 < NT:
                        load(t + 3)
            S.barrier()

        def phase_m1(l):
            TT = 512
            NT = T // TT
            with ExitStack() as st:
                win = sb(st, [128, 8, NIN], BF16, "win")
                xt = [sb(st, [128, 8, TT + 2], F32, "xt") for _ in range(2)]
                hT = [sb(st, [128, 8, TT + 2], BF16, "hT") for _ in range(2)]
                qb = [sb(st, [64, TT], BF16, "qb") for _ in range(2)]
                t1 = sb(st, [16, TT], F32, "t1")
                t2 = sb(st, [16, TT], F32, "t2")
                cs = [sb(st, [16, 2, TT], F32, "cs") for _ in range(2)]
                vx = [sb(st, [128, 4, 2, 65], BF16, "vx") for _ in range(2)]
                zs = [sb(st, [128, TT + 2], F32, "zs") for _ in range(2)]
                zc = [sb(st, [128, TT], F32, "zc") for _ in range(2)]
                x0b = sb(st, [128, 4, TT], BF16, "x0b")
                ub = sb(st, [128, 4, TT], BF16, "ub")
                tok0b = [sb(st, [128, 4, 512], BF16, "tok0") for _ in range(2)]
                psq = [ps(st, [64, 512], F32, "psq")] * 2
                psr = ps(st, [16, 512], F32, "psr")
                psv = ps(st, [128, 512], F32, "psv")
                psz = [ps(st, [128, 512], F32, "psz") for _ in range(2)]
                pszh = [ps(st, [128, 16], F32, "pszh") for _ in range(2)]
                ptr = ps(st, [128, 512], BF16, "ptr")
                S.dma('pool', win[:, :, 0:768], w_in[l].rearrange("(k p) f -> p k f", p=128)[:, :, 0:768],
                      writes=['win0'])
                for i in range(3):
                    S.dma('pool', win[:, :, 768 + i * 512:768 + (i + 1) * 512],
                          w_in[l].rearrange("(k p) f -> p k f", p=128)[:, :, 768 + i * 512:768 + (i + 1) * 512],
                          writes=[('win', i)])
                for b in range(2):
                    S.pool(lambda: G.memset(vx[b][:], 1.0), [], [('vx', b)])
                wkeys = ['win0'] + [('win', i) for i in range(3)]

                def load(g):
                    b = g % 2
                    tok0 = g * TT
                    s, s0, ln = seq_of(tok0)
                    lo = 1 if tok0 == s0 else 0
                    hi = 1 if tok0 + TT == s0 + ln else 0
                    S.dma('sp', xt[b][:, :, lo:TT + 2 - hi], xT_v[:, :, tok0 - 1 + lo:tok0 + TT + 1 - hi],
                          writes=[('xt', b)])
                    S.dma('sp', cs[b][:, 0, :], cd["cos_t"][:, tok0 - s0:tok0 - s0 + TT], writes=[('cs', b, 0)])
                    S.dma('sp', cs[b][:, 1, :], cd["sin_t"][:, tok0 - s0:tok0 - s0 + TT], writes=[('cs', b, 1)])

                def modulate(g):
                    b = g % 2
                    tok0 = g * TT
                    s, s0, ln = seq_of(tok0)
                    lo = 1 if tok0 == s0 else 0
                    hi = 1 if tok0 + TT == s0 + ln else 0
                    for k in range(8):
                        S.dve(lambda: V.tensor_scalar(out=hT[b][:, k, lo:TT + 2 - hi], in0=xt[b][:, k, lo:TT + 2 - hi],
                                                      scalar1=modsb[:, 4 * 8 + k, s:s + 1],
                                                      scalar2=modsb[:, 3 * 8 + k, s:s + 1],
                                                      op0=ALU.mult, op1=ALU.add), [('xt', b)], [('hT', b)])
                    if lo:
                        S.dve(lambda: V.memset(hT[b][:, :, 0:1], 0.0), [], [('hT', b)])
                    if hi:
                        S.dve(lambda: V.memset(hT[b][:, :, TT + 1:TT + 2], 0.0), [], [('hT', b)])

                load(0)
                modulate(0)
                nq = 0
                nz = 0
                for g in range(NT):
                    b = g % 2
                    tok0 = g * TT
                    if g + 1 < NT:
                        load(g + 1)
                    for hd in range(10):
                        qi = nq % 2
                        nq += 1
                        S.pe([(lambda k=k: P.matmul(psq[qi][:], win[:, k, hd * 64:(hd + 1) * 64], hT[b][:, k, 1:TT + 1],
                                                    start=(k == 0), stop=(k == 7))) for k in range(8)],
                             [('hT', b), 'win0'], ['psq'])
                        S.act(lambda: A.copy(out=qb[qi][:], in_=psq[qi][:]), ['psq'], [('qb', qi)])
                        S.pe([lambda: P.matmul(psr[:], rt_sb[:], qb[qi][:], start=True, stop=True)],
                             [('qb', qi)], ['psr'])
                        S.dve(lambda: V.tensor_tensor(out=t1[:], in0=psq[qi][0:16, :], in1=cs[b][:, 0, :], op=ALU.mult),
                              ['psq', ('cs', b, 0)], ['t1'])
                        S.dve(lambda: V.tensor_tensor(out=t2[:], in0=psr[:], in1=cs[b][:, 1, :], op=ALU.mult),
                              ['psr', ('cs', b, 1)], ['t2'])
                        S.pool(lambda: G.tensor_tensor(out=qb[qi][0:16, :], in0=t1[:], in1=t2[:], op=ALU.add),
                               ['t1', 't2', ('qb', qi)], [('qb', qi)])
                        S.dma('sp', q_d[hd, :, tok0:tok0 + TT], qb[qi][:], reads=[('qb', qi)])
                    for blk in range(4):
                        S.pe([(lambda k=k: P.matmul(psv[:, blk * 128:(blk + 1) * 128],
                                                    hT[b][:, k, 1 + blk * 128:1 + (blk + 1) * 128],
                                                    win[:, k, 640:768], start=(k == 0), stop=(k == 7)))
                              for k in range(8)], [('hT', b), 'win0'], ['psv'])
                    S.act(lambda: A.copy(out=vx[b][:, :, :, 0:64],
                                         in_=psv[:].rearrange("p (b k e) -> p b k e", b=4, k=2)),
                          ['psv', ('vx', b)], [('vx', b)])
                    S.dma('sp', v_d[tok0:tok0 + TT, :].rearrange("(b p) e -> p b e", p=128),
                          vx[b][:].rearrange("p b k e -> p b (k e)"), reads=[('vx', b)])
                    for i in range(4):
                        for part in range(3):
                            cz = part * 4 + i
                            zi = nz % 2
                            nz += 1
                            c0 = 768 + cz * 128
                            S.pe([(lambda k=k: P.matmul(psz[zi][:], win[:, k, c0:c0 + 128], hT[b][:, k, 1:TT + 1],
                                                        start=(k == 0), stop=(k == 7))) for k in range(8)] +
                                 [(lambda k=k: P.matmul(pszh[zi][:, 0:2], win[:, k, c0:c0 + 128],
                                                        hT[b][:, k, 0:TT + 2:TT + 1],
                                                        start=(k == 0), stop=(k == 7))) for k in range(8)],
                                 [('hT', b)] + wkeys, [('psz', zi)])
                            S.act(lambda: A.copy(out=zs[zi][:, 1:TT + 1], in_=psz[zi][:]), [('psz', zi)], [('zs', zi)])
                            S.act(lambda: A.copy(out=zs[zi][:, 0:TT + 2:TT + 1], in_=pszh[zi][:, 0:2]),
                                  [('psz', zi)], [('zs', zi)])
                            zz = zc[part % 2] if part > 0 else zc[0]
                            zkey = ('zc', part % 2 if part > 0 else 0)
                            S.pool(lambda: G.tensor_scalar(out=zz[:], in0=zs[zi][:, 1:TT + 1],
                                                           scalar1=cw[:, l * 3 + 1, cz:cz + 1],
                                                           scalar2=cbs[:, l, cz:cz + 1], op0=ALU.mult, op1=ALU.add),
                                   [('zs', zi)], [zkey])
                            S.dve(lambda: V.scalar_tensor_tensor(out=zz[:], in0=zs[zi][:, 0:TT],
                                                                  scalar=cw[:, l * 3 + 0, cz:cz + 1], in1=zz[:],
                                                                  op0=ALU.mult, op1=ALU.add), [('zs', zi), zkey], [zkey])
                            if part == 0:
                                S.dve(lambda: V.scalar_tensor_tensor(out=x0b[:, i, :], in0=zs[zi][:, 2:TT + 2],
                                                                     scalar=cw[:, l * 3 + 2, cz:cz + 1], in1=zz[:],
                                                                     op0=ALU.mult, op1=ALU.add),
                                      [('zs', zi), zkey], [('x0b', i)])
                            else:
                                S.dve(lambda: V.scalar_tensor_tensor(out=zz[:], in0=zs[zi][:, 2:TT + 2],
                                                                     scalar=cw[:, l * 3 + 2, cz:cz + 1], in1=zz[:],
                                                                     op0=ALU.mult, op1=ALU.add),
                                      [('zs', zi), zkey], [zkey])
                        S.pool(lambda: G.tensor_tensor(out=ub[:, i, :], in0=zc[0][:], in1=zc[1][:], op=ALU.mult),
                               [('zc', 0), ('zc', 1)], [('ub', i)])
                    for ti, (srcb, dst, key) in enumerate(((x0b, x0_d, 'x0b'), (ub, u_d, 'ub'))):
                        tb = tok0b[ti]
                        for blk in range(4):
                            S.pe([(lambda i=i: P.transpose(ptr[:, i * 128:(i + 1) * 128],
                                                           srcb[:, i, blk * 128:(blk + 1) * 128], ident_b[:]))
                                  for i in range(4)], [(key, i) for i in range(4)], ['ptr'])
                            S.act(lambda: A.copy(out=tb[:, blk, :], in_=ptr[:]), ['ptr'], [('tokb', ti)])
                        S.dma('sp', dst[tok0:tok0 + TT, :].rearrange("(b p) c -> p b c", p=128), tb[:],
                              reads=[('tokb', ti)])
                    if g + 1 < NT:
                        modulate(g + 1)
            S.barrier()

        def phase_attn(l):
            TT = 512
            NT = T // TT
            with ExitStack() as st:
                qT = [sb(st, [64, 8, TT], BF16, "qT") for _ in range(2)]
                kT = [sb(st, [64, 2, TT + 256], BF16, "kT") for _ in range(2)]
                vv = [sb(st, [128, 6, 130], BF16, "vv") for _ in range(2)]
                pT = [sb(st, [128, 3, 128], BF16, "pT") for _ in range(3)]
                gA = sb(st, [128, 512], F32, "gA")
                den = sb(st, [128, 8], F32, "den")
                osb = sb(st, [128, 8, 64], F32, "osb")
                osq = sb(st, [128, 512], F32, "osq")
                ssq = sb(st, [128, 1], F32, "ssq")
                onb = sb(st, [128, 512], BF16, "onb")
                oTs = [sb(st, [128, 4, TT], BF16, "oTs") for _ in range(2)]
                pS = [ps(st, [128, 3, 128], F32, "pS") for _ in range(3)]
                pO = [ps(st, [128, 4, 65], F32, "pO") for _ in range(4)]
                ptr = ps(st, [128, 512], BF16, "ptr")
                S.dma('sp', gA[:], grp_g[l, 0:512].partition_broadcast(128), writes=['gA'])

                def load(g):
                    b = g % 2
                    tok0 = g * TT
                    s, s0, ln = seq_of(tok0)
                    lo = 128 if tok0 == s0 else 0
                    hi = 128 if tok0 + TT == s0 + ln else 0
                    S.dma('sp', qT[b][:], q_d[0:8, :, tok0:tok0 + TT].rearrange("h p t -> p h t"), writes=[('qT', b)])
                    S.dma('sp', kT[b][:, :, lo:TT + 256 - hi],
                          q_d[8:10, :, tok0 - 128 + lo:tok0 + TT + 128 - hi].rearrange("h p t -> p h t"),
                          writes=[('kT', b)])
                    S.dma('sp', vv[b][:, lo // 128:6 - hi // 128, :],
                          v_d[tok0 - 128 + lo:tok0 + TT + 128 - hi, :].rearrange("(b p) e -> p b e", p=128),
                          writes=[('vv', b)])

                load(0)
                npi = 0
                npo = 0
                for g in range(NT):
                    b = g % 2
                    tok0 = g * TT
                    s, s0, ln = seq_of(tok0)
                    if g + 1 < NT:
                        load(g + 1)
                    for qb_ in range(4):
                        gpos = tok0 + qb_ * 128
                        jlo = 1 if gpos == s0 else 0
                        jhi = 2 if gpos + 128 == s0 + ln else 3
                        po = (pO[(npo % 2) * 2], pO[(npo % 2) * 2 + 1])
                        pok = ('pO', npo % 2)
                        npo += 1
                        for h in range(8):
                            kv = h // 4
                            pi = npi % 3
                            npi += 1
                            S.pe([(lambda jj=jj: P.matmul(pS[pi][:, jj, :],
                                                          kT[b][:, kv, (qb_ + jj) * 128:(qb_ + jj + 1) * 128],
                                                          qT[b][:, h, qb_ * 128:(qb_ + 1) * 128], start=True, stop=True))
                                  for jj in range(jlo, jhi)], [('qT', b), ('kT', b)], [('pS', pi)])
                            S.act(lambda: A.activation(out=pT[pi][:, jlo:jhi, :], in_=pS[pi][:, jlo:jhi, :], func=AF.Exp,
                                                       bias=nsink[:, l * 8 + h:l * 8 + h + 1], scale=0.125),
                                  [('pS', pi)], [('pT', pi)])
                            if jlo == 0:
                                S.pool(lambda: G.tensor_tensor(out=pT[pi][:, 0, :], in0=pT[pi][:, 0, :], in1=maskp[:],
                                                               op=ALU.mult), [('pT', pi)], [('pT', pi)])
                            if jhi == 3:
                                S.pool(lambda: G.tensor_tensor(out=pT[pi][:, 2, :], in0=pT[pi][:, 2, :], in1=maskn[:],
                                                               op=ALU.mult), [('pT', pi)], [('pT', pi)])
                            S.pe([(lambda jj=jj: P.matmul(po[h // 4][:, h % 4, :], pT[pi][:, jj, :],
                                                          vv[b][:, qb_ + jj, kv * 65:(kv + 1) * 65],
                                                          start=(jj == jlo), stop=(jj == jhi - 1)))
                                  for jj in range(jlo, jhi)], [('pT', pi), ('vv', b)], [pok + (h // 4,)])
                        hk = [pok + (0,), pok + (1,)]
                        for hh in range(2):
                            S.dve(lambda: V.tensor_scalar(out=den[:, hh * 4:(hh + 1) * 4], in0=po[hh][:, :, 64],
                                                          scalar1=1.0, scalar2=None, op0=ALU.add), hk + ['den'], ['den'])
                        S.dve(lambda: V.reciprocal(out=den[:], in_=den[:]), ['den'], ['den'])
                        for hh in range(2):
                            S.dve(lambda: V.tensor_tensor(out=osb[:, hh * 4:(hh + 1) * 4, :], in0=po[hh][:, :, 0:64],
                                                          in1=den[:, hh * 4:(hh + 1) * 4].unsqueeze(2).to_broadcast(
                                                              [128, 4, 64]), op=ALU.mult),
                                  hk + ['den', 'osb'], ['osb'])
                        of = osb[:].rearrange("p h e -> p (h e)")
                        S.pool(lambda: G.tensor_tensor(out=osq[:], in0=of, in1=of, op=ALU.mult), ['osb'], ['osq'])
                        S.dve(lambda: V.reduce_sum(out=ssq[:], in_=osq[:], axis=AX.X), ['osq'], ['ssq'])
                        S.dve(lambda: V.tensor_scalar(out=ssq[:], in0=ssq[:], scalar1=1.0 / 512, scalar2=RMS_EPS,
                                                      op0=ALU.mult, op1=ALU.add), ['ssq'], ['ssq'])
                        S.act(lambda: A.activation(out=ssq[:], in_=ssq[:], func=AF.Ln), ['ssq'], ['ssq'])
                        S.act(lambda: A.activation(out=ssq[:], in_=ssq[:], func=AF.Exp, scale=-0.5), ['ssq'], ['ssq'])
                        S.dve(lambda: V.scalar_tensor_tensor(out=onb[:], in0=of, scalar=ssq[:, 0:1], in1=gA[:],
                                                             op0=ALU.mult, op1=ALU.mult),
                              ['osb', 'ssq', 'gA', 'onb'], ['onb'])
                        S.pe([(lambda c=c: P.transpose(ptr[:, c * 128:(c + 1) * 128], onb[:, c * 128:(c + 1) * 128],
                                                       ident_b[:])) for c in range(4)], ['onb'], ['ptr'])
                        S.act(lambda: A.copy(out=oTs[b][:, :, qb_ * 128:(qb_ + 1) * 128],
                                             in_=ptr[:].rearrange("p (c t) -> p c t", c=4)), ['ptr'], [('oTs', b)])
                    S.dma('sp', oT_v[:, 0:4, tok0:tok0 + TT], oTs[b][:], reads=[('oTs', b)])
            S.barrier()

        def fft_stage_a(st, li, srcs, dstA, tag):
            L = LS[li]
            H1 = L // 128
            nk = 2 * NK1[li]
            fa = sb(st, [H1, nk], BF16, "fa")
            S.dma('sp', fa[:], cd[f"fa{li}"], writes=[(tag, 'fa')])
            xc = [sb(st, [H1, 16, 512], BF16, "xc") for _ in range(2)]
            asb = [sb(st, [nk, 16, 512], BF16, "asb") for _ in range(2)]
            pa = [ps(st, [nk, 512], F32, "pa") for _ in range(3)]
            n = 0
            it = 0
            for sq in range(2):
                sv = srcs[sq].rearrange("(a b) c -> a b c", b=128)
                for ch in range(8):
                    bb = it % 2
                    it += 1
                    S.dma('sp', xc[bb][:], sv[:, ch * 16:(ch + 1) * 16, :], writes=[(tag, 'xc', bb)])
                    for j in range(16):
                        pi = n % 3
                        n += 1
                        S.pe([lambda: P.matmul(pa[pi][:], fa[:], xc[bb][:, j, :], start=True, stop=True)],
                             [(tag, 'fa'), (tag, 'xc', bb)], [(tag, 'pa', pi)])
                        if j % 2 == 0:
                            S.act(lambda: A.copy(out=asb[bb][:, j, :], in_=pa[pi][:]), [(tag, 'pa', pi)],
                                  [(tag, 'asb', bb, j)])
                        else:
                            S.dve(lambda: V.tensor_copy(out=asb[bb][:, j, :], in_=pa[pi][:]), [(tag, 'pa', pi)],
                                  [(tag, 'asb', bb, j)])
                    S.dma('sp', dstA[sq, :, ch * 16:(ch + 1) * 16, :], asb[bb][:],
                          reads=[(tag, 'asb', bb, j) for j in range(16)])

        def fft_stage_c(st, li, srcA, mode, dstB, tag):
            nk1 = NK1[li]
            mk = [sb(st, [128, 6, 128], BF16, "mk") for _ in range(2)]
            ak = [sb(st, [128, 2, 512], BF16, "ak") for _ in range(2)]
            kf = [sb(st, [128, 2, 512], F32, "kf") for _ in range(2)]
            tt = [sb(st, [128, 512], F32, "tt") for _ in range(4)]
            yb = [sb(st, [128, 2, 512], BF16, "yb") for _ in range(2)]
            bsb = [sb(st, [128, 2, 512], BF16, "bsb") for _ in range(2)]
            px = [ps(st, [128, 512], F32, "px") for _ in range(4)]
            pb = [ps(st, [128, 512], F32, "pb") for _ in range(4)]
            it = 0
            for k1 in range(nk1):
                mb = k1 % 2
                S.dma('sp', mk[mb][:], cd[f"mm{li}"][k1], writes=[(tag, 'mk', mb)])
                if mode == 'conv':
                    S.dma('sp', kf[mb][:], Kf_d[li][k1], writes=[(tag, 'kf', mb)])
                for sq in range(2):
                    ab = it % 2
                    it += 1
                    av = srcA[sq].rearrange("(r k) n c -> n r k c", r=2)
                    S.dma('sp', ak[ab][:], av[:, :, k1, :], writes=[(tag, 'ak', ab)])
                    pr, pi_ = px[ab * 2], px[ab * 2 + 1]
                    S.pe([lambda: P.matmul(pr[:], mk[mb][:, 0, :], ak[ab][:, 0, :], start=True, stop=False),
                          lambda: P.matmul(pr[:], mk[mb][:, 2, :], ak[ab][:, 1, :], start=False, stop=True),
                          lambda: P.matmul(pi_[:], mk[mb][:, 1, :], ak[ab][:, 0, :], start=True, stop=False),
                          lambda: P.matmul(pi_[:], mk[mb][:, 0, :], ak[ab][:, 1, :], start=False, stop=True)],
                         [(tag, 'mk', mb), (tag, 'ak', ab)], [(tag, 'px', ab)])
                    if mode == 'filter':
                        kk = kf[mb]
                        if sq == 0:
                            S.act(lambda: A.copy(out=kk[:, 0, :], in_=pr[:]), [(tag, 'px', ab)], [(tag, 'kf', mb)])
                            S.dve(lambda: V.tensor_copy(out=kk[:, 1, :], in_=pi_[:]), [(tag, 'px', ab)],
                                  [(tag, 'kf', mb)])
                        else:
                            S.dve(lambda: V.tensor_tensor(out=kk[:, 0, :], in0=pr[:], in1=kk[:, 0, :], op=ALU.add),
                                  [(tag, 'px', ab), (tag, 'kf', mb)], [(tag, 'kf', mb)])
                            S.dve(lambda: V.scalar_tensor_tensor(out=kk[:, 1, :], in0=pi_[:], scalar=-1.0,
                                                                 in1=kk[:, 1, :], op0=ALU.mult, op1=ALU.add),
                                  [(tag, 'px', ab), (tag, 'kf', mb)], [(tag, 'kf', mb)])
                            S.dma('sp', Kf_d[li][k1], kk[:], reads=[(tag, 'kf', mb)])
                        continue
                    kk = kf[mb]
                    S.dve(lambda: V.tensor_tensor(out=tt[0][:], in0=pr[:], in1=kk[:, 0, :], op=ALU.mult),
                          [(tag, 'px', ab), (tag, 'kf', mb)], [(tag, 'tt', 0)])
                    S.dve(lambda: V.tensor_tensor(out=tt[1][:], in0=pi_[:], in1=kk[:, 1, :], op=ALU.mult),
                          [(tag, 'px', ab), (tag, 'kf', mb)], [(tag, 'tt', 1)])
                    S.pool(lambda: G.tensor_tensor(out=yb[ab][:, 0, :], in0=tt[0][:], in1=tt[1][:], op=ALU.subtract),
                           [(tag, 'tt', 0), (tag, 'tt', 1)], [(tag, 'yb', ab)])
                    S.dve(lambda: V.tensor_tensor(out=tt[2][:], in0=pr[:], in1=kk[:, 1, :], op=ALU.mult),
                          [(tag, 'px', ab), (tag, 'kf', mb)], [(tag, 'tt', 2)])
                    S.dve(lambda: V.tensor_tensor(out=tt[3][:], in0=pi_[:], in1=kk[:, 0, :], op=ALU.mult),
                          [(tag, 'px', ab), (tag, 'kf', mb)], [(tag, 'tt', 3)])
                    S.pool(lambda: G.tensor_tensor(out=yb[ab][:, 1, :], in0=tt[2][:], in1=tt[3][:], op=ALU.add),
                           [(tag, 'tt', 2), (tag, 'tt', 3)], [(tag, 'yb', ab)])
                    br, bi = pb[ab * 2], pb[ab * 2 + 1]
                    S.pe([lambda: P.matmul(br[:], mk[mb][:, 3, :], yb[ab][:, 0, :], start=True, stop=False),
                          lambda: P.matmul(br[:], mk[mb][:, 4, :], yb[ab][:, 1, :], start=False, stop=True),
                          lambda: P.matmul(bi[:], mk[mb][:, 3, :], yb[ab][:, 1, :], start=True, stop=False),
                          lambda: P.matmul(bi[:], mk[mb][:, 5, :], yb[ab][:, 0, :], start=False, stop=True)],
                         [(tag, 'mk', mb), (tag, 'yb', ab)], [(tag, 'pb', ab)])
                    S.act(lambda: A.copy(out=bsb[ab][:, 0, :], in_=br[:]), [(tag, 'pb', ab)], [(tag, 'bsb', ab, 0)])
                    S.act(lambda: A.copy(out=bsb[ab][:, 1, :], in_=bi[:]), [(tag, 'pb', ab)], [(tag, 'bsb', ab, 1)])
                    bv = dstB[sq].rearrange("(r k) n c -> n r k c", r=2)
                    S.dma('sp', bv[:, :, k1, :], bsb[ab][:], reads=[(tag, 'bsb', ab, 0), (tag, 'bsb', ab, 1)])

        def fft_stage_ah(st, li, srcB, dsts, tag):
            L = LS[li]
            H1 = L // 128
            nk = 2 * NK1[li]
            ga = sb(st, [nk, H1], BF16, "ga")
            S.dma('sp', ga[:], cd[f"ga{li}"], writes=[(tag, 'ga')])
            bc = [sb(st, [nk, 16, 512], BF16, "bc") for _ in range(2)]
            ysb = [sb(st, [H1, 16, 512], F32, "ysb") for _ in range(2)]
            py = [ps(st, [H1, 512], F32, "py") for _ in range(3)]
            n = 0
            it = 0
            for sq in range(2):
                dv = dsts[sq].rearrange("(a b) c -> a b c", b=128)
                for ch in range(8):
                    bb = it % 2
                    it += 1
                    S.dma('sp', bc[bb][:], srcB[sq, :, ch * 16:(ch + 1) * 16, :], writes=[(tag, 'bc', bb)])
                    for j in range(16):
                        pi = n % 3
                        n += 1
                        S.pe([lambda: P.matmul(py[pi][:], ga[:], bc[bb][:, j, :], start=True, stop=True)],
                             [(tag, 'ga'), (tag, 'bc', bb)], [(tag, 'py', pi)])
                        if j % 2 == 0:
                            S.act(lambda: A.mul(out=ysb[bb][:, j, :], in_=py[pi][:], mul=1.0 / (2 * L)),
                                  [(tag, 'py', pi)], [(tag, 'ysb', bb, j)])
                        else:
                            S.dve(lambda: V.tensor_scalar(out=ysb[bb][:, j, :], in0=py[pi][:], scalar1=1.0 / (2 * L),
                                                          scalar2=None, op0=ALU.mult),
                                  [(tag, 'py', pi)], [(tag, 'ysb', bb, j)])
                    S.dma('sp', dv[:, ch * 16:(ch + 1) * 16, :], ysb[bb][:],
                          reads=[(tag, 'ysb', bb, j) for j in range(16)])

        def phase_filter(l, li):
            L = LS[li]
            with ExitStack() as st:
                w1 = sb(st, [33, 64], F32, "w1")
                w2 = sb(st, [64, 64], F32, "w2")
                w3 = sb(st, [64, 64], F32, "w3")
                w4 = sb(st, [64, 1024], F32, "w4")
                fr = sb(st, [64, 1], F32, "fr")
                bb_ = sb(st, [64, 3], F32, "bb")
                dec = sb(st, [128, 1024], F32, "dec")
                negt = sb(st, [128, L // 128], F32, "negt")
                zt = [sb(st, [33, 512], F32, "zt") for _ in range(2)]
                hh = [sb(st, [64, 512], F32, "hh") for _ in range(3)]
                arg = sb(st, [64, 512], F32, "arg")
                kq = sb(st, [64, 512], F32, "kq")
                win_ = sb(st, [128, 1024], F32, "winw")
                fsb = [sb(st, [128, 1024], BF16, "fsb") for _ in range(2)]
                pm = [ps(st, [64, 512], F32, "pm") for _ in range(2)]
                pf = [ps(st, [128, 512], F32, "pf") for _ in range(4)]
                with nc.allow_non_contiguous_dma(reason="tiny filter params"):
                    S.dma('sp', w1[:], hy_w1[l], writes=['fw'])
                    S.dma('sp', w2[:], hy_w2[l], writes=['fw'])
                    S.dma('sp', w3[:], hy_w3[l], writes=['fw'])
                    S.dma('sp', w4[:], hy_w4[l], writes=['fw'])
                    S.dma('sp', fr[:], hy_freq[l].rearrange("(p o) -> p o", o=1), writes=['fr'])
                    S.dma('sp', bb_[:, 0:1], hy_b1[l].rearrange("(p o) -> p o", o=1), writes=['fb'])
                    S.dma('sp', bb_[:, 1:2], hy_b2[l].rearrange("(p o) -> p o", o=1), writes=['fb'])
                    S.dma('sp', bb_[:, 2:3], hy_b3[l].rearrange("(p o) -> p o", o=1), writes=['fb'])
                    S.dma('sp', dec[:], hy_decay[l].rearrange("a c -> (a c)").partition_broadcast(128), writes=['dec'])
                    S.dma('sp', negt[:], cd[f"negt{li}"], writes=['negt'])
                S.barrier()
                S.dve(lambda: V.tensor_scalar(out=bb_[:], in0=bb_[:], scalar1=fr[:, 0:1], scalar2=None, op0=ALU.mult),
                      ['fb', 'fr'], ['fb'])
                S.dve(lambda: V.scalar_tensor_tensor(out=dec[:], in0=dec[:], scalar=-1.0, in1=dec[:], op0=ALU.mult,
                                                     op1=ALU.max), ['dec'], ['dec'])
                ws = [w1, w2, w3]
                np_ = 0
                for ch in range(L // 512):
                    zb = ch % 2
                    S.dma('sp', zt[zb][:], cd[f"zt{li}"][:, ch * 512:(ch + 1) * 512], writes=[('zt', zb)])
                    cur = zt[zb]
                    ckey = ('zt', zb)
                    kdim = 33
                    for ly in range(3):
                        pi = np_ % 2
                        np_ += 1
                        S.pe([lambda: P.matmul(pm[pi][:], ws[ly][0:kdim, :], cur[0:kdim, :], start=True, stop=True)],
                             [ckey, 'fw'], [('pm', pi)])
                        S.dve(lambda: V.tensor_scalar(out=arg[:], in0=pm[pi][:], scalar1=fr[:, 0:1],
                                                      scalar2=bb_[:, ly:ly + 1], op0=ALU.mult, op1=ALU.add),
                              [('pm', pi), 'fb', 'fr'], ['arg'])
                        S.dve(lambda: V.tensor_scalar(out=kq[:], in0=arg[:], scalar1=1.0 / (2 * PI), scalar2=12582912.0,
                                                      op0=ALU.mult, op1=ALU.add), ['arg'], ['kq'])
                        S.dve(lambda: V.tensor_scalar(out=kq[:], in0=kq[:], scalar1=-12582912.0, scalar2=None,
                                                      op0=ALU.add), ['kq'], ['kq'])
                        S.dve(lambda: V.scalar_tensor_tensor(out=arg[:], in0=kq[:], scalar=-2 * PI, in1=arg[:],
                                                             op0=ALU.mult, op1=ALU.add), ['kq', 'arg'], ['arg'])
                        S.dve(lambda: V.tensor_scalar(out=arg[:], in0=arg[:], scalar1=PI, scalar2=-PI,
                                                      op0=ALU.min, op1=ALU.max), ['arg'], ['arg'])
                        S.act(lambda: A.activation(out=hh[ly][:], in_=arg[:], func=AF.Sin), ['arg'], [('hh', ly)])
                        cur = hh[ly]
                        ckey = ('hh', ly)
                        kdim = 64
                    for blk in range(4):
                        gb = ch * 4 + blk
                        fb = gb % 2
                        p0, p1 = pf[(gb % 2) * 2], pf[(gb % 2) * 2 + 1]
                        S.pe([lambda: P.matmul(p0[:], hh[2][:, blk * 128:(blk + 1) * 128], w4[:, 0:512],
                                               start=True, stop=True),
                              lambda: P.matmul(p1[:], hh[2][:, blk * 128:(blk + 1) * 128], w4[:, 512:1024],
                                               start=True, stop=True)], [('hh', 2), 'fw'], [('pf', gb % 2)])
                        S.act(lambda: A.activation(out=win_[:], in_=dec[:], func=AF.Exp, scale=negt[:, gb:gb + 1]),
                              ['dec', 'negt', 'winw'], ['winw'])
                        S.dve(lambda: V.tensor_tensor(out=fsb[fb][:, 0:512], in0=p0[:], in1=win_[:, 0:512], op=ALU.mult),
                              [('pf', gb % 2), 'winw'], [('fsb', fb, 0)])
                        S.dve(lambda: V.tensor_tensor(out=fsb[fb][:, 512:1024], in0=p1[:], in1=win_[:, 512:1024],
                                                      op=ALU.mult), [('pf', gb % 2), 'winw'], [('fsb', fb, 1)])
                        S.dma('sp', filt_d[li][0, gb * 128:(gb + 1) * 128, :], fsb[fb][:, 0:512],
                              reads=[('fsb', fb, 0)])
                        S.dma('sp', filt_d[li][1, gb * 128:(gb + 1) * 128, :], fsb[fb][:, 512:1024],
                              reads=[('fsb', fb, 1)])
            S.barrier()

        def phase_hyena(l):
            for li in range(2):
                phase_filter(l, li)
                with ExitStack() as st:
                    fft_stage_a(st, li, [filt_d[li][0], filt_d[li][1]], A_d[li], 'fa')
                S.barrier()
                with ExitStack() as st:
                    fft_stage_c(st, li, A_d[li], 'filter', None, 'fc')
                S.barrier()
                s0 = SEQS[li * 2][0]
                s1 = SEQS[li * 2 + 1][0]
                L = LS[li]
                with ExitStack() as st:
                    fft_stage_a(st, li, [u_d[s0:s0 + L, :], u_d[s1:s1 + L, :]], A_d[li], 'ua')
                S.barrier()
                with ExitStack() as st:
                    fft_stage_c(st, li, A_d[li], 'conv', B_d[li], 'uc')
                S.barrier()
                with ExitStack() as st:
                    fft_stage_ah(st, li, B_d[li], [y_d[s0:s0 + L, :], y_d[s1:s1 + L, :]], 'uh')
                S.barrier()
            with ExitStack() as st:
                TT = 512
                gH = sb(st, [128, 512], F32, "gH")
                hb = sb(st, [128, 512], F32, "hb")
                yt = [sb(st, [128, 4, 512], F32, "yt") for _ in range(2)]
                ut = [sb(st, [128, 4, 512], BF16, "ut") for _ in range(2)]
                x0t = [sb(st, [128, 4, 512], BF16, "x0t") for _ in range(2)]
                o1 = sb(st, [128, 512], F32, "o1")
                osq = sb(st, [128, 512], F32, "osq")
                ssq = sb(st, [128, 1], F32, "ssq")
                onb = sb(st, [128, 512], BF16, "onb")
                oTs = [sb(st, [128, 4, TT], BF16, "oTs") for _ in range(2)]
                ptr = ps(st, [128, 512], BF16, "ptr")
                S.dma('sp', gH[:], grp_g[l, 512:1024].partition_broadcast(128), writes=['gH'])
                S.dma('sp', hb[:], hy_bias[l].partition_broadcast(128), writes=['hb'])

                def load(g):
                    b = g % 2
                    sl = slice(g * TT, (g + 1) * TT)
                    S.dma('sp', yt[b][:], y_d[sl, :].rearrange("(b p) c -> p b c", p=128), writes=[('yt', b)])
                    S.dma('sp', ut[b][:], u_d[sl, :].rearrange("(b p) c -> p b c", p=128), writes=[('ut', b)])
                    S.dma('sp', x0t[b][:], x0_d[sl, :].rearrange("(b p) c -> p b c", p=128), writes=[('x0t', b)])

                load(0)
                for g in range(T // TT):
                    b = g % 2
                    if g + 1 < T // TT:
                        load(g + 1)
                    for blk in range(4):
                        S.pool(lambda: G.tensor_tensor(out=o1[:], in0=ut[b][:, blk, :], in1=hb[:], op=ALU.mult),
                               [('ut', b), 'hb', 'o1'], ['o1'])
                        S.dve(lambda: V.tensor_tensor(out=o1[:], in0=o1[:], in1=yt[b][:, blk, :], op=ALU.add),
                              ['o1', ('yt', b)], ['o1'])
                        S.dve(lambda: V.tensor_tensor(out=o1[:], in0=o1[:], in1=x0t[b][:, blk, :], op=ALU.mult),
                              ['o1', ('x0t', b)], ['o1'])
                        S.pool(lambda: G.tensor_tensor(out=osq[:], in0=o1[:], in1=o1[:], op=ALU.mult), ['o1'], ['osq'])
                        S.dve(lambda: V.reduce_sum(out=ssq[:], in_=osq[:], axis=AX.X), ['osq'], ['ssq'])
                        S.dve(lambda: V.tensor_scalar(out=ssq[:], in0=ssq[:], scalar1=1.0 / 512, scalar2=RMS_EPS,
                                                      op0=ALU.mult, op1=ALU.add), ['ssq'], ['ssq'])
                        S.act(lambda: A.activation(out=ssq[:], in_=ssq[:], func=AF.Ln), ['ssq'], ['ssq'])
                        S.act(lambda: A.activation(out=ssq[:], in_=ssq[:], func=AF.Exp, scale=-0.5), ['ssq'], ['ssq'])
                        S.dve(lambda: V.scalar_tensor_tensor(out=onb[:], in0=o1[:], scalar=ssq[:, 0:1], in1=gH[:],
                                                             op0=ALU.mult, op1=ALU.mult),
                              ['o1', 'ssq', 'gH', 'onb'], ['onb'])
                        S.pe([(lambda c=c: P.transpose(ptr[:, c * 128:(c + 1) * 128], onb[:, c * 128:(c + 1) * 128],
                                                       ident_b[:])) for c in range(4)], ['onb'], ['ptr'])
                        S.act(lambda: A.copy(out=oTs[b][:, :, blk * 128:(blk + 1) * 128],
                                             in_=ptr[:].rearrange("p (c t) -> p c t", c=4)), ['ptr'], [('oTs', b)])
                    S.dma('sp', oT_v[:, 4:8, g * TT:(g + 1) * TT], oTs[b][:], reads=[('oTs', b)])
            S.barrier()

        phase_tin()
        for l in range(RUN_LAYERS):
            if STOP_AFTER == ('tin', l):
                break
            phase_mod(l)
            if STOP_AFTER == ('mod', l):
                break
            phase_ffn(l, 0)
            if STOP_AFTER == ('ffn1', l):
                break
            phase_m1(l)
            phase_attn(l)
            phase_hyena(l)
            phase_wout(l)
            if STOP_AFTER == ('mix', l):
                break
            phase_ffn(l, 1)
        phase_tout()
        S.finish()
        print("sched: instr counts", S.cnt, S.dcnt, "waits", S.nwait)
    return nc, consts


def kernel(x_prompt, x_sample, c_prompt, c_sample, ada_w, ada_b, ffn1_wi, ffn1_wo, ffn2_wi, ffn2_wo,
           ln_g, ln_b, w_in, w_out, sink, grp_norm_g, hy_conv_w, hy_conv_b, hy_w1, hy_b1, hy_w2, hy_b2,
           hy_w3, hy_b3, hy_w4, hy_freq, hy_decay, hy_bias):
    nc, consts = build()
    f = lambda a: np.ascontiguousarray(np.asarray(a, dtype=np.float32))
    f0 = f
    if RUN_LAYERS < DEPTH:
        f = lambda a: f0(np.asarray(a)[:RUN_LAYERS])
    shared = {
        "ada_w": f(ada_w), "ada_b": f(ada_b), "ffn1_wi": f(ffn1_wi), "ffn1_wo": f(ffn1_wo),
        "ffn2_wi": f(ffn2_wi), "ffn2_wo": f(ffn2_wo), "ln_g": f(ln_g), "ln_b": f(ln_b),
        "w_in": f(w_in), "w_out": f(w_out), "sink": f(sink), "grp_norm_g": f(grp_norm_g),
        "hy_conv_w": f(hy_conv_w), "hy_conv_b": f(hy_conv_b), "hy_w1": f(hy_w1), "hy_b1": f(hy_b1),
        "hy_w2": f(hy_w2), "hy_b2": f(hy_b2), "hy_w3": f(hy_w3), "hy_b3": f(hy_b3), "hy_w4": f(hy_w4),
        "hy_freq": f(hy_freq), "hy_decay": f(hy_decay), "hy_bias": f(hy_bias),
    }
    for k, v in consts.items():
        shared["c_" + k] = v
    xp = f0(x_prompt)
    xs = f0(x_sample)
    cp = f0(c_prompt)
    cs_ = f0(c_sample)
    in_maps = []
    for c in range(NCORES):
        m = dict(shared)
        m["xin"] = np.concatenate([xp[2 * c].reshape(4096, D), xp[2 * c + 1].reshape(4096, D),
                                   xs[2 * c].reshape(2048, D), xs[2 * c + 1].reshape(2048, D)], axis=0)
        m["cin"] = np.concatenate([cp[2 * c:2 * c + 2], cs_[2 * c:2 * c + 2]], axis=0)
        in_maps.append(m)
    res = run_bass_kernel_spmd(nc, in_maps, core_ids=list(range(NCORES)))
    yp = np.empty((16, 4096, D), np.float32)
    ys = np.empty((16, 2048, D), np.float32)
    for c in range(NCORES):
        y = res.results[c]["yout"]
        yp[2 * c] = y[0:4096]
        yp[2 * c + 1] = y[4096:8192]
        ys[2 * c] = y[8192:10240]
        ys[2 * c + 1] = y[10240:12288]
    return yp, ys
```

```python
import math
from contextlib import ExitStack
import numpy as np
import ml_dtypes
import concourse.bass as bass
import concourse.mybir as mybir
from concourse.bass_utils import run_bass_kernel_spmd

F32 = mybir.dt.float32
BF16 = mybir.dt.bfloat16
AF = mybir.ActivationFunctionType
ALU = mybir.AluOpType
AX = mybir.AxisListType

D = 1024
DEPTH = 4
DFF = 2816
NIN = 2304
ALPHA = float((2 * DEPTH) ** 0.25)
LN_EPS = 1e-5 / (ALPHA * ALPHA)
RMS_EPS = 1e-6
NCORES = 8
SEQS = [(0, 4096), (4096, 4096), (8192, 2048), (10240, 2048)]
T = 12288
LS = [4096, 2048]
PI = math.pi

RUN_LAYERS = DEPTH
LAST_MARKS = None
FFN_DEBUG = 4
STOP_AFTER = None


def seq_of(tok):
    for i, (s0, ln) in enumerate(SEQS):
        if s0 <= tok < s0 + ln:
            return i, s0, ln
    raise ValueError


class Sched:
    NDMA = 8

    def __init__(self, nc, stack):
        self.nc = nc
        self.E = {'pe': nc.tensor, 'act': nc.scalar, 'dve': nc.vector, 'pool': nc.gpsimd, 'sp': nc.sync}
        self.sem = {e: stack.enter_context(nc.semaphore(f"s_{e}")) for e in ('pe', 'act', 'dve', 'pool')}
        self.cnt = {e: 0 for e in self.sem}
        self.nd = {'sp': 3, 'pool': 2}
        self.dsem = {q: [stack.enter_context(nc.semaphore(f"d_{q}{i}")) for i in range(self.nd[q])]
                     for q in ('sp', 'pool')}
        self.dcnt = {q: 0 for q in self.dsem}
        self.known = {e: {} for e in self.E}
        self.res = {}
        self.nwait = 0

    PS_NAMES = frozenset(['pt', 'pm', 'pst', 'gu', 'pf', 'psq', 'psr', 'psv', 'psz', 'ptr', 'pS', 'pO', 'pa', 'px',
                          'pb', 'py'])

    def _isp(self, k):
        if isinstance(k, tuple):
            return any(isinstance(x, str) and x in self.PS_NAMES for x in k)
        return k in self.PS_NAMES

    def _split(self, reads, writes):
        r2 = [k for k in reads if not self._isp(k)]
        w2 = list(writes) + [k for k in reads if self._isp(k)]
        return r2, w2

    def _wait(self, e, tok):
        if tok is None:
            return
        sem, val = tok
        k = self.known[e]
        sid = id(sem)
        if k.get(sid, 0) >= val:
            return
        self.E[e].wait_ge(sem, val)
        self.nwait += 1
        k[sid] = val

    def deps(self, e, reads, writes):
        for r in reads:
            st = self.res.get(r)
            if st is not None:
                self._wait(e, st[0])
        for w in writes:
            st = self.res.get(w)
            if st is not None:
                self._wait(e, st[0])
                for t in st[1]:
                    self._wait(e, t)

    def commit(self, tok, reads, writes):
        for r in reads:
            st = self.res.setdefault(r, [None, []])
            st[1].append(tok)
            if len(st[1]) > 10:
                d = {}
                for s, v in st[1]:
                    if id(s) not in d or d[id(s)][1] < v:
                        d[id(s)] = (s, v)
                st[1] = list(d.values())
        for w in writes:
            self.res[w] = [tok, []]

    def op(self, e, fn, reads=(), writes=()):
        reads, writes = self._split(reads, writes)
        self.deps(e, reads, writes)
        ins = fn()
        self.cnt[e] += 1
        ins.then_inc(self.sem[e], 1)
        tok = (self.sem[e], self.cnt[e])
        if e == 'pe':
            self.known[e][id(self.sem[e])] = self.cnt[e]
        self.commit(tok, reads, writes)
        return tok

    def pe(self, fns, reads=(), writes=()):
        reads, writes = self._split(reads, writes)
        self.deps('pe', reads, writes)
        ins = None
        for fn in fns:
            ins = fn()
        self.cnt['pe'] += 1
        ins.then_inc(self.sem['pe'], 1)
        tok = (self.sem['pe'], self.cnt['pe'])
        self.known['pe'][id(self.sem['pe'])] = self.cnt['pe']
        self.commit(tok, reads, writes)
        return tok

    def act(self, fn, r=(), w=()):
        return self.op('act', fn, r, w)

    def dve(self, fn, r=(), w=()):
        return self.op('dve', fn, r, w)

    def pool(self, fn, r=(), w=()):
        return self.op('pool', fn, r, w)

    def dma(self, q, out, in_, reads=(), writes=()):
        self.deps(q, reads, writes)
        j = self.dcnt[q]
        self.dcnt[q] += 1
        sem = self.dsem[q][j % self.nd[q]]
        rnd = j // self.nd[q]
        if rnd > 0:
            self._wait(q, (sem, 16 * rnd))
        self.E[q].dma_start(out=out, in_=in_).then_inc(sem, 16)
        tok = (sem, 16 * (rnd + 1))
        self.commit(tok, reads, writes)
        return tok

    def _all_tokens(self):
        toks = []
        for q in self.dsem:
            j = self.dcnt[q]
            for i in range(min(j, self.nd[q])):
                last = j - 1 - i
                toks.append((self.dsem[q][last % self.nd[q]], 16 * (last // self.nd[q] + 1)))
        for x in self.sem:
            if self.cnt[x] > 0:
                toks.append((self.sem[x], self.cnt[x]))
        return toks

    def barrier(self):
        import inspect
        if not hasattr(self, 'marks'):
            self.marks = []
        fr = inspect.stack()[1]
        self.marks.append((fr.function, fr.lineno, dict(self.cnt)))
        toks = self._all_tokens()
        for e in self.E:
            for t in toks:
                self._wait(e, t)
        self.res = {}

    def finish(self):
        for t in self._all_tokens():
            self._wait('sp', t)


def fft_consts(L):
    N = 2 * L
    N1 = N // 128
    H1 = N1 // 2
    nk1 = H1 + 1
    n1 = np.arange(H1)[:, None]
    k1 = np.arange(nk1)[None, :]
    ang = 2 * np.pi * n1 * k1 / N1
    FA = np.concatenate([np.cos(ang), -np.sin(ang)], axis=1)
    w = np.full(nk1, 2.0)
    w[0] = 1
    w[-1] = 1
    GA = np.concatenate([(w[None, :] * np.cos(ang)).T, (-w[None, :] * np.sin(ang)).T], axis=0)
    n2 = np.arange(128)[:, None]
    k2 = np.arange(128)[None, :]
    Ms = []
    for kk in range(nk1):
        th = 2 * np.pi * ((n2 * (kk + N1 * k2)) % N) / N
        Mre = np.cos(th)
        Mim = -np.sin(th)
        Ms.append(np.stack([Mre, Mim, -Mim, Mre.T, Mim.T, -Mim.T], axis=1))
    M = np.stack(Ms, axis=0)
    return FA, GA, M


def host_consts():
    bf = ml_dtypes.bfloat16
    c = {}
    c["ident_f"] = np.eye(128, dtype=np.float32)
    c["ident_b"] = np.eye(128, dtype=np.float32).astype(bf)
    c["ones_b"] = np.full((128, 128), 1.0 / D, dtype=np.float32).astype(bf)
    rt = np.zeros((64, 16), np.float32)
    for i in range(8):
        rt[i + 8, i] = -1.0
        rt[i, i + 8] = 1.0
    c["rt"] = rt.astype(bf)
    inv = 500000.0 ** (-np.arange(0, 16, 2, dtype=np.float64) / 16)
    ang = np.arange(4096, dtype=np.float64)[None, :] * np.concatenate([inv, inv])[:, None]
    c["cos_t"] = np.cos(ang).astype(np.float32)
    c["sin_t"] = np.sin(ang).astype(np.float32)
    s = np.arange(128)[:, None]
    q = np.arange(128)[None, :]
    c["maskp"] = (s >= q).astype(np.float32).astype(bf)
    c["maskn"] = (s <= q).astype(np.float32).astype(bf)
    for i, L in enumerate(LS):
        FA, GA, M = fft_consts(L)
        c[f"fa{i}"] = FA.astype(np.float32).astype(bf)
        c[f"ga{i}"] = GA.astype(np.float32).astype(bf)
        c[f"mm{i}"] = M.astype(np.float32).astype(bf)
        t = np.linspace(0.0, 1.0, L, dtype=np.float32).astype(np.float64)
        wv = 2.0 * np.pi * np.arange(L, dtype=np.float64) / L
        fb = np.linspace(1e-4, 15, 16, dtype=np.float32).astype(np.float64)[None]
        z = np.concatenate([t[:, None], np.cos(fb * wv[:, None]), -np.sin(fb * wv[:, None])], axis=-1)
        c[f"zt{i}"] = np.ascontiguousarray(z.T).astype(np.float32)
        c[f"negt{i}"] = np.ascontiguousarray((-t).reshape(L // 128, 128).T).astype(np.float32)
    return c


def build():
    nc = bass.Bass("TRN2", target_bir_lowering=False)
    DEPTH = RUN_LAYERS
    consts = host_consts()

    def din(name, shape, dt=F32):
        return nc.dram_tensor(name, list(shape), dt, kind="ExternalInput").ap()

    def dscr(name, shape, dt=F32):
        return nc.dram_tensor(name, list(shape), dt).ap()

    xin = din("xin", [T, D])
    cin = din("cin", [4, D])
    ada_w = din("ada_w", [DEPTH, D, 9 * D])
    ada_b = din("ada_b", [DEPTH, 9 * D])
    ffn_wi = [din("ffn1_wi", [DEPTH, D, 2 * DFF]), din("ffn2_wi", [DEPTH, D, 2 * DFF])]
    ffn_wo = [din("ffn1_wo", [DEPTH, DFF, D]), din("ffn2_wo", [DEPTH, DFF, D])]
    ln_g = din("ln_g", [DEPTH, 3, D])
    ln_b = din("ln_b", [DEPTH, 3, D])
    w_in = din("w_in", [DEPTH, D, NIN])
    w_out = din("w_out", [DEPTH, D, D])
    sink = din("sink", [DEPTH, 8])
    grp_g = din("grp_norm_g", [DEPTH, D])
    hy_cw = din("hy_conv_w", [DEPTH, 3, 1536])
    hy_cb = din("hy_conv_b", [DEPTH, 1536])
    hy_w1 = din("hy_w1", [DEPTH, 33, 64])
    hy_b1 = din("hy_b1", [DEPTH, 64])
    hy_w2 = din("hy_w2", [DEPTH, 64, 64])
    hy_b2 = din("hy_b2", [DEPTH, 64])
    hy_w3 = din("hy_w3", [DEPTH, 64, 64])
    hy_b3 = din("hy_b3", [DEPTH, 64])
    hy_w4 = din("hy_w4", [DEPTH, 64, 1024])
    hy_freq = din("hy_freq", [DEPTH, 64])
    hy_decay = din("hy_decay", [DEPTH, 2, 512])
    hy_bias = din("hy_bias", [DEPTH, 512])
    cd = {}
    for k, v in consts.items():
        cd[k] = din("c_" + k, v.shape, BF16 if v.dtype == ml_dtypes.bfloat16 else F32)
    yout = nc.dram_tensor("yout", [T, D], F32, kind="ExternalOutput").ap()

    xT_d = dscr("xT_d", [D, T])
    q_d = dscr("q_d", [10, 64, T], BF16)
    v_d = dscr("v_d", [T, 130], BF16)
    u_d = dscr("u_d", [T, 512], BF16)
    x0_d = dscr("x0_d", [T, 512], BF16)
    y_d = dscr("y_d", [T, 512], F32)
    oT_d = dscr("oT_d", [D, T], BF16)
    filt_d = [dscr(f"filt_d{i}", [2, L, 512], BF16) for i, L in enumerate(LS)]
    NK1 = [L // 128 + 1 for L in LS]
    A_d = [dscr(f"A_d{i}", [2, 2 * NK1[i], 128, 512], BF16) for i in range(2)]
    B_d = [dscr(f"B_d{i}", [2, 2 * NK1[i], 128, 512], BF16) for i in range(2)]
    Kf_d = [dscr(f"Kf_d{i}", [NK1[i], 128, 2, 512], F32) for i in range(2)]

    xT_v = xT_d.rearrange("(k p) t -> p k t", p=128)
    oT_v = oT_d.rearrange("(k p) t -> p k t", p=128)

    top = ExitStack()
    with top:
        S = Sched(nc, top)
        uid = [0]

        def sb(st, shape, dt, name=None):
            uid[0] += 1
            return st.enter_context(nc.sbuf_tensor(f"{name or 't'}_{uid[0]}", list(shape), dt))

        def ps(st, shape, dt=F32, name=None):
            uid[0] += 1
            return st.enter_context(nc.psum_tensor(f"{name or 'p'}_{uid[0]}", list(shape), dt))

        V, G, A, P = nc.vector, nc.gpsimd, nc.scalar, nc.tensor

        ident_f = sb(top, [128, 128], F32, "identf")
        ident_b = sb(top, [128, 128], BF16, "identb")
        ones_b = sb(top, [128, 128], BF16, "onesb")
        rt_sb = sb(top, [64, 16], BF16, "rt")
        maskp = sb(top, [128, 128], BF16, "maskp")
        maskn = sb(top, [128, 128], BF16, "maskn")
        scT = sb(top, [128, 8, 4], F32, "scT")
        scT_b = sb(top, [128, 8, 4], BF16, "scTb")
        modsb = sb(top, [128, 72, 4], F32, "modsb")
        adab = sb(top, [128, DEPTH, 72], F32, "adab")
        lng = sb(top, [128, DEPTH * 3, 8], F32, "lng")
        nlng = sb(top, [128, DEPTH * 3, 8], F32, "nlng")
        lnb = sb(top, [128, DEPTH * 3, 8], F32, "lnb")
        cw = sb(top, [128, DEPTH * 3, 12], F32, "cw")
        cbs = sb(top, [128, DEPTH, 12], F32, "cbs")
        nsink = sb(top, [128, DEPTH * 8], F32, "nsink")
        eps_ln = sb(top, [128, 1], F32, "epsln")
        S.dve(lambda: V.memset(eps_ln[:], LN_EPS), [], ['epsln'])

        with nc.allow_non_contiguous_dma(reason="tiny one-time parameter loads"):
            S.dma('sp', ident_f[:], cd["ident_f"], writes=['c0'])
            S.dma('sp', ident_b[:], cd["ident_b"], writes=['c1'])
            S.dma('sp', ones_b[:], cd["ones_b"], writes=['c2'])
            S.dma('sp', rt_sb[:], cd["rt"], writes=['c3'])
            S.dma('sp', maskp[:], cd["maskp"], writes=['c4'])
            S.dma('sp', maskn[:], cd["maskn"], writes=['c5'])
            for s_ in range(4):
                S.dma('sp', scT[:, :, s_], cin[s_].rearrange("(k p) -> p k", p=128), writes=[('scT', s_)])
            for l_ in range(DEPTH):
                S.dma('sp', adab[:, l_, :], ada_b[l_].rearrange("(j p) -> p j", p=128), writes=[('adab', l_)])
                S.dma('sp', cbs[:, l_, :], hy_cb[l_].rearrange("(c p) -> p c", p=128), writes=[('cbs', l_)])
                for m_ in range(3):
                    S.dma('sp', lng[:, l_ * 3 + m_, :], ln_g[l_, m_].rearrange("(k p) -> p k", p=128),
                          writes=[('lng', l_, m_)])
                    S.dma('sp', lnb[:, l_ * 3 + m_, :], ln_b[l_, m_].rearrange("(k p) -> p k", p=128),
                          writes=[('lnb', l_, m_)])
                    S.dma('sp', cw[:, l_ * 3 + m_, :], hy_cw[l_, m_].rearrange("(c p) -> p c", p=128),
                          writes=[('cw', l_, m_)])
            S.dma('sp', nsink[:], sink.rearrange("l h -> (l h)").partition_broadcast(128), writes=['nsink'])
        S.barrier()
        S.act(lambda: A.activation(out=scT_b[:], in_=scT[:], func=AF.Silu), ['scT'], ['scTb'])
        S.dve(lambda: V.tensor_scalar(out=nlng[:], in0=lng[:], scalar1=-1.0, scalar2=None, op0=ALU.mult),
              ['lng'], ['nlng'])
        S.dve(lambda: V.tensor_scalar(out=nsink[:], in0=nsink[:], scalar1=-1.0, scalar2=None, op0=ALU.mult),
              ['nsink'], ['nsink'])
        S.barrier()

        def phase_tin():
            with ExitStack() as st:
                xtok = [sb(st, [128, 4, D], F32, "xtok") for _ in range(2)]
                xo = [sb(st, [128, 8, 512], F32, "xo") for _ in range(2)]
                pt = [ps(st, [128, 512], F32, "pt") for _ in range(3)]
                n = 0
                for g in range(T // 512):
                    b = g % 2
                    S.dma('sp', xtok[b][:], xin[g * 512:(g + 1) * 512, :].rearrange("(j p) d -> p j d", p=128),
                          writes=[('xtok', b)])
                    for k in range(8):
                        pb = n % 3
                        n += 1
                        S.pe([(lambda j=j: P.transpose(pt[pb][:, j * 128:(j + 1) * 128],
                                                        xtok[b][:, j, k * 128:(k + 1) * 128], ident_f[:]))
                              for j in range(4)], [('xtok', b)], [('pt', pb)])
                        if k % 2 == 0:
                            S.act(lambda: A.copy(out=xo[b][:, k, :], in_=pt[pb][:]), [('pt', pb)], [('xo', b, k)])
                        else:
                            S.dve(lambda: V.tensor_copy(out=xo[b][:, k, :], in_=pt[pb][:]), [('pt', pb)], [('xo', b, k)])
                    S.dma('sp', xT_v[:, :, g * 512:(g + 1) * 512], xo[b][:],
                          reads=[('xo', b, k) for k in range(8)])
            S.barrier()

        def phase_tout():
            with ExitStack() as st:
                xo = [sb(st, [128, 8, 512], F32, "xo") for _ in range(2)]
                ytok = [sb(st, [128, 4, D], F32, "ytok") for _ in range(2)]
                pt = [ps(st, [128, 512], F32, "pt") for _ in range(4)]
                n = 0
                for g in range(T // 512):
                    b = g % 2
                    S.dma('sp', xo[b][:], xT_v[:, :, g * 512:(g + 1) * 512], writes=[('xo', b)])
                    for j in range(4):
                        for hf in range(2):
                            pb = n % 4
                            n += 1
                            S.pe([(lambda kk=kk: P.transpose(pt[pb][:, kk * 128:(kk + 1) * 128],
                                                              xo[b][:, hf * 4 + kk, j * 128:(j + 1) * 128], ident_f[:]))
                                  for kk in range(4)], [('xo', b)], [('pt', pb)])
                            if hf == 0:
                                S.act(lambda: A.copy(out=ytok[b][:, j, 0:512], in_=pt[pb][:]),
                                      [('pt', pb)], [('ytok', b, j, 0)])
                            else:
                                S.dve(lambda: V.tensor_copy(out=ytok[b][:, j, 512:1024], in_=pt[pb][:]),
                                      [('pt', pb)], [('ytok', b, j, 1)])
                    S.dma('sp', yout[g * 512:(g + 1) * 512, :].rearrange("(j p) d -> p j d", p=128), ytok[b][:],
                          reads=[('ytok', b, j, h) for j in range(4) for h in range(2)])
            S.barrier()

        def phase_mod(l):
            with ExitStack() as st:
                wt = [sb(st, [128, 8, 1152], BF16, "adaw") for _ in range(2)]
                pm = ps(st, [128, 72, 4], F32, "pm")
                src = ada_w[l].rearrange("(k p) f -> p k f", p=128)
                for cg in range(8):
                    b = cg % 2
                    S.dma('pool', wt[b][:], src[:, :, cg * 1152:(cg + 1) * 1152], writes=[('adaw', b)])
                    for jj in range(9):
                        j = cg * 9 + jj
                        S.pe([(lambda k=k: P.matmul(pm[:, j, :], wt[b][:, k, jj * 128:(jj + 1) * 128], scT_b[:, k, :],
                                                    start=(k == 0), stop=(k == 7))) for k in range(8)],
                             [('adaw', b), 'scTb'], ['pm'])
                allpm = ['pm']
                S.dve(lambda: V.tensor_tensor(out=modsb[:], in0=pm[:],
                                              in1=adab[:, l, :].unsqueeze(2).to_broadcast([128, 72, 4]), op=ALU.add),
                      allpm + ['modsb'], ['modsb'])
                for m, cf in ((1, None), (4, None), (7, None), (2, 0.5 / ALPHA), (5, 1.0 / ALPHA), (8, 0.5 / ALPHA)):
                    sl = modsb[:, m * 8:(m + 1) * 8, :]
                    if cf is None:
                        S.dve(lambda: V.tensor_scalar(out=sl, in0=sl, scalar1=1.0, scalar2=None, op0=ALU.add),
                              ['modsb'], ['modsb'])
                    else:
                        S.dve(lambda: V.tensor_scalar(out=sl, in0=sl, scalar1=1.0, scalar2=cf, op0=ALU.add,
                                                      op1=ALU.mult), ['modsb'], ['modsb'])
            S.barrier()

        def ln_tail(st_objs, l, m, b, xt, rb, rq, pst, tmps, TT, rbk='rb', rqk='rq'):
            mean_sb, var, mr = tmps[0:3]
            S.pe([(lambda k=k: P.matmul(pst[:, 0:TT], ones_b[:], rb[:, k, :], start=(k == 0), stop=(k == 7)))
                  for k in range(8)] +
                 [(lambda k=k: P.matmul(pst[:, TT:2 * TT], ones_b[:], rq[:, k, :], start=(k == 0), stop=(k == 7)))
                  for k in range(8)], [rbk, rqk], ['pst'])
            S.dve(lambda: V.tensor_copy(out=mean_sb[:], in_=pst[:, 0:TT]), ['pst'], ['mean'])
            S.dve(lambda: V.tensor_tensor(out=var[:], in0=mean_sb[:], in1=mean_sb[:], op=ALU.mult), ['mean'], ['var'])
            S.dve(lambda: V.tensor_tensor(out=var[:], in0=pst[:, TT:2 * TT], in1=var[:], op=ALU.subtract),
                  ['pst', 'var'], ['var'])
            S.act(lambda: A.activation(out=var[:], in_=var[:], func=AF.Sqrt, bias=eps_ln[:, 0:1], scale=1.0),
                  ['var'], ['var'])
            S.dve(lambda: V.reciprocal(out=var[:], in_=var[:]), ['var'], ['var'])
            S.dve(lambda: V.tensor_tensor(out=mr[:], in0=mean_sb[:], in1=var[:], op=ALU.mult), ['mean', 'var'], ['mr'])
            li = l * 3 + m
            for dk in range(8):
                x = xt[b][:, dk, :]
                S.dve(lambda: V.scalar_tensor_tensor(out=x, in0=x, scalar=lng[:, li, dk:dk + 1], in1=var[:],
                                                     op0=ALU.mult, op1=ALU.mult), [('xt', b, dk), 'var'], [('xt', b, dk)])
                tq = tmps[3 + dk % 2]
                S.pool(lambda: G.tensor_scalar(out=tq[:], in0=mr[:], scalar1=nlng[:, li, dk:dk + 1],
                                               scalar2=lnb[:, li, dk:dk + 1], op0=ALU.mult, op1=ALU.add),
                       ['mr'], [('tq', dk % 2)])
                S.pool(lambda: G.tensor_tensor(out=x, in0=x, in1=tq[:], op=ALU.add), [('xt', b, dk), ('tq', dk % 2)],
                       [('xt', b, dk)])

        def phase_ffn(l, which):
            TT = 256
            NT = T // TT
            jS, jB, jG = (1, 0, 2) if which == 0 else (7, 6, 8)
            lnm = 0 if which == 0 else 2
            with ExitStack() as st:
                wi = sb(st, [128, 8, 2 * DFF], BF16, "wi")
                wo = sb(st, [128, 22, D], BF16, "wo")
                xt = [sb(st, [128, 8, TT], F32, "xt") for _ in range(3)]
                hT = [sb(st, [128, 8, TT], BF16, "hT") for _ in range(2)]
                aT = sb(st, [128, 22, TT], BF16, "aT")
                rb = sb(st, [128, 8, TT], BF16, "rb")
                rq = sb(st, [128, 8, TT], BF16, "rq")
                sg = [sb(st, [128, TT], BF16, "sg") for _ in range(2)]
                tmps = [sb(st, [128, TT], F32, "lnt") for _ in range(5)]
                gu = [ps(st, [128, 512], F32, "gu") for _ in range(3)]
                pf = [ps(st, [128, 512], F32, "pf") for _ in range(4)]
                pst = ps(st, [128, 512], F32, "pst")
                wsrc = ffn_wi[which][l].rearrange("(k p) f -> p k f", p=128)
                osrc = ffn_wo[which][l].rearrange("(c p) d -> p c d", p=128)
                for cg in range(11):
                    S.dma('pool', wi[:, :, cg * 256:(cg + 1) * 256], wsrc[:, :, cg * 256:(cg + 1) * 256],
                          writes=[('wig', cg)])
                    S.dma('pool', wi[:, :, DFF + cg * 256:DFF + (cg + 1) * 256],
                          wsrc[:, :, DFF + cg * 256:DFF + (cg + 1) * 256], writes=[('wiu', cg)])
                for cg in range(11):
                    S.dma('pool', wo[:, cg * 2:cg * 2 + 2, :], osrc[:, cg * 2:cg * 2 + 2, :], writes=[('wo', cg)])

                def load(t):
                    S.dma('sp', xt[t % 3][:], xT_v[:, :, t * TT:(t + 1) * TT], writes=[('xt', t % 3, k_) for k_ in range(8)])

                def modulate(t):
                    b = t % 2
                    xb = t % 3
                    s = seq_of(t * TT)[0]
                    for k in range(8):
                        S.dve(lambda: V.tensor_scalar(out=hT[b][:, k, :], in0=xt[xb][:, k, :],
                                                      scalar1=modsb[:, jS * 8 + k, s:s + 1],
                                                      scalar2=modsb[:, jB * 8 + k, s:s + 1],
                                                      op0=ALU.mult, op1=ALU.add), [('xt', xb, k)], [('hT', b)])

                def gu_part(t):
                    b = t % 2
                    for c in range(22):
                        g = gu[c % 3]
                        S.pe([(lambda k=k: P.matmul(g[:, 0:TT], wi[:, k, c * 128:(c + 1) * 128], hT[b][:, k, :],
                                                    start=(k == 0), stop=(k == 7))) for k in range(8)] +
                             [(lambda k=k: P.matmul(g[:, TT:2 * TT], wi[:, k, DFF + c * 128:DFF + (c + 1) * 128],
                                                    hT[b][:, k, :], start=(k == 0), stop=(k == 7))) for k in range(8)],
                             [('hT', b), ('wig', c // 2), ('wiu', c // 2)], [('gu', c % 3)])
                        S.act(lambda: A.activation(out=sg[c % 2][:], in_=g[:, 0:TT], func=AF.Silu),
                              [('gu', c % 3)], [('sg', c % 2)])
                        S.dve(lambda: V.tensor_tensor(out=aT[:, c, :], in0=g[:, TT:2 * TT], in1=sg[c % 2][:],
                                                      op=ALU.mult), [('gu', c % 3), ('sg', c % 2)], [('aT', c)])

                def down_part(t):
                    b = t % 3
                    s = seq_of(t * TT)[0]
                    for dp in range(4):
                        bank = pf[dp]
                        S.pe([(lambda c=c, h_=h_: P.matmul(bank[:, h_ * TT:(h_ + 1) * TT],
                                                           wo[:, c, (2 * dp + h_) * 128:(2 * dp + h_ + 1) * 128],
                                                           aT[:, c, :], start=(c == 0), stop=(c == 21)))
                              for h_ in range(2) for c in range(22)],
                             [('aT', c) for c in range(22)] + [('wo', cg) for cg in range(11)], [('pf', dp)])
                        for h_ in range(2):
                            dk = 2 * dp + h_
                            pp = bank[:, h_ * TT:(h_ + 1) * TT]
                            x = xt[b][:, dk, :]
                            S.dve(lambda: V.scalar_tensor_tensor(out=x, in0=pp, scalar=modsb[:, jG * 8 + dk, s:s + 1],
                                                                 in1=x, op0=ALU.mult, op1=ALU.add),
                                  [('pf', dp), ('xt', b, dk)], [('xt', b, dk)])
                            S.pool(lambda: G.tensor_copy(out=rb[:, dk, :], in_=x), [('xt', b, dk)], ['rb'])
                            S.pool(lambda: G.tensor_tensor(out=rq[:, dk, :], in0=x, in1=x, op=ALU.mult),
                                   [('xt', b, dk)], ['rq'])

                def tail_part(t):
                    b = t % 3
                    ln_tail(None, l, lnm, b, xt, rb, rq, pst, tmps, TT)
                    S.dma('sp', xT_v[:, :, t * TT:(t + 1) * TT], xt[b][:], reads=[('xt', b, k_) for k_ in range(8)])

                for t in range(min(3, NT)):
                    load(t)
                modulate(0)
                gu_part(0)
                if NT > 1:
                    modulate(1)
                down_part(0)
                for t in range(NT):
                    if t + 1 < NT:
                        gu_part(t + 1)
                    if t + 2 < NT:
                        modulate(t + 2)
                    tail_part(t)
                    if t + 3 < NT:
                        load(t + 3)
                    if t + 1 < NT:
                        down_part(t + 1)
            S.barrier()

        def phase_wout(l):
            TT = 256
            NT = T // TT
            with ExitStack() as st:
                wo = sb(st, [128, 8, D], BF16, "wout")
                xt = [sb(st, [128, 8, TT], F32, "xt") for _ in range(3)]
                oT = [sb(st, [128, 8, TT], BF16, "oT") for _ in range(3)]
                rb = [sb(st, [128, 8, TT], BF16, "rb") for _ in range(2)]
                rq = [sb(st, [128, 8, TT], BF16, "rq") for _ in range(2)]
                tmps = [sb(st, [128, TT], F32, "lnt") for _ in range(5)]
                pf = [ps(st, [128, 512], F32, "pf") for _ in range(4)]
                pst = ps(st, [128, 512], F32, "pst")
                S.dma('pool', wo[:], w_out[l].rearrange("(k p) d -> p k d", p=128), writes=['wout'])

                def load(t):
                    S.dma('sp', xt[t % 3][:], xT_v[:, :, t * TT:(t + 1) * TT], writes=[('xt', t % 3, k_) for k_ in range(8)])
                    S.dma('sp', oT[t % 3][:], oT_v[:, :, t * TT:(t + 1) * TT], writes=[('oT', t % 3)])

                def mm(t):
                    b = t % 3
                    b2 = t % 2
                    s = seq_of(t * TT)[0]
                    for dp in range(4):
                        bank = pf[dp]
                        S.pe([(lambda k=k, h_=h_: P.matmul(bank[:, h_ * TT:(h_ + 1) * TT],
                                                           wo[:, k, (2 * dp + h_) * 128:(2 * dp + h_ + 1) * 128],
                                                           oT[b][:, k, :], start=(k == 0), stop=(k == 7)))
                              for h_ in range(2) for k in range(8)], [('oT', b), 'wout'], [('pf', dp)])
                        for h_ in range(2):
                            dk = 2 * dp + h_
                            pp = bank[:, h_ * TT:(h_ + 1) * TT]
                            x = xt[b][:, dk, :]
                            S.dve(lambda: V.scalar_tensor_tensor(out=x, in0=pp, scalar=modsb[:, 5 * 8 + dk, s:s + 1],
                                                                 in1=x, op0=ALU.mult, op1=ALU.add),
                                  [('pf', dp), ('xt', b, dk)], [('xt', b, dk)])
                            S.pool(lambda: G.tensor_copy(out=rb[b2][:, dk, :], in_=x), [('xt', b, dk)], [('rb', b2)])
                            S.pool(lambda: G.tensor_tensor(out=rq[b2][:, dk, :], in0=x, in1=x, op=ALU.mult),
                                   [('xt', b, dk)], [('rq', b2)])

                def tail(t):
                    b = t % 3
                    b2 = t % 2
                    ln_tail(None, l, 1, b, xt, rb[b2], rq[b2], pst, tmps, TT, rbk=('rb', b2), rqk=('rq', b2))
                    S.dma('sp', xT_v[:, :, t * TT:(t + 1) * TT], xt[b][:], reads=[('xt', b, k_) for k_ in range(8)])

                for t in range(min(3, NT)):
                    load(t)
                mm(0)
                for t in range(NT):
                    if t + 1 < NT:
                        mm(t + 1)
                    tail(t)
                    if t + 3 < NT:
                        load(t + 3)
            S.barrier()

        def phase_m1(l):
            TT = 512
            NT = T // TT
            with ExitStack() as st:
                win = sb(st, [128, 8, NIN], BF16, "win")
                xt = [sb(st, [128, 8, TT + 2], F32, "xt") for _ in range(2)]
                hT = [sb(st, [128, 8, TT + 2], BF16, "hT") for _ in range(2)]
                qb = [sb(st, [64, TT], BF16, "qb") for _ in range(2)]
                t1 = sb(st, [16, TT], F32, "t1")
                t2 = sb(st, [16, TT], F32, "t2")
                cs = [sb(st, [16, 2, TT], F32, "cs") for _ in range(2)]
                vx = [sb(st, [128, 4, 2, 65], BF16, "vx") for _ in range(2)]
                zs = [sb(st, [128, TT + 2], F32, "zs") for _ in range(2)]
                zc = [sb(st, [128, TT], F32, "zc") for _ in range(2)]
                x0b = sb(st, [128, 4, TT], BF16, "x0b")
                ub = sb(st, [128, 4, TT], BF16, "ub")
                tok0b = [sb(st, [128, 4, 512], BF16, "tok0") for _ in range(2)]
                psq = [ps(st, [64, 512], F32, "psq")] * 2
                psr = ps(st, [16, 512], F32, "psr")
                psv = ps(st, [128, 512], F32, "psv")
                psz = [ps(st, [128, 512], F32, "psz") for _ in range(2)]
                pszh = [ps(st, [128, 16], F32, "pszh") for _ in range(2)]
                ptr = ps(st, [128, 512], BF16, "ptr")
                S.dma('pool', win[:, :, 0:768], w_in[l].rearrange("(k p) f -> p k f", p=128)[:, :, 0:768],
                      writes=['win0'])
                for i in range(3):
                    S.dma('pool', win[:, :, 768 + i * 512:768 + (i + 1) * 512],
                          w_in[l].rearrange("(k p) f -> p k f", p=128)[:, :, 768 + i * 512:768 + (i + 1) * 512],
                          writes=[('win', i)])
                for b in range(2):
                    S.pool(lambda: G.memset(vx[b][:], 1.0), [], [('vx', b)])
                wkeys = ['win0'] + [('win', i) for i in range(3)]

                def load(g):
                    b = g % 2
                    tok0 = g * TT
                    s, s0, ln = seq_of(tok0)
                    lo = 1 if tok0 == s0 else 0
                    hi = 1 if tok0 + TT == s0 + ln else 0
                    S.dma('sp', xt[b][:, :, lo:TT + 2 - hi], xT_v[:, :, tok0 - 1 + lo:tok0 + TT + 1 - hi],
                          writes=[('xt', b)])
                    S.dma('sp', cs[b][:, 0, :], cd["cos_t"][:, tok0 - s0:tok0 - s0 + TT], writes=[('cs', b, 0)])
                    S.dma('sp', cs[b][:, 1, :], cd["sin_t"][:, tok0 - s0:tok0 - s0 + TT], writes=[('cs', b, 1)])

                def modulate(g):
                    b = g % 2
                    tok0 = g * TT
                    s, s0, ln = seq_of(tok0)
                    lo = 1 if tok0 == s0 else 0
                    hi = 1 if tok0 + TT == s0 + ln else 0
                    for k in range(8):
                        S.dve(lambda: V.tensor_scalar(out=hT[b][:, k, lo:TT + 2 - hi], in0=xt[b][:, k, lo:TT + 2 - hi],
                                                      scalar1=modsb[:, 4 * 8 + k, s:s + 1],
                                                      scalar2=modsb[:, 3 * 8 + k, s:s + 1],
                                                      op0=ALU.mult, op1=ALU.add), [('xt', b)], [('hT', b)])
                    if lo:
                        S.dve(lambda: V.memset(hT[b][:, :, 0:1], 0.0), [], [('hT', b)])
                    if hi:
                        S.dve(lambda: V.memset(hT[b][:, :, TT + 1:TT + 2], 0.0), [], [('hT', b)])

                load(0)
                modulate(0)
                nq = 0
                nz = 0
                for g in range(NT):
                    b = g % 2
                    tok0 = g * TT
                    if g + 1 < NT:
                        load(g + 1)
                    for hd in range(10):
                        qi = nq % 2
                        nq += 1
                        S.pe([(lambda k=k: P.matmul(psq[qi][:], win[:, k, hd * 64:(hd + 1) * 64], hT[b][:, k, 1:TT + 1],
                                                    start=(k == 0), stop=(k == 7))) for k in range(8)],
                             [('hT', b), 'win0'], ['psq'])
                        S.act(lambda: A.copy(out=qb[qi][:], in_=psq[qi][:]), ['psq'], [('qb', qi)])
                        S.pe([lambda: P.matmul(psr[:], rt_sb[:], qb[qi][:], start=True, stop=True)],
                             [('qb', qi)], ['psr'])
                        S.dve(lambda: V.tensor_tensor(out=t1[:], in0=psq[qi][0:16, :], in1=cs[b][:, 0, :], op=ALU.mult),
                              ['psq', ('cs', b, 0)], ['t1'])
                        S.dve(lambda: V.tensor_tensor(out=t2[:], in0=psr[:], in1=cs[b][:, 1, :], op=ALU.mult),
                              ['psr', ('cs', b, 1)], ['t2'])
                        S.pool(lambda: G.tensor_tensor(out=qb[qi][0:16, :], in0=t1[:], in1=t2[:], op=ALU.add),
                               ['t1', 't2', ('qb', qi)], [('qb', qi)])
                        S.dma('sp', q_d[hd, :, tok0:tok0 + TT], qb[qi][:], reads=[('qb', qi)])
                    for blk in range(4):
                        S.pe([(lambda k=k: P.matmul(psv[:, blk * 128:(blk + 1) * 128],
                                                    hT[b][:, k, 1 + blk * 128:1 + (blk + 1) * 128],
                                                    win[:, k, 640:768], start=(k == 0), stop=(k == 7)))
                              for k in range(8)], [('hT', b), 'win0'], ['psv'])
                    S.act(lambda: A.copy(out=vx[b][:, :, :, 0:64],
                                         in_=psv[:].rearrange("p (b k e) -> p b k e", b=4, k=2)),
                          ['psv', ('vx', b)], [('vx', b)])
                    S.dma('sp', v_d[tok0:tok0 + TT, :].rearrange("(b p) e -> p b e", p=128),
                          vx[b][:].rearrange("p b k e -> p b (k e)"), reads=[('vx', b)])
                    for i in range(4):
                        for part in range(3):
                            cz = part * 4 + i
                            zi = nz % 2
                            nz += 1
                            c0 = 768 + cz * 128
                            S.pe([(lambda k=k: P.matmul(psz[zi][:], win[:, k, c0:c0 + 128], hT[b][:, k, 1:TT + 1],
                                                        start=(k == 0), stop=(k == 7))) for k in range(8)] +
                                 [(lambda k=k: P.matmul(pszh[zi][:, 0:2], win[:, k, c0:c0 + 128],
                                                        hT[b][:, k, 0:TT + 2:TT + 1],
                                                        start=(k == 0), stop=(k == 7))) for k in range(8)],
                                 [('hT', b)] + wkeys, [('psz', zi)])
                            S.act(lambda: A.copy(out=zs[zi][:, 1:TT + 1], in_=psz[zi][:]), [('psz', zi)], [('zs', zi)])
                            S.act(lambda: A.copy(out=zs[zi][:, 0:TT + 2:TT + 1], in_=pszh[zi][:, 0:2]),
                                  [('psz', zi)], [('zs', zi)])
                            zz = zc[part % 2] if part > 0 else zc[0]
                            zkey = ('zc', part % 2 if part > 0 else 0)
                            S.pool(lambda: G.tensor_scalar(out=zz[:], in0=zs[zi][:, 1:TT + 1],
                                                           scalar1=cw[:, l * 3 + 1, cz:cz + 1],
                                                           scalar2=cbs[:, l, cz:cz + 1], op0=ALU.mult, op1=ALU.add),
                                   [('zs', zi)], [zkey])
                            S.dve(lambda: V.scalar_tensor_tensor(out=zz[:], in0=zs[zi][:, 0:TT],
                                                                  scalar=cw[:, l * 3 + 0, cz:cz + 1], in1=zz[:],
                                                                  op0=ALU.mult, op1=ALU.add), [('zs', zi), zkey], [zkey])
                            if part == 0:
                                S.dve(lambda: V.scalar_tensor_tensor(out=x0b[:, i, :], in0=zs[zi][:, 2:TT + 2],
                                                                     scalar=cw[:, l * 3 + 2, cz:cz + 1], in1=zz[:],
                                                                     op0=ALU.mult, op1=ALU.add),
                                      [('zs', zi), zkey], [('x0b', i)])
                            else:
                                S.dve(lambda: V.scalar_tensor_tensor(out=zz[:], in0=zs[zi][:, 2:TT + 2],
                                                                     scalar=cw[:, l * 3 + 2, cz:cz + 1], in1=zz[:],
                                                                     op0=ALU.mult, op1=ALU.add),
                                      [('zs', zi), zkey], [zkey])
                        S.pool(lambda: G.tensor_tensor(out=ub[:, i, :], in0=zc[0][:], in1=zc[1][:], op=ALU.mult),
                               [('zc', 0), ('zc', 1)], [('ub', i)])
                    for ti, (srcb, dst, key) in enumerate(((x0b, x0_d, 'x0b'), (ub, u_d, 'ub'))):
                        tb = tok0b[ti]
                        for blk in range(4):
                            S.pe([(lambda i=i: P.transpose(ptr[:, i * 128:(i + 1) * 128],
                                                           srcb[:, i, blk * 128:(blk + 1) * 128], ident_b[:]))
                                  for i in range(4)], [(key, i) for i in range(4)], ['ptr'])
                            S.act(lambda: A.copy(out=tb[:, blk, :], in_=ptr[:]), ['ptr'], [('tokb', ti)])
                        S.dma('sp', dst[tok0:tok0 + TT, :].rearrange("(b p) c -> p b c", p=128), tb[:],
                              reads=[('tokb', ti)])
                    if g + 1 < NT:
                        modulate(g + 1)
            S.barrier()

        def phase_attn(l):
            TT = 512
            NT = T // TT
            with ExitStack() as st:
                qT = [sb(st, [64, 8, TT], BF16, "qT") for _ in range(2)]
                kT = [sb(st, [64, 2, TT + 256], BF16, "kT") for _ in range(2)]
                vv = [sb(st, [128, 6, 130], BF16, "vv") for _ in range(2)]
                pT = [sb(st, [128, 3, 128], BF16, "pT") for _ in range(3)]
                gA = sb(st, [128, 512], F32, "gA")
                den = sb(st, [128, 8], F32, "den")
                osb = sb(st, [128, 8, 64], F32, "osb")
                osq = sb(st, [128, 512], F32, "osq")
                ssq = sb(st, [128, 1], F32, "ssq")
                onb = sb(st, [128, 512], BF16, "onb")
                oTs = [sb(st, [128, 4, TT], BF16, "oTs") for _ in range(2)]
                pS = [ps(st, [128, 3, 128], F32, "pS") for _ in range(3)]
                pO = [ps(st, [128, 4, 65], F32, "pO") for _ in range(4)]
                ptr = ps(st, [128, 512], BF16, "ptr")
                S.dma('sp', gA[:], grp_g[l, 0:512].partition_broadcast(128), writes=['gA'])

                def load(g):
                    b = g % 2
                    tok0 = g * TT
                    s, s0, ln = seq_of(tok0)
                    lo = 128 if tok0 == s0 else 0
                    hi = 128 if tok0 + TT == s0 + ln else 0
                    S.dma('sp', qT[b][:], q_d[0:8, :, tok0:tok0 + TT].rearrange("h p t -> p h t"), writes=[('qT', b)])
                    S.dma('sp', kT[b][:, :, lo:TT + 256 - hi],
                          q_d[8:10, :, tok0 - 128 + lo:tok0 + TT + 128 - hi].rearrange("h p t -> p h t"),
                          writes=[('kT', b)])
                    S.dma('sp', vv[b][:, lo // 128:6 - hi // 128, :],
                          v_d[tok0 - 128 + lo:tok0 + TT + 128 - hi, :].rearrange("(b p) e -> p b e", p=128),
                          writes=[('vv', b)])

                def emit_norm(b, qb_, po, pok, tok0, do_store):
                    hk = [pok + (0,), pok + (1,)]
                    for hh in range(2):
                        S.dve(lambda: V.tensor_scalar(out=den[:, hh * 4:(hh + 1) * 4], in0=po[hh][:, :, 64],
                                                      scalar1=1.0, scalar2=None, op0=ALU.add), hk + ['den'], ['den'])
                    S.dve(lambda: V.reciprocal(out=den[:], in_=den[:]), ['den'], ['den'])
                    for hh in range(2):
                        S.dve(lambda: V.tensor_tensor(out=osb[:, hh * 4:(hh + 1) * 4, :], in0=po[hh][:, :, 0:64],
                                                      in1=den[:, hh * 4:(hh + 1) * 4].unsqueeze(2).to_broadcast(
                                                          [128, 4, 64]), op=ALU.mult),
                              hk + ['den', 'osb'], ['osb'])
                    of = osb[:].rearrange("p h e -> p (h e)")
                    S.pool(lambda: G.tensor_tensor(out=osq[:], in0=of, in1=of, op=ALU.mult), ['osb'], ['osq'])
                    S.dve(lambda: V.reduce_sum(out=ssq[:], in_=osq[:], axis=AX.X), ['osq'], ['ssq'])
                    S.dve(lambda: V.tensor_scalar(out=ssq[:], in0=ssq[:], scalar1=1.0 / 512, scalar2=RMS_EPS,
                                                  op0=ALU.mult, op1=ALU.add), ['ssq'], ['ssq'])
                    S.act(lambda: A.activation(out=ssq[:], in_=ssq[:], func=AF.Ln), ['ssq'], ['ssq'])
                    S.act(lambda: A.activation(out=ssq[:], in_=ssq[:], func=AF.Exp, scale=-0.5), ['ssq'], ['ssq'])
                    S.dve(lambda: V.scalar_tensor_tensor(out=onb[:], in0=of, scalar=ssq[:, 0:1], in1=gA[:],
                                                         op0=ALU.mult, op1=ALU.mult),
                          ['osb', 'ssq', 'gA', 'onb'], ['onb'])
                    S.pe([(lambda c=c: P.transpose(ptr[:, c * 128:(c + 1) * 128], onb[:, c * 128:(c + 1) * 128],
                                                   ident_b[:])) for c in range(4)], ['onb'], ['ptr'])
                    S.act(lambda: A.copy(out=oTs[b][:, :, qb_ * 128:(qb_ + 1) * 128],
                                         in_=ptr[:].rearrange("p (c t) -> p c t", c=4)), ['ptr'], [('oTs', b)])
                    if do_store:
                        S.dma('sp', oT_v[:, 0:4, tok0:tok0 + TT], oTs[b][:], reads=[('oTs', b)])

                load(0)
                npi = 0
                npo = 0
                pending = None
                for g in range(NT):
                    b = g % 2
                    tok0 = g * TT
                    s, s0, ln = seq_of(tok0)
                    if g + 1 < NT:
                        load(g + 1)
                    for qb_ in range(4):
                        gpos = tok0 + qb_ * 128
                        jlo = 1 if gpos == s0 else 0
                        jhi = 2 if gpos + 128 == s0 + ln else 3
                        po = (pO[(npo % 2) * 2], pO[(npo % 2) * 2 + 1])
                        pok = ('pO', npo % 2)
                        npo += 1
                        for h in range(8):
                            kv = h // 4
                            pi = npi % 3
                            npi += 1
                            S.pe([(lambda jj=jj: P.matmul(pS[pi][:, jj, :],
                                                          kT[b][:, kv, (qb_ + jj) * 128:(qb_ + jj + 1) * 128],
                                                          qT[b][:, h, qb_ * 128:(qb_ + 1) * 128], start=True, stop=True))
                                  for jj in range(jlo, jhi)], [('qT', b), ('kT', b)], [('pS', pi)])
                            S.act(lambda: A.activation(out=pT[pi][:, jlo:jhi, :], in_=pS[pi][:, jlo:jhi, :], func=AF.Exp,
                                                       bias=nsink[:, l * 8 + h:l * 8 + h + 1], scale=0.125),
                                  [('pS', pi)], [('pT', pi)])
                            if jlo == 0:
                                S.pool(lambda: G.tensor_tensor(out=pT[pi][:, 0, :], in0=pT[pi][:, 0, :], in1=maskp[:],
                                                               op=ALU.mult), [('pT', pi)], [('pT', pi)])
                            if jhi == 3:
                                S.pool(lambda: G.tensor_tensor(out=pT[pi][:, 2, :], in0=pT[pi][:, 2, :], in1=maskn[:],
                                                               op=ALU.mult), [('pT', pi)], [('pT', pi)])
                            S.pe([(lambda jj=jj: P.matmul(po[h // 4][:, h % 4, :], pT[pi][:, jj, :],
                                                          vv[b][:, qb_ + jj, kv * 65:(kv + 1) * 65],
                                                          start=(jj == jlo), stop=(jj == jhi - 1)))
                                  for jj in range(jlo, jhi)], [('pT', pi), ('vv', b)], [pok + (h // 4,)])
                        if pending is not None:
                            emit_norm(*pending)
                        pending = (b, qb_, po, pok, tok0, qb_ == 3)
                if pending is not None:
                    emit_norm(*pending)
            S.barrier()

        def fft_stage_a(st, li, srcs, dstA, tag):
            L = LS[li]
            H1 = L // 128
            nk = 2 * NK1[li]
            fa = sb(st, [H1, nk], BF16, "fa")
            S.dma('sp', fa[:], cd[f"fa{li}"], writes=[(tag, 'fa')])
            xc = [sb(st, [H1, 16, 512], BF16, "xc") for _ in range(2)]
            asb = [sb(st, [nk, 16, 512], BF16, "asb") for _ in range(2)]
            pa = [ps(st, [nk, 512], F32, "pa") for _ in range(3)]
            n = 0
            it = 0
            for sq in range(2):
                sv = srcs[sq].rearrange("(a b) c -> a b c", b=128)
                for ch in range(8):
                    bb = it % 2
                    it += 1
                    S.dma('sp', xc[bb][:], sv[:, ch * 16:(ch + 1) * 16, :], writes=[(tag, 'xc', bb)])
                    for j in range(16):
                        pi = n % 3
                        n += 1
                        S.pe([lambda: P.matmul(pa[pi][:], fa[:], xc[bb][:, j, :], start=True, stop=True)],
                             [(tag, 'fa'), (tag, 'xc', bb)], [(tag, 'pa', pi)])
                        if j % 2 == 0:
                            S.act(lambda: A.copy(out=asb[bb][:, j, :], in_=pa[pi][:]), [(tag, 'pa', pi)],
                                  [(tag, 'asb', bb, j)])
                        else:
                            S.dve(lambda: V.tensor_copy(out=asb[bb][:, j, :], in_=pa[pi][:]), [(tag, 'pa', pi)],
                                  [(tag, 'asb', bb, j)])
                    S.dma('sp', dstA[sq, :, ch * 16:(ch + 1) * 16, :], asb[bb][:],
                          reads=[(tag, 'asb', bb, j) for j in range(16)])

        def fft_stage_c(st, li, srcA, mode, dstB, tag):
            nk1 = NK1[li]
            mk = [sb(st, [128, 6, 128], BF16, "mk") for _ in range(2)]
            ak = [sb(st, [128, 2, 512], BF16, "ak") for _ in range(2)]
            kf = [sb(st, [128, 2, 512], F32, "kf") for _ in range(2)]
            tt = [sb(st, [128, 512], F32, "tt") for _ in range(8)]
            yb = [sb(st, [128, 2, 512], BF16, "yb") for _ in range(2)]
            bsb = [sb(st, [128, 2, 512], BF16, "bsb") for _ in range(2)]
            px = [ps(st, [128, 512], F32, "px") for _ in range(4)]
            pb = [ps(st, [128, 512], F32, "pb") for _ in range(4)]
            it = 0
            for k1 in range(nk1):
                mb = k1 % 2
                S.dma('sp', mk[mb][:], cd[f"mm{li}"][k1], writes=[(tag, 'mk', mb)])
                if mode == 'conv':
                    S.dma('sp', kf[mb][:], Kf_d[li][k1], writes=[(tag, 'kf', mb)])
                for sq in range(2):
                    ab = it % 2
                    it += 1
                    av = srcA[sq].rearrange("(r k) n c -> n r k c", r=2)
                    S.dma('sp', ak[ab][:], av[:, :, k1, :], writes=[(tag, 'ak', ab)])
                    pr, pi_ = px[ab * 2], px[ab * 2 + 1]
                    S.pe([lambda: P.matmul(pr[:], mk[mb][:, 0, :], ak[ab][:, 0, :], start=True, stop=False),
                          lambda: P.matmul(pr[:], mk[mb][:, 2, :], ak[ab][:, 1, :], start=False, stop=True),
                          lambda: P.matmul(pi_[:], mk[mb][:, 1, :], ak[ab][:, 0, :], start=True, stop=False),
                          lambda: P.matmul(pi_[:], mk[mb][:, 0, :], ak[ab][:, 1, :], start=False, stop=True)],
                         [(tag, 'mk', mb), (tag, 'ak', ab)], [(tag, 'px', ab)])
                    if mode == 'filter':
                        kk = kf[mb]
                        if sq == 0:
                            S.act(lambda: A.copy(out=kk[:, 0, :], in_=pr[:]), [(tag, 'px', ab)], [(tag, 'kf', mb)])
                            S.dve(lambda: V.tensor_copy(out=kk[:, 1, :], in_=pi_[:]), [(tag, 'px', ab)],
                                  [(tag, 'kf', mb)])
                        else:
                            S.dve(lambda: V.tensor_tensor(out=kk[:, 0, :], in0=pr[:], in1=kk[:, 0, :], op=ALU.add),
                                  [(tag, 'px', ab), (tag, 'kf', mb)], [(tag, 'kf', mb)])
                            S.dve(lambda: V.scalar_tensor_tensor(out=kk[:, 1, :], in0=pi_[:], scalar=-1.0,
                                                                 in1=kk[:, 1, :], op0=ALU.mult, op1=ALU.add),
                                  [(tag, 'px', ab), (tag, 'kf', mb)], [(tag, 'kf', mb)])
                            S.dma('sp', Kf_d[li][k1], kk[:], reads=[(tag, 'kf', mb)])
                        continue
                    kk = kf[mb]
                    S.dve(lambda: V.tensor_tensor(out=tt[ab * 4 + 0][:], in0=pr[:], in1=kk[:, 0, :], op=ALU.mult),
                          [(tag, 'px', ab), (tag, 'kf', mb)], [(tag, 'tt', ab * 4 + 0)])
                    S.dve(lambda: V.tensor_tensor(out=tt[ab * 4 + 1][:], in0=pi_[:], in1=kk[:, 1, :], op=ALU.mult),
                          [(tag, 'px', ab), (tag, 'kf', mb)], [(tag, 'tt', ab * 4 + 1)])
                    S.pool(lambda: G.tensor_tensor(out=yb[ab][:, 0, :], in0=tt[ab * 4 + 0][:], in1=tt[ab * 4 + 1][:], op=ALU.subtract),
                           [(tag, 'tt', ab * 4 + 0), (tag, 'tt', ab * 4 + 1)], [(tag, 'yb', ab)])
                    S.dve(lambda: V.tensor_tensor(out=tt[ab * 4 + 2][:], in0=pr[:], in1=kk[:, 1, :], op=ALU.mult),
                          [(tag, 'px', ab), (tag, 'kf', mb)], [(tag, 'tt', ab * 4 + 2)])
                    S.dve(lambda: V.tensor_tensor(out=tt[ab * 4 + 3][:], in0=pi_[:], in1=kk[:, 0, :], op=ALU.mult),
                          [(tag, 'px', ab), (tag, 'kf', mb)], [(tag, 'tt', ab * 4 + 3)])
                    S.pool(lambda: G.tensor_tensor(out=yb[ab][:, 1, :], in0=tt[ab * 4 + 2][:], in1=tt[ab * 4 + 3][:], op=ALU.add),
                           [(tag, 'tt', ab * 4 + 2), (tag, 'tt', ab * 4 + 3)], [(tag, 'yb', ab)])
                    br, bi = pb[ab * 2], pb[ab * 2 + 1]
                    S.pe([lambda: P.matmul(br[:], mk[mb][:, 3, :], yb[ab][:, 0, :], start=True, stop=False),
                          lambda: P.matmul(br[:], mk[mb][:, 4, :], yb[ab][:, 1, :], start=False, stop=True),
                          lambda: P.matmul(bi[:], mk[mb][:, 3, :], yb[ab][:, 1, :], start=True, stop=False),
                          lambda: P.matmul(bi[:], mk[mb][:, 5, :], yb[ab][:, 0, :], start=False, stop=True)],
                         [(tag, 'mk', mb), (tag, 'yb', ab)], [(tag, 'pb', ab)])
                    S.act(lambda: A.copy(out=bsb[ab][:, 0, :], in_=br[:]), [(tag, 'pb', ab)], [(tag, 'bsb', ab, 0)])
                    S.act(lambda: A.copy(out=bsb[ab][:, 1, :], in_=bi[:]), [(tag, 'pb', ab)], [(tag, 'bsb', ab, 1)])
                    bv = dstB[sq].rearrange("(r k) n c -> n r k c", r=2)
                    S.dma('sp', bv[:, :, k1, :], bsb[ab][:], reads=[(tag, 'bsb', ab, 0), (tag, 'bsb', ab, 1)])

        def fft_stage_ah(st, li, srcB, dsts, tag):
            L = LS[li]
            H1 = L // 128
            nk = 2 * NK1[li]
            ga = sb(st, [nk, H1], BF16, "ga")
            S.dma('sp', ga[:], cd[f"ga{li}"], writes=[(tag, 'ga')])
            bc = [sb(st, [nk, 16, 512], BF16, "bc") for _ in range(2)]
            ysb = [sb(st, [H1, 16, 512], F32, "ysb") for _ in range(2)]
            py = [ps(st, [H1, 512], F32, "py") for _ in range(3)]
            n = 0
            it = 0
            for sq in range(2):
                dv = dsts[sq].rearrange("(a b) c -> a b c", b=128)
                for ch in range(8):
                    bb = it % 2
                    it += 1
                    S.dma('sp', bc[bb][:], srcB[sq, :, ch * 16:(ch + 1) * 16, :], writes=[(tag, 'bc', bb)])
                    for j in range(16):
                        pi = n % 3
                        n += 1
                        S.pe([lambda: P.matmul(py[pi][:], ga[:], bc[bb][:, j, :], start=True, stop=True)],
                             [(tag, 'ga'), (tag, 'bc', bb)], [(tag, 'py', pi)])
                        if j % 2 == 0:
                            S.act(lambda: A.mul(out=ysb[bb][:, j, :], in_=py[pi][:], mul=1.0 / (2 * L)),
                                  [(tag, 'py', pi)], [(tag, 'ysb', bb, j)])
                        else:
                            S.dve(lambda: V.tensor_scalar(out=ysb[bb][:, j, :], in0=py[pi][:], scalar1=1.0 / (2 * L),
                                                          scalar2=None, op0=ALU.mult),
                                  [(tag, 'py', pi)], [(tag, 'ysb', bb, j)])
                    S.dma('sp', dv[:, ch * 16:(ch + 1) * 16, :], ysb[bb][:],
                          reads=[(tag, 'ysb', bb, j) for j in range(16)])

        def phase_filter(l, li):
            L = LS[li]
            with ExitStack() as st:
                w1 = sb(st, [33, 64], F32, "w1")
                w2 = sb(st, [64, 64], F32, "w2")
                w3 = sb(st, [64, 64], F32, "w3")
                w4 = sb(st, [64, 1024], F32, "w4")
                fr = sb(st, [64, 1], F32, "fr")
                bb_ = sb(st, [64, 3], F32, "bb")
                dec = sb(st, [128, 1024], F32, "dec")
                negt = sb(st, [128, L // 128], F32, "negt")
                zt = [sb(st, [33, 512], F32, "zt") for _ in range(2)]
                hh = [sb(st, [64, 512], F32, "hh") for _ in range(3)]
                arg = sb(st, [64, 512], F32, "arg")
                kq = sb(st, [64, 512], F32, "kq")
                win_ = sb(st, [128, 1024], F32, "winw")
                fsb = [sb(st, [128, 1024], BF16, "fsb") for _ in range(2)]
                pm = [ps(st, [64, 512], F32, "pm") for _ in range(2)]
                pf = [ps(st, [128, 512], F32, "pf") for _ in range(4)]
                with nc.allow_non_contiguous_dma(reason="tiny filter params"):
                    S.dma('sp', w1[:], hy_w1[l], writes=['fw'])
                    S.dma('sp', w2[:], hy_w2[l], writes=['fw'])
                    S.dma('sp', w3[:], hy_w3[l], writes=['fw'])
                    S.dma('sp', w4[:], hy_w4[l], writes=['fw'])
                    S.dma('sp', fr[:], hy_freq[l].rearrange("(p o) -> p o", o=1), writes=['fr'])
                    S.dma('sp', bb_[:, 0:1], hy_b1[l].rearrange("(p o) -> p o", o=1), writes=['fb'])
                    S.dma('sp', bb_[:, 1:2], hy_b2[l].rearrange("(p o) -> p o", o=1), writes=['fb'])
                    S.dma('sp', bb_[:, 2:3], hy_b3[l].rearrange("(p o) -> p o", o=1), writes=['fb'])
                    S.dma('sp', dec[:], hy_decay[l].rearrange("a c -> (a c)").partition_broadcast(128), writes=['dec'])
                    S.dma('sp', negt[:], cd[f"negt{li}"], writes=['negt'])
                S.barrier()
                S.dve(lambda: V.tensor_scalar(out=bb_[:], in0=bb_[:], scalar1=fr[:, 0:1], scalar2=None, op0=ALU.mult),
                      ['fb', 'fr'], ['fb'])
                S.dve(lambda: V.scalar_tensor_tensor(out=dec[:], in0=dec[:], scalar=-1.0, in1=dec[:], op0=ALU.mult,
                                                     op1=ALU.max), ['dec'], ['dec'])
                ws = [w1, w2, w3]
                np_ = 0
                for ch in range(L // 512):
                    zb = ch % 2
                    S.dma('sp', zt[zb][:], cd[f"zt{li}"][:, ch * 512:(ch + 1) * 512], writes=[('zt', zb)])
                    cur = zt[zb]
                    ckey = ('zt', zb)
                    kdim = 33
                    for ly in range(3):
                        pi = np_ % 2
                        np_ += 1
                        S.pe([lambda: P.matmul(pm[pi][:], ws[ly][0:kdim, :], cur[0:kdim, :], start=True, stop=True)],
                             [ckey, 'fw'], [('pm', pi)])
                        S.dve(lambda: V.tensor_scalar(out=arg[:], in0=pm[pi][:], scalar1=fr[:, 0:1],
                                                      scalar2=bb_[:, ly:ly + 1], op0=ALU.mult, op1=ALU.add),
                              [('pm', pi), 'fb', 'fr'], ['arg'])
                        S.dve(lambda: V.tensor_scalar(out=kq[:], in0=arg[:], scalar1=1.0 / (2 * PI), scalar2=12582912.0,
                                                      op0=ALU.mult, op1=ALU.add), ['arg'], ['kq'])
                        S.dve(lambda: V.tensor_scalar(out=kq[:], in0=kq[:], scalar1=-12582912.0, scalar2=None,
                                                      op0=ALU.add), ['kq'], ['kq'])
                        S.dve(lambda: V.scalar_tensor_tensor(out=arg[:], in0=kq[:], scalar=-2 * PI, in1=arg[:],
                                                             op0=ALU.mult, op1=ALU.add), ['kq', 'arg'], ['arg'])
                        S.dve(lambda: V.tensor_scalar(out=arg[:], in0=arg[:], scalar1=PI, scalar2=-PI,
                                                      op0=ALU.min, op1=ALU.max), ['arg'], ['arg'])
                        S.act(lambda: A.activation(out=hh[ly][:], in_=arg[:], func=AF.Sin), ['arg'], [('hh', ly)])
                        cur = hh[ly]
                        ckey = ('hh', ly)
                        kdim = 64
                    for blk in range(4):
                        gb = ch * 4 + blk
                        fb = gb % 2
                        p0, p1 = pf[(gb % 2) * 2], pf[(gb % 2) * 2 + 1]
                        S.pe([lambda: P.matmul(p0[:], hh[2][:, blk * 128:(blk + 1) * 128], w4[:, 0:512],
                                               start=True, stop=True),
                              lambda: P.matmul(p1[:], hh[2][:, blk * 128:(blk + 1) * 128], w4[:, 512:1024],
                                               start=True, stop=True)], [('hh', 2), 'fw'], [('pf', gb % 2)])
                        S.act(lambda: A.activation(out=win_[:], in_=dec[:], func=AF.Exp, scale=negt[:, gb:gb + 1]),
                              ['dec', 'negt', 'winw'], ['winw'])
                        S.dve(lambda: V.tensor_tensor(out=fsb[fb][:, 0:512], in0=p0[:], in1=win_[:, 0:512], op=ALU.mult),
                              [('pf', gb % 2), 'winw'], [('fsb', fb, 0)])
                        S.dve(lambda: V.tensor_tensor(out=fsb[fb][:, 512:1024], in0=p1[:], in1=win_[:, 512:1024],
                                                      op=ALU.mult), [('pf', gb % 2), 'winw'], [('fsb', fb, 1)])
                        S.dma('sp', filt_d[li][0, gb * 128:(gb + 1) * 128, :], fsb[fb][:, 0:512],
                              reads=[('fsb', fb, 0)])
                        S.dma('sp', filt_d[li][1, gb * 128:(gb + 1) * 128, :], fsb[fb][:, 512:1024],
                              reads=[('fsb', fb, 1)])
            S.barrier()

        def phase_hyena(l):
            for li in range(2):
                phase_filter(l, li)
                with ExitStack() as st:
                    fft_stage_a(st, li, [filt_d[li][0], filt_d[li][1]], A_d[li], 'fa')
                S.barrier()
                with ExitStack() as st:
                    fft_stage_c(st, li, A_d[li], 'filter', None, 'fc')
                S.barrier()
                s0 = SEQS[li * 2][0]
                s1 = SEQS[li * 2 + 1][0]
                L = LS[li]
                with ExitStack() as st:
                    fft_stage_a(st, li, [u_d[s0:s0 + L, :], u_d[s1:s1 + L, :]], A_d[li], 'ua')
                S.barrier()
                with ExitStack() as st:
                    fft_stage_c(st, li, A_d[li], 'conv', B_d[li], 'uc')
                S.barrier()
                with ExitStack() as st:
                    fft_stage_ah(st, li, B_d[li], [y_d[s0:s0 + L, :], y_d[s1:s1 + L, :]], 'uh')
                S.barrier()
            with ExitStack() as st:
                TT = 512
                gH = sb(st, [128, 512], F32, "gH")
                hb = sb(st, [128, 512], F32, "hb")
                yt = [sb(st, [128, 4, 512], F32, "yt") for _ in range(2)]
                ut = [sb(st, [128, 4, 512], BF16, "ut") for _ in range(2)]
                x0t = [sb(st, [128, 4, 512], BF16, "x0t") for _ in range(2)]
                o1 = sb(st, [128, 512], F32, "o1")
                osq = sb(st, [128, 512], F32, "osq")
                ssq = sb(st, [128, 1], F32, "ssq")
                onb = sb(st, [128, 512], BF16, "onb")
                oTs = [sb(st, [128, 4, TT], BF16, "oTs") for _ in range(2)]
                ptr = ps(st, [128, 512], BF16, "ptr")
                S.dma('sp', gH[:], grp_g[l, 512:1024].partition_broadcast(128), writes=['gH'])
                S.dma('sp', hb[:], hy_bias[l].partition_broadcast(128), writes=['hb'])

                def load(g):
                    b = g % 2
                    sl = slice(g * TT, (g + 1) * TT)
                    S.dma('sp', yt[b][:], y_d[sl, :].rearrange("(b p) c -> p b c", p=128), writes=[('yt', b)])
                    S.dma('sp', ut[b][:], u_d[sl, :].rearrange("(b p) c -> p b c", p=128), writes=[('ut', b)])
                    S.dma('sp', x0t[b][:], x0_d[sl, :].rearrange("(b p) c -> p b c", p=128), writes=[('x0t', b)])

                load(0)
                for g in range(T // TT):
                    b = g % 2
                    if g + 1 < T // TT:
                        load(g + 1)
                    for blk in range(4):
                        S.pool(lambda: G.tensor_tensor(out=o1[:], in0=ut[b][:, blk, :], in1=hb[:], op=ALU.mult),
                               [('ut', b), 'hb', 'o1'], ['o1'])
                        S.dve(lambda: V.tensor_tensor(out=o1[:], in0=o1[:], in1=yt[b][:, blk, :], op=ALU.add),
                              ['o1', ('yt', b)], ['o1'])
                        S.dve(lambda: V.tensor_tensor(out=o1[:], in0=o1[:], in1=x0t[b][:, blk, :], op=ALU.mult),
                              ['o1', ('x0t', b)], ['o1'])
                        S.pool(lambda: G.tensor_tensor(out=osq[:], in0=o1[:], in1=o1[:], op=ALU.mult), ['o1'], ['osq'])
                        S.dve(lambda: V.reduce_sum(out=ssq[:], in_=osq[:], axis=AX.X), ['osq'], ['ssq'])
                        S.dve(lambda: V.tensor_scalar(out=ssq[:], in0=ssq[:], scalar1=1.0 / 512, scalar2=RMS_EPS,
                                                      op0=ALU.mult, op1=ALU.add), ['ssq'], ['ssq'])
                        S.act(lambda: A.activation(out=ssq[:], in_=ssq[:], func=AF.Ln), ['ssq'], ['ssq'])
                        S.act(lambda: A.activation(out=ssq[:], in_=ssq[:], func=AF.Exp, scale=-0.5), ['ssq'], ['ssq'])
                        S.dve(lambda: V.scalar_tensor_tensor(out=onb[:], in0=o1[:], scalar=ssq[:, 0:1], in1=gH[:],
                                                             op0=ALU.mult, op1=ALU.mult),
                              ['o1', 'ssq', 'gH', 'onb'], ['onb'])
                        S.pe([(lambda c=c: P.transpose(ptr[:, c * 128:(c + 1) * 128], onb[:, c * 128:(c + 1) * 128],
                                                       ident_b[:])) for c in range(4)], ['onb'], ['ptr'])
                        S.act(lambda: A.copy(out=oTs[b][:, :, blk * 128:(blk + 1) * 128],
                                             in_=ptr[:].rearrange("p (c t) -> p c t", c=4)), ['ptr'], [('oTs', b)])
                    S.dma('sp', oT_v[:, 4:8, g * TT:(g + 1) * TT], oTs[b][:], reads=[('oTs', b)])
            S.barrier()

        phase_tin()
        for l in range(RUN_LAYERS):
            if STOP_AFTER == ('tin', l):
                break
            phase_mod(l)
            if STOP_AFTER == ('mod', l):
                break
            phase_ffn(l, 0)
            if STOP_AFTER == ('ffn1', l):
                break
            phase_m1(l)
            phase_attn(l)
            phase_hyena(l)
            phase_wout(l)
            if STOP_AFTER == ('mix', l):
                break
            phase_ffn(l, 1)
        phase_tout()
        S.finish()
        print("sched: instr counts", S.cnt, S.dcnt, "waits", S.nwait)
        global LAST_MARKS
        LAST_MARKS = S.marks
    return nc, consts


def kernel(x_prompt, x_sample, c_prompt, c_sample, ada_w, ada_b, ffn1_wi, ffn1_wo, ffn2_wi, ffn2_wo,
           ln_g, ln_b, w_in, w_out, sink, grp_norm_g, hy_conv_w, hy_conv_b, hy_w1, hy_b1, hy_w2, hy_b2,
           hy_w3, hy_b3, hy_w4, hy_freq, hy_decay, hy_bias):
    nc, consts = build()
    f = lambda a: np.ascontiguousarray(np.asarray(a, dtype=np.float32))
    f0 = f
    if RUN_LAYERS < DEPTH:
        f = lambda a: f0(np.asarray(a)[:RUN_LAYERS])
    shared = {
        "ada_w": f(ada_w), "ada_b": f(ada_b), "ffn1_wi": f(ffn1_wi), "ffn1_wo": f(ffn1_wo),
        "ffn2_wi": f(ffn2_wi), "ffn2_wo": f(ffn2_wo), "ln_g": f(ln_g), "ln_b": f(ln_b),
        "w_in": f(w_in), "w_out": f(w_out), "sink": f(sink), "grp_norm_g": f(grp_norm_g),
        "hy_conv_w": f(hy_conv_w), "hy_conv_b": f(hy_conv_b), "hy_w1": f(hy_w1), "hy_b1": f(hy_b1),
        "hy_w2": f(hy_w2), "hy_b2": f(hy_b2), "hy_w3": f(hy_w3), "hy_b3": f(hy_b3), "hy_w4": f(hy_w4),
        "hy_freq": f(hy_freq), "hy_decay": f(hy_decay), "hy_bias": f(hy_bias),
    }
    for k, v in consts.items():
        shared["c_" + k] = v
    xp = f0(x_prompt)
    xs = f0(x_sample)
    cp = f0(c_prompt)
    cs_ = f0(c_sample)
    in_maps = []
    for c in range(NCORES):
        m = dict(shared)
        m["xin"] = np.concatenate([xp[2 * c].reshape(4096, D), xp[2 * c + 1].reshape(4096, D),
                                   xs[2 * c].reshape(2048, D), xs[2 * c + 1].reshape(2048, D)], axis=0)
        m["cin"] = np.concatenate([cp[2 * c:2 * c + 2], cs_[2 * c:2 * c + 2]], axis=0)
        in_maps.append(m)
    res = run_bass_kernel_spmd(nc, in_maps, core_ids=list(range(NCORES)))
    yp = np.empty((16, 4096, D), np.float32)
    ys = np.empty((16, 2048, D), np.float32)
    for c in range(NCORES):
        y = res.results[c]["yout"]
        yp[2 * c] = y[0:4096]
        yp[2 * c + 1] = y[4096:8192]
        ys[2 * c] = y[8192:10240]
        ys[2 * c + 1] = y[10240:12288]
    return yp, ys
```

```python
import math
from contextlib import ExitStack
import numpy as np
import ml_dtypes
import concourse.bass as bass
import concourse.mybir as mybir
from concourse.bass_utils import run_bass_kernel_spmd

F32 = mybir.dt.float32
BF16 = mybir.dt.bfloat16
AF = mybir.ActivationFunctionType
ALU = mybir.AluOpType
AX = mybir.AxisListType

D = 1024
DEPTH = 4
DFF = 2816
NIN = 2304
ALPHA = float((2 * DEPTH) ** 0.25)
LN_EPS = 1e-5 / (ALPHA * ALPHA)
RMS_EPS = 1e-6
NCORES = 8
SEQS = [(0, 4096), (4096, 4096), (8192, 2048), (10240, 2048)]
T = 12288
LS = [4096, 2048]
PI = math.pi

RUN_LAYERS = DEPTH
LAST_MARKS = None
FFN_DEBUG = 4
STOP_AFTER = None


def seq_of(tok):
    for i, (s0, ln) in enumerate(SEQS):
        if s0 <= tok < s0 + ln:
            return i, s0, ln
    raise ValueError


class Sched:
    NDMA = 8

    def __init__(self, nc, stack):
        self.nc = nc
        self.E = {'pe': nc.tensor, 'act': nc.scalar, 'dve': nc.vector, 'pool': nc.gpsimd, 'sp': nc.sync}
        self.sem = {e: stack.enter_context(nc.semaphore(f"s_{e}")) for e in ('pe', 'act', 'dve', 'pool')}
        self.cnt = {e: 0 for e in self.sem}
        self.nd = {'sp': 3, 'pool': 2}
        self.dsem = {q: [stack.enter_context(nc.semaphore(f"d_{q}{i}")) for i in range(self.nd[q])]
                     for q in ('sp', 'pool')}
        self.dcnt = {q: 0 for q in self.dsem}
        self.known = {e: {} for e in self.E}
        self.res = {}
        self.nwait = 0

    PS_NAMES = frozenset(['pt', 'pm', 'pst', 'gu', 'pf', 'psq', 'psr', 'psv', 'psz', 'ptr', 'pS', 'pO', 'pa', 'px',
                          'pb', 'py'])

    def _isp(self, k):
        if isinstance(k, tuple):
            return any(isinstance(x, str) and x in self.PS_NAMES for x in k)
        return k in self.PS_NAMES

    def _split(self, reads, writes):
        r2 = [k for k in reads if not self._isp(k)]
        w2 = list(writes) + [k for k in reads if self._isp(k)]
        return r2, w2

    def _wait(self, e, tok):
        if tok is None:
            return
        sem, val = tok
        k = self.known[e]
        sid = id(sem)
        if k.get(sid, 0) >= val:
            return
        self.E[e].wait_ge(sem, val)
        self.nwait += 1
        k[sid] = val

    def deps(self, e, reads, writes):
        for r in reads:
            st = self.res.get(r)
            if st is not None:
                self._wait(e, st[0])
        for w in writes:
            st = self.res.get(w)
            if st is not None:
                self._wait(e, st[0])
                for t in st[1]:
                    self._wait(e, t)

    def commit(self, tok, reads, writes):
        for r in reads:
            st = self.res.setdefault(r, [None, []])
            st[1].append(tok)
            if len(st[1]) > 10:
                d = {}
                for s, v in st[1]:
                    if id(s) not in d or d[id(s)][1] < v:
                        d[id(s)] = (s, v)
                st[1] = list(d.values())
        for w in writes:
            self.res[w] = [tok, []]

    def op(self, e, fn, reads=(), writes=()):
        reads, writes = self._split(reads, writes)
        self.deps(e, reads, writes)
        ins = fn()
        self.cnt[e] += 1
        ins.then_inc(self.sem[e], 1)
        tok = (self.sem[e], self.cnt[e])
        if e == 'pe':
            self.known[e][id(self.sem[e])] = self.cnt[e]
        self.commit(tok, reads, writes)
        return tok

    def pe(self, fns, reads=(), writes=()):
        reads, writes = self._split(reads, writes)
        self.deps('pe', reads, writes)
        ins = None
        for fn in fns:
            ins = fn()
        self.cnt['pe'] += 1
        ins.then_inc(self.sem['pe'], 1)
        tok = (self.sem['pe'], self.cnt['pe'])
        self.known['pe'][id(self.sem['pe'])] = self.cnt['pe']
        self.commit(tok, reads, writes)
        return tok

    def act(self, fn, r=(), w=()):
        return self.op('act', fn, r, w)

    def dve(self, fn, r=(), w=()):
        return self.op('dve', fn, r, w)

    def pool(self, fn, r=(), w=()):
        return self.op('pool', fn, r, w)

    def dma(self, q, out, in_, reads=(), writes=()):
        self.deps(q, reads, writes)
        j = self.dcnt[q]
        self.dcnt[q] += 1
        sem = self.dsem[q][j % self.nd[q]]
        rnd = j // self.nd[q]
        if rnd > 0:
            self._wait(q, (sem, 16 * rnd))
        self.E[q].dma_start(out=out, in_=in_).then_inc(sem, 16)
        tok = (sem, 16 * (rnd + 1))
        self.commit(tok, reads, writes)
        return tok

    def _all_tokens(self):
        toks = []
        for q in self.dsem:
            j = self.dcnt[q]
            for i in range(min(j, self.nd[q])):
                last = j - 1 - i
                toks.append((self.dsem[q][last % self.nd[q]], 16 * (last // self.nd[q] + 1)))
        for x in self.sem:
            if self.cnt[x] > 0:
                toks.append((self.sem[x], self.cnt[x]))
        return toks

    def barrier(self):
        import inspect
        if not hasattr(self, 'marks'):
            self.marks = []
        fr = inspect.stack()[1]
        self.marks.append((fr.function, fr.lineno, dict(self.cnt)))
        toks = self._all_tokens()
        for e in self.E:
            for t in toks:
                self._wait(e, t)
        self.res = {}

    def finish(self):
        for t in self._all_tokens():
            self._wait('sp', t)


def fft_consts(L):
    N = 2 * L
    N1 = N // 128
    H1 = N1 // 2
    nk1 = H1 + 1
    n1 = np.arange(H1)[:, None]
    k1 = np.arange(nk1)[None, :]
    ang = 2 * np.pi * n1 * k1 / N1
    FA = np.concatenate([np.cos(ang), -np.sin(ang)], axis=1)
    w = np.full(nk1, 2.0)
    w[0] = 1
    w[-1] = 1
    GA = np.concatenate([(w[None, :] * np.cos(ang)).T, (-w[None, :] * np.sin(ang)).T], axis=0)
    n2 = np.arange(128)[:, None]
    k2 = np.arange(128)[None, :]
    Ms = []
    for kk in range(nk1):
        th = 2 * np.pi * ((n2 * (kk + N1 * k2)) % N) / N
        Mre = np.cos(th)
        Mim = -np.sin(th)
        Ms.append(np.stack([Mre, Mim, -Mim, Mre.T, Mim.T, -Mim.T], axis=1))
    M = np.stack(Ms, axis=0)
    return FA, GA, M


def host_consts():
    bf = ml_dtypes.bfloat16
    c = {}
    c["ident_f"] = np.eye(128, dtype=np.float32)
    c["ident_b"] = np.eye(128, dtype=np.float32).astype(bf)
    c["ones_b"] = np.full((128, 128), 1.0 / D, dtype=np.float32).astype(bf)
    rt = np.zeros((64, 16), np.float32)
    for i in range(8):
        rt[i + 8, i] = -1.0
        rt[i, i + 8] = 1.0
    c["rt"] = rt.astype(bf)
    inv = 500000.0 ** (-np.arange(0, 16, 2, dtype=np.float64) / 16)
    ang = np.arange(4096, dtype=np.float64)[None, :] * np.concatenate([inv, inv])[:, None]
    c["cos_t"] = np.cos(ang).astype(np.float32)
    c["sin_t"] = np.sin(ang).astype(np.float32)
    s = np.arange(128)[:, None]
    q = np.arange(128)[None, :]
    c["maskp"] = (s >= q).astype(np.float32).astype(bf)
    c["maskn"] = (s <= q).astype(np.float32).astype(bf)
    for i, L in enumerate(LS):
        FA, GA, M = fft_consts(L)
        c[f"fa{i}"] = FA.astype(np.float32).astype(bf)
        c[f"ga{i}"] = GA.astype(np.float32).astype(bf)
        c[f"mm{i}"] = M.astype(np.float32).astype(bf)
        t = np.linspace(0.0, 1.0, L, dtype=np.float32).astype(np.float64)
        wv = 2.0 * np.pi * np.arange(L, dtype=np.float64) / L
        fb = np.linspace(1e-4, 15, 16, dtype=np.float32).astype(np.float64)[None]
        z = np.concatenate([t[:, None], np.cos(fb * wv[:, None]), -np.sin(fb * wv[:, None])], axis=-1)
        c[f"zt{i}"] = np.ascontiguousarray(z.T).astype(np.float32)
        c[f"negt{i}"] = np.ascontiguousarray((-t).reshape(L // 128, 128).T).astype(np.float32)
    return c


def build():
    nc = bass.Bass("TRN2", target_bir_lowering=False)
    DEPTH = RUN_LAYERS
    consts = host_consts()

    def din(name, shape, dt=F32):
        return nc.dram_tensor(name, list(shape), dt, kind="ExternalInput").ap()

    def dscr(name, shape, dt=F32):
        return nc.dram_tensor(name, list(shape), dt).ap()

    xin = din("xin", [T, D])
    cin = din("cin", [4, D])
    ada_w = din("ada_w", [DEPTH, D, 9 * D])
    ada_b = din("ada_b", [DEPTH, 9 * D])
    ffn_wi = [din("ffn1_wi", [DEPTH, D, 2 * DFF]), din("ffn2_wi", [DEPTH, D, 2 * DFF])]
    ffn_wo = [din("ffn1_wo", [DEPTH, DFF, D]), din("ffn2_wo", [DEPTH, DFF, D])]
    ln_g = din("ln_g", [DEPTH, 3, D])
    ln_b = din("ln_b", [DEPTH, 3, D])
    w_in = din("w_in", [DEPTH, D, NIN])
    w_out = din("w_out", [DEPTH, D, D])
    sink = din("sink", [DEPTH, 8])
    grp_g = din("grp_norm_g", [DEPTH, D])
    hy_cw = din("hy_conv_w", [DEPTH, 3, 1536])
    hy_cb = din("hy_conv_b", [DEPTH, 1536])
    hy_w1 = din("hy_w1", [DEPTH, 33, 64])
    hy_b1 = din("hy_b1", [DEPTH, 64])
    hy_w2 = din("hy_w2", [DEPTH, 64, 64])
    hy_b2 = din("hy_b2", [DEPTH, 64])
    hy_w3 = din("hy_w3", [DEPTH, 64, 64])
    hy_b3 = din("hy_b3", [DEPTH, 64])
    hy_w4 = din("hy_w4", [DEPTH, 64, 1024])
    hy_freq = din("hy_freq", [DEPTH, 64])
    hy_decay = din("hy_decay", [DEPTH, 2, 512])
    hy_bias = din("hy_bias", [DEPTH, 512])
    cd = {}
    for k, v in consts.items():
        cd[k] = din("c_" + k, v.shape, BF16 if v.dtype == ml_dtypes.bfloat16 else F32)
    yout = nc.dram_tensor("yout", [T, D], F32, kind="ExternalOutput").ap()

    xT_d = dscr("xT_d", [D, T])
    q_d = dscr("q_d", [10, 64, T], BF16)
    v_d = dscr("v_d", [T, 130], BF16)
    u_d = dscr("u_d", [T, 512], BF16)
    x0_d = dscr("x0_d", [T, 512], BF16)
    y_d = dscr("y_d", [T, 512], F32)
    oT_d = dscr("oT_d", [D, T], BF16)
    filt_d = [dscr(f"filt_d{i}", [2, L, 512], BF16) for i, L in enumerate(LS)]
    NK1 = [L // 128 + 1 for L in LS]
    A_d = [dscr(f"A_d{i}", [2, 2 * NK1[i], 128, 512], BF16) for i in range(2)]
    B_d = [dscr(f"B_d{i}", [2, 2 * NK1[i], 128, 512], BF16) for i in range(2)]
    Kf_d = [dscr(f"Kf_d{i}", [NK1[i], 128, 2, 512], F32) for i in range(2)]

    xT_v = xT_d.rearrange("(k p) t -> p k t", p=128)
    oT_v = oT_d.rearrange("(k p) t -> p k t", p=128)

    top = ExitStack()
    with top:
        S = Sched(nc, top)
        uid = [0]

        def sb(st, shape, dt, name=None):
            uid[0] += 1
            return st.enter_context(nc.sbuf_tensor(f"{name or 't'}_{uid[0]}", list(shape), dt))

        def ps(st, shape, dt=F32, name=None):
            uid[0] += 1
            return st.enter_context(nc.psum_tensor(f"{name or 'p'}_{uid[0]}", list(shape), dt))

        V, G, A, P = nc.vector, nc.gpsimd, nc.scalar, nc.tensor

        ident_f = sb(top, [128, 128], F32, "identf")
        ident_b = sb(top, [128, 128], BF16, "identb")
        ones_b = sb(top, [128, 128], BF16, "onesb")
        rt_sb = sb(top, [64, 16], BF16, "rt")
        maskp = sb(top, [128, 128], BF16, "maskp")
        maskn = sb(top, [128, 128], BF16, "maskn")
        scT = sb(top, [128, 8, 4], F32, "scT")
        scT_b = sb(top, [128, 8, 4], BF16, "scTb")
        modsb = sb(top, [128, 72, 4], F32, "modsb")
        adab = sb(top, [128, DEPTH, 72], F32, "adab")
        lng = sb(top, [128, DEPTH * 3, 8], F32, "lng")
        nlng = sb(top, [128, DEPTH * 3, 8], F32, "nlng")
        lnb = sb(top, [128, DEPTH * 3, 8], F32, "lnb")
        cw = sb(top, [128, DEPTH * 3, 12], F32, "cw")
        cbs = sb(top, [128, DEPTH, 12], F32, "cbs")
        nsink = sb(top, [128, DEPTH * 8], F32, "nsink")
        eps_ln = sb(top, [128, 1], F32, "epsln")
        S.dve(lambda: V.memset(eps_ln[:], LN_EPS), [], ['epsln'])

        with nc.allow_non_contiguous_dma(reason="tiny one-time parameter loads"):
            S.dma('sp', ident_f[:], cd["ident_f"], writes=['c0'])
            S.dma('sp', ident_b[:], cd["ident_b"], writes=['c1'])
            S.dma('sp', ones_b[:], cd["ones_b"], writes=['c2'])
            S.dma('sp', rt_sb[:], cd["rt"], writes=['c3'])
            S.dma('sp', maskp[:], cd["maskp"], writes=['c4'])
            S.dma('sp', maskn[:], cd["maskn"], writes=['c5'])
            for s_ in range(4):
                S.dma('sp', scT[:, :, s_], cin[s_].rearrange("(k p) -> p k", p=128), writes=[('scT', s_)])
            for l_ in range(DEPTH):
                S.dma('sp', adab[:, l_, :], ada_b[l_].rearrange("(j p) -> p j", p=128), writes=[('adab', l_)])
                S.dma('sp', cbs[:, l_, :], hy_cb[l_].rearrange("(c p) -> p c", p=128), writes=[('cbs', l_)])
                for m_ in range(3):
                    S.dma('sp', lng[:, l_ * 3 + m_, :], ln_g[l_, m_].rearrange("(k p) -> p k", p=128),
                          writes=[('lng', l_, m_)])
                    S.dma('sp', lnb[:, l_ * 3 + m_, :], ln_b[l_, m_].rearrange("(k p) -> p k", p=128),
                          writes=[('lnb', l_, m_)])
                    S.dma('sp', cw[:, l_ * 3 + m_, :], hy_cw[l_, m_].rearrange("(c p) -> p c", p=128),
                          writes=[('cw', l_, m_)])
            S.dma('sp', nsink[:], sink.rearrange("l h -> (l h)").partition_broadcast(128), writes=['nsink'])
        S.barrier()
        S.act(lambda: A.activation(out=scT_b[:], in_=scT[:], func=AF.Silu), ['scT'], ['scTb'])
        S.dve(lambda: V.tensor_scalar(out=nlng[:], in0=lng[:], scalar1=-1.0, scalar2=None, op0=ALU.mult),
              ['lng'], ['nlng'])
        S.dve(lambda: V.tensor_scalar(out=nsink[:], in0=nsink[:], scalar1=-1.0, scalar2=None, op0=ALU.mult),
              ['nsink'], ['nsink'])
        S.barrier()

        def phase_tin():
            with ExitStack() as st:
                xtok = [sb(st, [128, 4, D], F32, "xtok") for _ in range(2)]
                xo = [sb(st, [128, 8, 512], F32, "xo") for _ in range(2)]
                pt = [ps(st, [128, 512], F32, "pt") for _ in range(3)]
                n = 0
                for g in range(T // 512):
                    b = g % 2
                    S.dma('sp', xtok[b][:], xin[g * 512:(g + 1) * 512, :].rearrange("(j p) d -> p j d", p=128),
                          writes=[('xtok', b)])
                    for k in range(8):
                        pb = n % 3
                        n += 1
                        S.pe([(lambda j=j: P.transpose(pt[pb][:, j * 128:(j + 1) * 128],
                                                        xtok[b][:, j, k * 128:(k + 1) * 128], ident_f[:]))
                              for j in range(4)], [('xtok', b)], [('pt', pb)])
                        if k % 2 == 0:
                            S.act(lambda: A.copy(out=xo[b][:, k, :], in_=pt[pb][:]), [('pt', pb)], [('xo', b, k)])
                        else:
                            S.dve(lambda: V.tensor_copy(out=xo[b][:, k, :], in_=pt[pb][:]), [('pt', pb)], [('xo', b, k)])
                    S.dma('sp', xT_v[:, :, g * 512:(g + 1) * 512], xo[b][:],
                          reads=[('xo', b, k) for k in range(8)])
            S.barrier()

        def phase_tout():
            with ExitStack() as st:
                xo = [sb(st, [128, 8, 512], F32, "xo") for _ in range(2)]
                ytok = [sb(st, [128, 4, D], F32, "ytok") for _ in range(2)]
                pt = [ps(st, [128, 512], F32, "pt") for _ in range(4)]
                n = 0
                for g in range(T // 512):
                    b = g % 2
                    S.dma('sp', xo[b][:], xT_v[:, :, g * 512:(g + 1) * 512], writes=[('xo', b)])
                    for j in range(4):
                        for hf in range(2):
                            pb = n % 4
                            n += 1
                            S.pe([(lambda kk=kk: P.transpose(pt[pb][:, kk * 128:(kk + 1) * 128],
                                                              xo[b][:, hf * 4 + kk, j * 128:(j + 1) * 128], ident_f[:]))
                                  for kk in range(4)], [('xo', b)], [('pt', pb)])
                            if hf == 0:
                                S.act(lambda: A.copy(out=ytok[b][:, j, 0:512], in_=pt[pb][:]),
                                      [('pt', pb)], [('ytok', b, j, 0)])
                            else:
                                S.dve(lambda: V.tensor_copy(out=ytok[b][:, j, 512:1024], in_=pt[pb][:]),
                                      [('pt', pb)], [('ytok', b, j, 1)])
                    S.dma('sp', yout[g * 512:(g + 1) * 512, :].rearrange("(j p) d -> p j d", p=128), ytok[b][:],
                          reads=[('ytok', b, j, h) for j in range(4) for h in range(2)])
            S.barrier()

        def phase_mod(l):
            with ExitStack() as st:
                wt = [sb(st, [128, 8, 1152], BF16, "adaw") for _ in range(2)]
                pm = ps(st, [128, 72, 4], F32, "pm")
                src = ada_w[l].rearrange("(k p) f -> p k f", p=128)
                for cg in range(8):
                    b = cg % 2
                    S.dma('pool', wt[b][:], src[:, :, cg * 1152:(cg + 1) * 1152], writes=[('adaw', b)])
                    for jj in range(9):
                        j = cg * 9 + jj
                        S.pe([(lambda k=k: P.matmul(pm[:, j, :], wt[b][:, k, jj * 128:(jj + 1) * 128], scT_b[:, k, :],
                                                    start=(k == 0), stop=(k == 7))) for k in range(8)],
                             [('adaw', b), 'scTb'], ['pm'])
                allpm = ['pm']
                S.dve(lambda: V.tensor_tensor(out=modsb[:], in0=pm[:],
                                              in1=adab[:, l, :].unsqueeze(2).to_broadcast([128, 72, 4]), op=ALU.add),
                      allpm + ['modsb'], ['modsb'])
                for m, cf in ((1, None), (4, None), (7, None), (2, 0.5 / ALPHA), (5, 1.0 / ALPHA), (8, 0.5 / ALPHA)):
                    sl = modsb[:, m * 8:(m + 1) * 8, :]
                    if cf is None:
                        S.dve(lambda: V.tensor_scalar(out=sl, in0=sl, scalar1=1.0, scalar2=None, op0=ALU.add),
                              ['modsb'], ['modsb'])
                    else:
                        S.dve(lambda: V.tensor_scalar(out=sl, in0=sl, scalar1=1.0, scalar2=cf, op0=ALU.add,
                                                      op1=ALU.mult), ['modsb'], ['modsb'])
            S.barrier()

        def ln_tail(st_objs, l, m, b, xt, rb, rq, pst, tmps, TT, rbk='rb', rqk='rq'):
            mean_sb, var, mr = tmps[0:3]
            S.pe([(lambda k=k: P.matmul(pst[:, 0:TT], ones_b[:], rb[:, k, :], start=(k == 0), stop=(k == 7)))
                  for k in range(8)] +
                 [(lambda k=k: P.matmul(pst[:, TT:2 * TT], ones_b[:], rq[:, k, :], start=(k == 0), stop=(k == 7)))
                  for k in range(8)], [rbk, rqk], ['pst'])
            S.dve(lambda: V.tensor_copy(out=mean_sb[:], in_=pst[:, 0:TT]), ['pst'], ['mean'])
            S.dve(lambda: V.tensor_tensor(out=var[:], in0=mean_sb[:], in1=mean_sb[:], op=ALU.mult), ['mean'], ['var'])
            S.dve(lambda: V.tensor_tensor(out=var[:], in0=pst[:, TT:2 * TT], in1=var[:], op=ALU.subtract),
                  ['pst', 'var'], ['var'])
            S.act(lambda: A.activation(out=var[:], in_=var[:], func=AF.Sqrt, bias=eps_ln[:, 0:1], scale=1.0),
                  ['var'], ['var'])
            S.dve(lambda: V.reciprocal(out=var[:], in_=var[:]), ['var'], ['var'])
            S.dve(lambda: V.tensor_tensor(out=mr[:], in0=mean_sb[:], in1=var[:], op=ALU.mult), ['mean', 'var'], ['mr'])
            li = l * 3 + m
            for dk in range(8):
                x = xt[b][:, dk, :]
                S.dve(lambda: V.scalar_tensor_tensor(out=x, in0=x, scalar=lng[:, li, dk:dk + 1], in1=var[:],
                                                     op0=ALU.mult, op1=ALU.mult), [('xt', b, dk), 'var'], [('xt', b, dk)])
                tq = tmps[3 + dk % 2]
                S.pool(lambda: G.tensor_scalar(out=tq[:], in0=mr[:], scalar1=nlng[:, li, dk:dk + 1],
                                               scalar2=lnb[:, li, dk:dk + 1], op0=ALU.mult, op1=ALU.add),
                       ['mr'], [('tq', dk % 2)])
                S.pool(lambda: G.tensor_tensor(out=x, in0=x, in1=tq[:], op=ALU.add), [('xt', b, dk), ('tq', dk % 2)],
                       [('xt', b, dk)])

        def phase_ffn(l, which):
            TT = 256
            NT = T // TT
            jS, jB, jG = (1, 0, 2) if which == 0 else (7, 6, 8)
            lnm = 0 if which == 0 else 2
            with ExitStack() as st:
                wi = sb(st, [128, 8, 2 * DFF], BF16, "wi")
                wo = sb(st, [128, 22, D], BF16, "wo")
                xt = [sb(st, [128, 8, TT], F32, "xt") for _ in range(3)]
                hT = [sb(st, [128, 8, TT], BF16, "hT") for _ in range(2)]
                aT = sb(st, [128, 22, TT], BF16, "aT")
                rb = sb(st, [128, 8, TT], BF16, "rb")
                rq = sb(st, [128, 8, TT], BF16, "rq")
                sg = [sb(st, [128, TT], BF16, "sg") for _ in range(2)]
                tmps = [sb(st, [128, TT], F32, "lnt") for _ in range(5)]
                gu = [ps(st, [128, 512], F32, "gu") for _ in range(3)]
                pf = [ps(st, [128, 512], F32, "pf") for _ in range(4)]
                pst = ps(st, [128, 512], F32, "pst")
                wsrc = ffn_wi[which][l].rearrange("(k p) f -> p k f", p=128)
                osrc = ffn_wo[which][l].rearrange("(c p) d -> p c d", p=128)
                for cg in range(11):
                    S.dma('pool', wi[:, :, cg * 256:(cg + 1) * 256], wsrc[:, :, cg * 256:(cg + 1) * 256],
                          writes=[('wig', cg)])
                    S.dma('pool', wi[:, :, DFF + cg * 256:DFF + (cg + 1) * 256],
                          wsrc[:, :, DFF + cg * 256:DFF + (cg + 1) * 256], writes=[('wiu', cg)])
                for cg in range(11):
                    S.dma('pool', wo[:, cg * 2:cg * 2 + 2, :], osrc[:, cg * 2:cg * 2 + 2, :], writes=[('wo', cg)])

                def load(t):
                    S.dma('sp', xt[t % 3][:], xT_v[:, :, t * TT:(t + 1) * TT], writes=[('xt', t % 3, k_) for k_ in range(8)])

                def modulate(t):
                    b = t % 2
                    xb = t % 3
                    s = seq_of(t * TT)[0]
                    for k in range(8):
                        S.dve(lambda: V.tensor_scalar(out=hT[b][:, k, :], in0=xt[xb][:, k, :],
                                                      scalar1=modsb[:, jS * 8 + k, s:s + 1],
                                                      scalar2=modsb[:, jB * 8 + k, s:s + 1],
                                                      op0=ALU.mult, op1=ALU.add), [('xt', xb, k)], [('hT', b)])

                def gu_part(t):
                    b = t % 2
                    for c in range(22):
                        g = gu[c % 3]
                        S.pe([(lambda k=k: P.matmul(g[:, 0:TT], wi[:, k, c * 128:(c + 1) * 128], hT[b][:, k, :],
                                                    start=(k == 0), stop=(k == 7))) for k in range(8)] +
                             [(lambda k=k: P.matmul(g[:, TT:2 * TT], wi[:, k, DFF + c * 128:DFF + (c + 1) * 128],
                                                    hT[b][:, k, :], start=(k == 0), stop=(k == 7))) for k in range(8)],
                             [('hT', b), ('wig', c // 2), ('wiu', c // 2)], [('gu', c % 3)])
                        S.act(lambda: A.activation(out=sg[c % 2][:], in_=g[:, 0:TT], func=AF.Silu),
                              [('gu', c % 3)], [('sg', c % 2)])
                        S.dve(lambda: V.tensor_tensor(out=aT[:, c, :], in0=g[:, TT:2 * TT], in1=sg[c % 2][:],
                                                      op=ALU.mult), [('gu', c % 3), ('sg', c % 2)], [('aT', c)])

                def down_part(t):
                    b = t % 3
                    s = seq_of(t * TT)[0]
                    for dp in range(4):
                        bank = pf[dp]
                        S.pe([(lambda c=c, h_=h_: P.matmul(bank[:, h_ * TT:(h_ + 1) * TT],
                                                           wo[:, c, (2 * dp + h_) * 128:(2 * dp + h_ + 1) * 128],
                                                           aT[:, c, :], start=(c == 0), stop=(c == 21)))
                              for h_ in range(2) for c in range(22)],
                             [('aT', c) for c in range(22)] + [('wo', cg) for cg in range(11)], [('pf', dp)])
                        for h_ in range(2):
                            dk = 2 * dp + h_
                            pp = bank[:, h_ * TT:(h_ + 1) * TT]
                            x = xt[b][:, dk, :]
                            S.dve(lambda: V.scalar_tensor_tensor(out=x, in0=pp, scalar=modsb[:, jG * 8 + dk, s:s + 1],
                                                                 in1=x, op0=ALU.mult, op1=ALU.add),
                                  [('pf', dp), ('xt', b, dk)], [('xt', b, dk)])
                            S.pool(lambda: G.tensor_copy(out=rb[:, dk, :], in_=x), [('xt', b, dk)], ['rb'])
                            S.pool(lambda: G.tensor_tensor(out=rq[:, dk, :], in0=x, in1=x, op=ALU.mult),
                                   [('xt', b, dk)], ['rq'])

                def tail_part(t):
                    b = t % 3
                    ln_tail(None, l, lnm, b, xt, rb, rq, pst, tmps, TT)
                    S.dma('sp', xT_v[:, :, t * TT:(t + 1) * TT], xt[b][:], reads=[('xt', b, k_) for k_ in range(8)])

                for t in range(min(3, NT)):
                    load(t)
                modulate(0)
                gu_part(0)
                if NT > 1:
                    modulate(1)
                down_part(0)
                for t in range(NT):
                    if t + 1 < NT:
                        gu_part(t + 1)
                    if t + 2 < NT:
                        modulate(t + 2)
                    tail_part(t)
                    if t + 3 < NT:
                        load(t + 3)
                    if t + 1 < NT:
                        down_part(t + 1)
            S.barrier()

        def phase_wout(l):
            TT = 256
            NT = T // TT
            with ExitStack() as st:
                wo = sb(st, [128, 8, D], BF16, "wout")
                xt = [sb(st, [128, 8, TT], F32, "xt") for _ in range(3)]
                oT = [sb(st, [128, 8, TT], BF16, "oT") for _ in range(3)]
                rb = [sb(st, [128, 8, TT], BF16, "rb") for _ in range(2)]
                rq = [sb(st, [128, 8, TT], BF16, "rq") for _ in range(2)]
                tmps = [sb(st, [128, TT], F32, "lnt") for _ in range(5)]
                pf = [ps(st, [128, 512], F32, "pf") for _ in range(4)]
                pst = ps(st, [128, 512], F32, "pst")
                S.dma('pool', wo[:], w_out[l].rearrange("(k p) d -> p k d", p=128), writes=['wout'])

                def load(t):
                    S.dma('sp', xt[t % 3][:], xT_v[:, :, t * TT:(t + 1) * TT], writes=[('xt', t % 3, k_) for k_ in range(8)])
                    S.dma('sp', oT[t % 3][:], oT_v[:, :, t * TT:(t + 1) * TT], writes=[('oT', t % 3)])

                def mm(t):
                    b = t % 3
                    b2 = t % 2
                    s = seq_of(t * TT)[0]
                    for dp in range(4):
                        bank = pf[dp]
                        S.pe([(lambda k=k, h_=h_: P.matmul(bank[:, h_ * TT:(h_ + 1) * TT],
                                                           wo[:, k, (2 * dp + h_) * 128:(2 * dp + h_ + 1) * 128],
                                                           oT[b][:, k, :], start=(k == 0), stop=(k == 7)))
                              for h_ in range(2) for k in range(8)], [('oT', b), 'wout'], [('pf', dp)])
                        for h_ in range(2):
                            dk = 2 * dp + h_
                            pp = bank[:, h_ * TT:(h_ + 1) * TT]
                            x = xt[b][:, dk, :]
                            S.dve(lambda: V.scalar_tensor_tensor(out=x, in0=pp, scalar=modsb[:, 5 * 8 + dk, s:s + 1],
                                                                 in1=x, op0=ALU.mult, op1=ALU.add),
                                  [('pf', dp), ('xt', b, dk)], [('xt', b, dk)])
                            S.pool(lambda: G.tensor_copy(out=rb[b2][:, dk, :], in_=x), [('xt', b, dk)], [('rb', b2)])
                            S.pool(lambda: G.tensor_tensor(out=rq[b2][:, dk, :], in0=x, in1=x, op=ALU.mult),
                                   [('xt', b, dk)], [('rq', b2)])

                def tail(t):
                    b = t % 3
                    b2 = t % 2
                    ln_tail(None, l, 1, b, xt, rb[b2], rq[b2], pst, tmps, TT, rbk=('rb', b2), rqk=('rq', b2))
                    S.dma('sp', xT_v[:, :, t * TT:(t + 1) * TT], xt[b][:], reads=[('xt', b, k_) for k_ in range(8)])

                for t in range(min(3, NT)):
                    load(t)
                mm(0)
                for t in range(NT):
                    if t + 1 < NT:
                        mm(t + 1)
                    tail(t)
                    if t + 3 < NT:
                        load(t + 3)
            S.barrier()

        def phase_m1(l):
            TT = 512
            NT = T // TT
            with ExitStack() as st:
                win = sb(st, [128, 8, NIN], BF16, "win")
                xt = [sb(st, [128, 8, TT + 2], F32, "xt") for _ in range(2)]
                hT = [sb(st, [128, 8, TT + 2], BF16, "hT") for _ in range(2)]
                qb = [sb(st, [64, TT], BF16, "qb") for _ in range(2)]
                t1 = sb(st, [16, TT], F32, "t1")
                t2 = sb(st, [16, TT], F32, "t2")
                cs = [sb(st, [16, 2, TT], F32, "cs") for _ in range(2)]
                vx = [sb(st, [128, 4, 2, 65], BF16, "vx") for _ in range(2)]
                zs = [sb(st, [128, TT + 2], F32, "zs") for _ in range(2)]
                zc = [sb(st, [128, TT], F32, "zc") for _ in range(2)]
                x0b = sb(st, [128, 4, TT], BF16, "x0b")
                ub = sb(st, [128, 4, TT], BF16, "ub")
                tok0b = [sb(st, [128, 4, 512], BF16, "tok0") for _ in range(2)]
                psq = [ps(st, [64, 512], F32, "psq")] * 2
                psr = ps(st, [16, 512], F32, "psr")
                psv = ps(st, [128, 512], F32, "psv")
                psz = [ps(st, [128, 512], F32, "psz") for _ in range(2)]
                pszh = [ps(st, [128, 16], F32, "pszh") for _ in range(2)]
                ptr = ps(st, [128, 512], BF16, "ptr")
                S.dma('pool', win[:, :, 0:768], w_in[l].rearrange("(k p) f -> p k f", p=128)[:, :, 0:768],
                      writes=['win0'])
                for i in range(3):
                    S.dma('pool', win[:, :, 768 + i * 512:768 + (i + 1) * 512],
                          w_in[l].rearrange("(k p) f -> p k f", p=128)[:, :, 768 + i * 512:768 + (i + 1) * 512],
                          writes=[('win', i)])
                for b in range(2):
                    S.pool(lambda: G.memset(vx[b][:], 1.0), [], [('vx', b)])
                wkeys = ['win0'] + [('win', i) for i in range(3)]

                def load(g):
                    b = g % 2
                    tok0 = g * TT
                    s, s0, ln = seq_of(tok0)
                    lo = 1 if tok0 == s0 else 0
                    hi = 1 if tok0 + TT == s0 + ln else 0
                    S.dma('sp', xt[b][:, :, lo:TT + 2 - hi], xT_v[:, :, tok0 - 1 + lo:tok0 + TT + 1 - hi],
                          writes=[('xt', b)])
                    S.dma('sp', cs[b][:, 0, :], cd["cos_t"][:, tok0 - s0:tok0 - s0 + TT], writes=[('cs', b, 0)])
                    S.dma('sp', cs[b][:, 1, :], cd["sin_t"][:, tok0 - s0:tok0 - s0 + TT], writes=[('cs', b, 1)])

                def modulate(g):
                    b = g % 2
                    tok0 = g * TT
                    s, s0, ln = seq_of(tok0)
                    lo = 1 if tok0 == s0 else 0
                    hi = 1 if tok0 + TT == s0 + ln else 0
                    for k in range(8):
                        S.dve(lambda: V.tensor_scalar(out=hT[b][:, k, lo:TT + 2 - hi], in0=xt[b][:, k, lo:TT + 2 - hi],
                                                      scalar1=modsb[:, 4 * 8 + k, s:s + 1],
                                                      scalar2=modsb[:, 3 * 8 + k, s:s + 1],
                                                      op0=ALU.mult, op1=ALU.add), [('xt', b)], [('hT', b)])
                    if lo:
                        S.dve(lambda: V.memset(hT[b][:, :, 0:1], 0.0), [], [('hT', b)])
                    if hi:
                        S.dve(lambda: V.memset(hT[b][:, :, TT + 1:TT + 2], 0.0), [], [('hT', b)])

                load(0)
                modulate(0)
                nq = 0
                nz = 0
                for g in range(NT):
                    b = g % 2
                    tok0 = g * TT
                    if g + 1 < NT:
                        load(g + 1)
                    for hd in range(10):
                        qi = nq % 2
                        nq += 1
                        S.pe([(lambda k=k: P.matmul(psq[qi][:], win[:, k, hd * 64:(hd + 1) * 64], hT[b][:, k, 1:TT + 1],
                                                    start=(k == 0), stop=(k == 7))) for k in range(8)],
                             [('hT', b), 'win0'], ['psq'])
                        S.act(lambda: A.copy(out=qb[qi][:], in_=psq[qi][:]), ['psq'], [('qb', qi)])
                        S.pe([lambda: P.matmul(psr[:], rt_sb[:], qb[qi][:], start=True, stop=True)],
                             [('qb', qi)], ['psr'])
                        S.dve(lambda: V.tensor_tensor(out=t1[:], in0=psq[qi][0:16, :], in1=cs[b][:, 0, :], op=ALU.mult),
                              ['psq', ('cs', b, 0)], ['t1'])
                        S.dve(lambda: V.tensor_tensor(out=t2[:], in0=psr[:], in1=cs[b][:, 1, :], op=ALU.mult),
                              ['psr', ('cs', b, 1)], ['t2'])
                        S.pool(lambda: G.tensor_tensor(out=qb[qi][0:16, :], in0=t1[:], in1=t2[:], op=ALU.add),
                               ['t1', 't2', ('qb', qi)], [('qb', qi)])
                        S.dma('sp', q_d[hd, :, tok0:tok0 + TT], qb[qi][:], reads=[('qb', qi)])
                    for blk in range(4):
                        S.pe([(lambda k=k: P.matmul(psv[:, blk * 128:(blk + 1) * 128],
                                                    hT[b][:, k, 1 + blk * 128:1 + (blk + 1) * 128],
                                                    win[:, k, 640:768], start=(k == 0), stop=(k == 7)))
                              for k in range(8)], [('hT', b), 'win0'], ['psv'])
                    S.act(lambda: A.copy(out=vx[b][:, :, :, 0:64],
                                         in_=psv[:].rearrange("p (b k e) -> p b k e", b=4, k=2)),
                          ['psv', ('vx', b)], [('vx', b)])
                    S.dma('sp', v_d[tok0:tok0 + TT, :].rearrange("(b p) e -> p b e", p=128),
                          vx[b][:].rearrange("p b k e -> p b (k e)"), reads=[('vx', b)])
                    for i in range(4):
                        for part in range(3):
                            cz = part * 4 + i
                            zi = nz % 2
                            nz += 1
                            c0 = 768 + cz * 128
                            S.pe([(lambda k=k: P.matmul(psz[zi][:], win[:, k, c0:c0 + 128], hT[b][:, k, 1:TT + 1],
                                                        start=(k == 0), stop=(k == 7))) for k in range(8)] +
                                 [(lambda k=k: P.matmul(pszh[zi][:, 0:2], win[:, k, c0:c0 + 128],
                                                        hT[b][:, k, 0:TT + 2:TT + 1],
                                                        start=(k == 0), stop=(k == 7))) for k in range(8)],
                                 [('hT', b)] + wkeys, [('psz', zi)])
                            S.act(lambda: A.copy(out=zs[zi][:, 1:TT + 1], in_=psz[zi][:]), [('psz', zi)], [('zs', zi)])
                            S.act(lambda: A.copy(out=zs[zi][:, 0:TT + 2:TT + 1], in_=pszh[zi][:, 0:2]),
                                  [('psz', zi)], [('zs', zi)])
                            zz = zc[part % 2] if part > 0 else zc[0]
                            zkey = ('zc', part % 2 if part > 0 else 0)
                            S.pool(lambda: G.tensor_scalar(out=zz[:], in0=zs[zi][:, 1:TT + 1],
                                                           scalar1=cw[:, l * 3 + 1, cz:cz + 1],
                                                           scalar2=cbs[:, l, cz:cz + 1], op0=ALU.mult, op1=ALU.add),
                                   [('zs', zi)], [zkey])
                            S.dve(lambda: V.scalar_tensor_tensor(out=zz[:], in0=zs[zi][:, 0:TT],
                                                                  scalar=cw[:, l * 3 + 0, cz:cz + 1], in1=zz[:],
                                                                  op0=ALU.mult, op1=ALU.add), [('zs', zi), zkey], [zkey])
                            if part == 0:
                                S.dve(lambda: V.scalar_tensor_tensor(out=x0b[:, i, :], in0=zs[zi][:, 2:TT + 2],
                                                                     scalar=cw[:, l * 3 + 2, cz:cz + 1], in1=zz[:],
                                                                     op0=ALU.mult, op1=ALU.add),
                                      [('zs', zi), zkey], [('x0b', i)])
                            else:
                                S.dve(lambda: V.scalar_tensor_tensor(out=zz[:], in0=zs[zi][:, 2:TT + 2],
                                                                     scalar=cw[:, l * 3 + 2, cz:cz + 1], in1=zz[:],
                                                                     op0=ALU.mult, op1=ALU.add),
                                      [('zs', zi), zkey], [zkey])
                        S.pool(lambda: G.tensor_tensor(out=ub[:, i, :], in0=zc[0][:], in1=zc[1][:], op=ALU.mult),
                               [('zc', 0), ('zc', 1)], [('ub', i)])
                    for ti, (srcb, dst, key) in enumerate(((x0b, x0_d, 'x0b'), (ub, u_d, 'ub'))):
                        tb = tok0b[ti]
                        for blk in range(4):
                            S.pe([(lambda i=i: P.transpose(ptr[:, i * 128:(i + 1) * 128],
                                                           srcb[:, i, blk * 128:(blk + 1) * 128], ident_b[:]))
                                  for i in range(4)], [(key, i) for i in range(4)], ['ptr'])
                            S.act(lambda: A.copy(out=tb[:, blk, :], in_=ptr[:]), ['ptr'], [('tokb', ti)])
                        S.dma('sp', dst[tok0:tok0 + TT, :].rearrange("(b p) c -> p b c", p=128), tb[:],
                              reads=[('tokb', ti)])
                    if g + 1 < NT:
                        modulate(g + 1)
            S.barrier()

        def phase_attn(l):
            TT = 512
            NT = T // TT
            with ExitStack() as st:
                qT = [sb(st, [64, 8, TT], BF16, "qT") for _ in range(2)]
                kT = [sb(st, [64, 2, TT + 256], BF16, "kT") for _ in range(2)]
                vv = [sb(st, [128, 6, 130], BF16, "vv") for _ in range(2)]
                pT = [sb(st, [128, 3, 128], BF16, "pT") for _ in range(3)]
                gA = sb(st, [128, 512], F32, "gA")
                den = sb(st, [128, 8], F32, "den")
                osb = sb(st, [128, 8, 64], F32, "osb")
                osq = sb(st, [128, 512], F32, "osq")
                ssq = sb(st, [128, 1], F32, "ssq")
                onb = sb(st, [128, 512], BF16, "onb")
                oTs = [sb(st, [128, 4, TT], BF16, "oTs") for _ in range(2)]
                pS = [ps(st, [128, 3, 128], F32, "pS") for _ in range(3)]
                pO = [ps(st, [128, 4, 65], F32, "pO") for _ in range(4)]
                ptr = ps(st, [128, 512], BF16, "ptr")
                S.dma('sp', gA[:], grp_g[l, 0:512].partition_broadcast(128), writes=['gA'])

                def load(g):
                    b = g % 2
                    tok0 = g * TT
                    s, s0, ln = seq_of(tok0)
                    lo = 128 if tok0 == s0 else 0
                    hi = 128 if tok0 + TT == s0 + ln else 0
                    S.dma('sp', qT[b][:], q_d[0:8, :, tok0:tok0 + TT].rearrange("h p t -> p h t"), writes=[('qT', b)])
                    S.dma('sp', kT[b][:, :, lo:TT + 256 - hi],
                          q_d[8:10, :, tok0 - 128 + lo:tok0 + TT + 128 - hi].rearrange("h p t -> p h t"),
                          writes=[('kT', b)])
                    S.dma('sp', vv[b][:, lo // 128:6 - hi // 128, :],
                          v_d[tok0 - 128 + lo:tok0 + TT + 128 - hi, :].rearrange("(b p) e -> p b e", p=128),
                          writes=[('vv', b)])

                def emit_norm(b, qb_, po, pok, tok0, do_store):
                    hk = [pok + (0,), pok + (1,)]
                    for hh in range(2):
                        S.dve(lambda: V.tensor_scalar(out=den[:, hh * 4:(hh + 1) * 4], in0=po[hh][:, :, 64],
                                                      scalar1=1.0, scalar2=None, op0=ALU.add), hk + ['den'], ['den'])
                    S.dve(lambda: V.reciprocal(out=den[:], in_=den[:]), ['den'], ['den'])
                    for hh in range(2):
                        S.dve(lambda: V.tensor_tensor(out=osb[:, hh * 4:(hh + 1) * 4, :], in0=po[hh][:, :, 0:64],
                                                      in1=den[:, hh * 4:(hh + 1) * 4].unsqueeze(2).to_broadcast(
                                                          [128, 4, 64]), op=ALU.mult),
                              hk + ['den', 'osb'], ['osb'])
                    of = osb[:].rearrange("p h e -> p (h e)")
                    S.pool(lambda: G.tensor_tensor(out=osq[:], in0=of, in1=of, op=ALU.mult), ['osb'], ['osq'])
                    S.dve(lambda: V.reduce_sum(out=ssq[:], in_=osq[:], axis=AX.X), ['osq'], ['ssq'])
                    S.dve(lambda: V.tensor_scalar(out=ssq[:], in0=ssq[:], scalar1=1.0 / 512, scalar2=RMS_EPS,
                                                  op0=ALU.mult, op1=ALU.add), ['ssq'], ['ssq'])
                    S.act(lambda: A.activation(out=ssq[:], in_=ssq[:], func=AF.Ln), ['ssq'], ['ssq'])
                    S.act(lambda: A.activation(out=ssq[:], in_=ssq[:], func=AF.Exp, scale=-0.5), ['ssq'], ['ssq'])
                    S.dve(lambda: V.scalar_tensor_tensor(out=onb[:], in0=of, scalar=ssq[:, 0:1], in1=gA[:],
                                                         op0=ALU.mult, op1=ALU.mult),
                          ['osb', 'ssq', 'gA', 'onb'], ['onb'])
                    S.pe([(lambda c=c: P.transpose(ptr[:, c * 128:(c + 1) * 128], onb[:, c * 128:(c + 1) * 128],
                                                   ident_b[:])) for c in range(4)], ['onb'], ['ptr'])
                    S.act(lambda: A.copy(out=oTs[b][:, :, qb_ * 128:(qb_ + 1) * 128],
                                         in_=ptr[:].rearrange("p (c t) -> p c t", c=4)), ['ptr'], [('oTs', b)])
                    if do_store:
                        S.dma('sp', oT_v[:, 0:4, tok0:tok0 + TT], oTs[b][:], reads=[('oTs', b)])

                jobs = []
                npi = 0
                npo = 0
                for g in range(NT):
                    b = g % 2
                    tok0 = g * TT
                    s, s0, ln = seq_of(tok0)
                    for qb_ in range(4):
                        gpos = tok0 + qb_ * 128
                        jlo = 1 if gpos == s0 else 0
                        jhi = 2 if gpos + 128 == s0 + ln else 3
                        po = (pO[(npo % 2) * 2], pO[(npo % 2) * 2 + 1])
                        pok = ('pO', npo % 2)
                        npo += 1
                        for h in range(8):
                            jobs.append(dict(g=g, b=b, tok0=tok0, qb=qb_, h=h, kv=h // 4, pi=npi % 3, jlo=jlo, jhi=jhi,
                                             po=po, pok=pok))
                            npi += 1

                def scores(j):
                    b, qb_, h, kv, pi, jlo, jhi = j['b'], j['qb'], j['h'], j['kv'], j['pi'], j['jlo'], j['jhi']
                    S.pe([(lambda jj=jj: P.matmul(pS[pi][:, jj, :],
                                                  kT[b][:, kv, (qb_ + jj) * 128:(qb_ + jj + 1) * 128],
                                                  qT[b][:, h, qb_ * 128:(qb_ + 1) * 128], start=True, stop=True))
                          for jj in range(jlo, jhi)], [('qT', b), ('kT', b)], [('pS', pi)])

                def softmax_pv(j):
                    b, qb_, h, kv, pi, jlo, jhi = j['b'], j['qb'], j['h'], j['kv'], j['pi'], j['jlo'], j['jhi']
                    po, pok = j['po'], j['pok']
                    S.act(lambda: A.activation(out=pT[pi][:, jlo:jhi, :], in_=pS[pi][:, jlo:jhi, :], func=AF.Exp,
                                               bias=nsink[:, l * 8 + h:l * 8 + h + 1], scale=0.125),
                          [('pS', pi)], [('pT', pi)])
                    if jlo == 0:
                        S.pool(lambda: G.tensor_tensor(out=pT[pi][:, 0, :], in0=pT[pi][:, 0, :], in1=maskp[:],
                                                       op=ALU.mult), [('pT', pi)], [('pT', pi)])
                    if jhi == 3:
                        S.pool(lambda: G.tensor_tensor(out=pT[pi][:, 2, :], in0=pT[pi][:, 2, :], in1=maskn[:],
                                                       op=ALU.mult), [('pT', pi)], [('pT', pi)])
                    S.pe([(lambda jj=jj: P.matmul(po[h // 4][:, h % 4, :], pT[pi][:, jj, :],
                                                  vv[b][:, qb_ + jj, kv * 65:(kv + 1) * 65],
                                                  start=(jj == jlo), stop=(jj == jhi - 1)))
                          for jj in range(jlo, jhi)], [('pT', pi), ('vv', b)], [pok + (h // 4,)])

                load(0)
                if NT > 1:
                    load(1)
                pending = None
                scores(jobs[0])
                for i, j in enumerate(jobs):
                    if i + 1 < len(jobs):
                        scores(jobs[i + 1])
                    softmax_pv(j)
                    if j['h'] == 7:
                        if pending is not None:
                            emit_norm(*pending)
                        pending = (j['b'], j['qb'], j['po'], j['pok'], j['tok0'], j['qb'] == 3)
                        if j['qb'] == 3 and j['g'] + 2 < NT:
                            load(j['g'] + 2)
                if pending is not None:
                    emit_norm(*pending)
            S.barrier()

        def fft_stage_a(st, li, srcs, dstA, tag):
            L = LS[li]
            H1 = L // 128
            nk = 2 * NK1[li]
            fa = sb(st, [H1, nk], BF16, "fa")
            S.dma('sp', fa[:], cd[f"fa{li}"], writes=[(tag, 'fa')])
            xc = [sb(st, [H1, 16, 512], BF16, "xc") for _ in range(2)]
            asb = [sb(st, [nk, 16, 512], BF16, "asb") for _ in range(2)]
            pa = [ps(st, [nk, 512], F32, "pa") for _ in range(3)]
            n = 0
            it = 0
            for sq in range(2):
                sv = srcs[sq].rearrange("(a b) c -> a b c", b=128)
                for ch in range(8):
                    bb = it % 2
                    it += 1
                    S.dma('sp', xc[bb][:], sv[:, ch * 16:(ch + 1) * 16, :], writes=[(tag, 'xc', bb)])
                    for j in range(16):
                        pi = n % 3
                        n += 1
                        S.pe([lambda: P.matmul(pa[pi][:], fa[:], xc[bb][:, j, :], start=True, stop=True)],
                             [(tag, 'fa'), (tag, 'xc', bb)], [(tag, 'pa', pi)])
                        if j % 2 == 0:
                            S.act(lambda: A.copy(out=asb[bb][:, j, :], in_=pa[pi][:]), [(tag, 'pa', pi)],
                                  [(tag, 'asb', bb, j)])
                        else:
                            S.dve(lambda: V.tensor_copy(out=asb[bb][:, j, :], in_=pa[pi][:]), [(tag, 'pa', pi)],
                                  [(tag, 'asb', bb, j)])
                    S.dma('sp', dstA[sq, :, ch * 16:(ch + 1) * 16, :], asb[bb][:],
                          reads=[(tag, 'asb', bb, j) for j in range(16)])

        def fft_stage_c(st, li, srcA, mode, dstB, tag):
            nk1 = NK1[li]
            mk = [sb(st, [128, 6, 128], BF16, "mk") for _ in range(2)]
            ak = [sb(st, [128, 2, 512], BF16, "ak") for _ in range(2)]
            kf = [sb(st, [128, 2, 512], F32, "kf") for _ in range(2)]
            tt = [sb(st, [128, 512], F32, "tt") for _ in range(8)]
            yb = [sb(st, [128, 2, 512], BF16, "yb") for _ in range(2)]
            bsb = [sb(st, [128, 2, 512], BF16, "bsb") for _ in range(2)]
            px = [ps(st, [128, 512], F32, "px") for _ in range(4)]
            pb = [ps(st, [128, 512], F32, "pb") for _ in range(4)]
            it = 0
            for k1 in range(nk1):
                mb = k1 % 2
                S.dma('sp', mk[mb][:], cd[f"mm{li}"][k1], writes=[(tag, 'mk', mb)])
                if mode == 'conv':
                    S.dma('sp', kf[mb][:], Kf_d[li][k1], writes=[(tag, 'kf', mb)])
                for sq in range(2):
                    ab = it % 2
                    it += 1
                    av = srcA[sq].rearrange("(r k) n c -> n r k c", r=2)
                    S.dma('sp', ak[ab][:], av[:, :, k1, :], writes=[(tag, 'ak', ab)])
                    pr, pi_ = px[ab * 2], px[ab * 2 + 1]
                    S.pe([lambda: P.matmul(pr[:], mk[mb][:, 0, :], ak[ab][:, 0, :], start=True, stop=False),
                          lambda: P.matmul(pr[:], mk[mb][:, 2, :], ak[ab][:, 1, :], start=False, stop=True),
                          lambda: P.matmul(pi_[:], mk[mb][:, 1, :], ak[ab][:, 0, :], start=True, stop=False),
                          lambda: P.matmul(pi_[:], mk[mb][:, 0, :], ak[ab][:, 1, :], start=False, stop=True)],
                         [(tag, 'mk', mb), (tag, 'ak', ab)], [(tag, 'px', ab)])
                    if mode == 'filter':
                        kk = kf[mb]
                        if sq == 0:
                            S.act(lambda: A.copy(out=kk[:, 0, :], in_=pr[:]), [(tag, 'px', ab)], [(tag, 'kf', mb)])
                            S.dve(lambda: V.tensor_copy(out=kk[:, 1, :], in_=pi_[:]), [(tag, 'px', ab)],
                                  [(tag, 'kf', mb)])
                        else:
                            S.dve(lambda: V.tensor_tensor(out=kk[:, 0, :], in0=pr[:], in1=kk[:, 0, :], op=ALU.add),
                                  [(tag, 'px', ab), (tag, 'kf', mb)], [(tag, 'kf', mb)])
                            S.dve(lambda: V.scalar_tensor_tensor(out=kk[:, 1, :], in0=pi_[:], scalar=-1.0,
                                                                 in1=kk[:, 1, :], op0=ALU.mult, op1=ALU.add),
                                  [(tag, 'px', ab), (tag, 'kf', mb)], [(tag, 'kf', mb)])
                            S.dma('sp', Kf_d[li][k1], kk[:], reads=[(tag, 'kf', mb)])
                        continue
                    kk = kf[mb]
                    S.dve(lambda: V.tensor_tensor(out=tt[ab * 4 + 0][:], in0=pr[:], in1=kk[:, 0, :], op=ALU.mult),
                          [(tag, 'px', ab), (tag, 'kf', mb)], [(tag, 'tt', ab * 4 + 0)])
                    S.dve(lambda: V.tensor_tensor(out=tt[ab * 4 + 1][:], in0=pi_[:], in1=kk[:, 1, :], op=ALU.mult),
                          [(tag, 'px', ab), (tag, 'kf', mb)], [(tag, 'tt', ab * 4 + 1)])
                    S.pool(lambda: G.tensor_tensor(out=yb[ab][:, 0, :], in0=tt[ab * 4 + 0][:], in1=tt[ab * 4 + 1][:], op=ALU.subtract),
                           [(tag, 'tt', ab * 4 + 0), (tag, 'tt', ab * 4 + 1)], [(tag, 'yb', ab)])
                    S.dve(lambda: V.tensor_tensor(out=tt[ab * 4 + 2][:], in0=pr[:], in1=kk[:, 1, :], op=ALU.mult),
                          [(tag, 'px', ab), (tag, 'kf', mb)], [(tag, 'tt', ab * 4 + 2)])
                    S.dve(lambda: V.tensor_tensor(out=tt[ab * 4 + 3][:], in0=pi_[:], in1=kk[:, 0, :], op=ALU.mult),
                          [(tag, 'px', ab), (tag, 'kf', mb)], [(tag, 'tt', ab * 4 + 3)])
                    S.pool(lambda: G.tensor_tensor(out=yb[ab][:, 1, :], in0=tt[ab * 4 + 2][:], in1=tt[ab * 4 + 3][:], op=ALU.add),
                           [(tag, 'tt', ab * 4 + 2), (tag, 'tt', ab * 4 + 3)], [(tag, 'yb', ab)])
                    br, bi = pb[ab * 2], pb[ab * 2 + 1]
                    S.pe([lambda: P.matmul(br[:], mk[mb][:, 3, :], yb[ab][:, 0, :], start=True, stop=False),
                          lambda: P.matmul(br[:], mk[mb][:, 4, :], yb[ab][:, 1, :], start=False, stop=True),
                          lambda: P.matmul(bi[:], mk[mb][:, 3, :], yb[ab][:, 1, :], start=True, stop=False),
                          lambda: P.matmul(bi[:], mk[mb][:, 5, :], yb[ab][:, 0, :], start=False, stop=True)],
                         [(tag, 'mk', mb), (tag, 'yb', ab)], [(tag, 'pb', ab)])
                    S.act(lambda: A.copy(out=bsb[ab][:, 0, :], in_=br[:]), [(tag, 'pb', ab)], [(tag, 'bsb', ab, 0)])
                    S.act(lambda: A.copy(out=bsb[ab][:, 1, :], in_=bi[:]), [(tag, 'pb', ab)], [(tag, 'bsb', ab, 1)])
                    bv = dstB[sq].rearrange("(r k) n c -> n r k c", r=2)
                    S.dma('sp', bv[:, :, k1, :], bsb[ab][:], reads=[(tag, 'bsb', ab, 0), (tag, 'bsb', ab, 1)])

        def fft_stage_ah(st, li, srcB, dsts, tag):
            L = LS[li]
            H1 = L // 128
            nk = 2 * NK1[li]
            ga = sb(st, [nk, H1], BF16, "ga")
            S.dma('sp', ga[:], cd[f"ga{li}"], writes=[(tag, 'ga')])
            bc = [sb(st, [nk, 16, 512], BF16, "bc") for _ in range(2)]
            ysb = [sb(st, [H1, 16, 512], F32, "ysb") for _ in range(2)]
            py = [ps(st, [H1, 512], F32, "py") for _ in range(3)]
            n = 0
            it = 0
            for sq in range(2):
                dv = dsts[sq].rearrange("(a b) c -> a b c", b=128)
                for ch in range(8):
                    bb = it % 2
                    it += 1
                    S.dma('sp', bc[bb][:], srcB[sq, :, ch * 16:(ch + 1) * 16, :], writes=[(tag, 'bc', bb)])
                    for j in range(16):
                        pi = n % 3
                        n += 1
                        S.pe([lambda: P.matmul(py[pi][:], ga[:], bc[bb][:, j, :], start=True, stop=True)],
                             [(tag, 'ga'), (tag, 'bc', bb)], [(tag, 'py', pi)])
                        if j % 2 == 0:
                            S.act(lambda: A.mul(out=ysb[bb][:, j, :], in_=py[pi][:], mul=1.0 / (2 * L)),
                                  [(tag, 'py', pi)], [(tag, 'ysb', bb, j)])
                        else:
                            S.dve(lambda: V.tensor_scalar(out=ysb[bb][:, j, :], in0=py[pi][:], scalar1=1.0 / (2 * L),
                                                          scalar2=None, op0=ALU.mult),
                                  [(tag, 'py', pi)], [(tag, 'ysb', bb, j)])
                    S.dma('sp', dv[:, ch * 16:(ch + 1) * 16, :], ysb[bb][:],
                          reads=[(tag, 'ysb', bb, j) for j in range(16)])

        def phase_filter(l, li):
            L = LS[li]
            with ExitStack() as st:
                w1 = sb(st, [33, 64], F32, "w1")
                w2 = sb(st, [64, 64], F32, "w2")
                w3 = sb(st, [64, 64], F32, "w3")
                w4 = sb(st, [64, 1024], F32, "w4")
                fr = sb(st, [64, 1], F32, "fr")
                bb_ = sb(st, [64, 3], F32, "bb")
                dec = sb(st, [128, 1024], F32, "dec")
                negt = sb(st, [128, L // 128], F32, "negt")
                zt = [sb(st, [33, 512], F32, "zt") for _ in range(2)]
                hh = [sb(st, [64, 512], F32, "hh") for _ in range(3)]
                arg = sb(st, [64, 512], F32, "arg")
                kq = sb(st, [64, 512], F32, "kq")
                win_ = sb(st, [128, 1024], F32, "winw")
                fsb = [sb(st, [128, 1024], BF16, "fsb") for _ in range(2)]
                pm = [ps(st, [64, 512], F32, "pm") for _ in range(2)]
                pf = [ps(st, [128, 512], F32, "pf") for _ in range(4)]
                with nc.allow_non_contiguous_dma(reason="tiny filter params"):
                    S.dma('sp', w1[:], hy_w1[l], writes=['fw'])
                    S.dma('sp', w2[:], hy_w2[l], writes=['fw'])
                    S.dma('sp', w3[:], hy_w3[l], writes=['fw'])
                    S.dma('sp', w4[:], hy_w4[l], writes=['fw'])
                    S.dma('sp', fr[:], hy_freq[l].rearrange("(p o) -> p o", o=1), writes=['fr'])
                    S.dma('sp', bb_[:, 0:1], hy_b1[l].rearrange("(p o) -> p o", o=1), writes=['fb'])
                    S.dma('sp', bb_[:, 1:2], hy_b2[l].rearrange("(p o) -> p o", o=1), writes=['fb'])
                    S.dma('sp', bb_[:, 2:3], hy_b3[l].rearrange("(p o) -> p o", o=1), writes=['fb'])
                    S.dma('sp', dec[:], hy_decay[l].rearrange("a c -> (a c)").partition_broadcast(128), writes=['dec'])
                    S.dma('sp', negt[:], cd[f"negt{li}"], writes=['negt'])
                S.barrier()
                S.dve(lambda: V.tensor_scalar(out=bb_[:], in0=bb_[:], scalar1=fr[:, 0:1], scalar2=None, op0=ALU.mult),
                      ['fb', 'fr'], ['fb'])
                S.dve(lambda: V.scalar_tensor_tensor(out=dec[:], in0=dec[:], scalar=-1.0, in1=dec[:], op0=ALU.mult,
                                                     op1=ALU.max), ['dec'], ['dec'])
                ws = [w1, w2, w3]
                np_ = 0
                for ch in range(L // 512):
                    zb = ch % 2
                    S.dma('sp', zt[zb][:], cd[f"zt{li}"][:, ch * 512:(ch + 1) * 512], writes=[('zt', zb)])
                    cur = zt[zb]
                    ckey = ('zt', zb)
                    kdim = 33
                    for ly in range(3):
                        pi = np_ % 2
                        np_ += 1
                        S.pe([lambda: P.matmul(pm[pi][:], ws[ly][0:kdim, :], cur[0:kdim, :], start=True, stop=True)],
                             [ckey, 'fw'], [('pm', pi)])
                        S.dve(lambda: V.tensor_scalar(out=arg[:], in0=pm[pi][:], scalar1=fr[:, 0:1],
                                                      scalar2=bb_[:, ly:ly + 1], op0=ALU.mult, op1=ALU.add),
                              [('pm', pi), 'fb', 'fr'], ['arg'])
                        S.dve(lambda: V.tensor_scalar(out=kq[:], in0=arg[:], scalar1=1.0 / (2 * PI), scalar2=12582912.0,
                                                      op0=ALU.mult, op1=ALU.add), ['arg'], ['kq'])
                        S.dve(lambda: V.tensor_scalar(out=kq[:], in0=kq[:], scalar1=-12582912.0, scalar2=None,
                                                      op0=ALU.add), ['kq'], ['kq'])
                        S.dve(lambda: V.scalar_tensor_tensor(out=arg[:], in0=kq[:], scalar=-2 * PI, in1=arg[:],
                                                             op0=ALU.mult, op1=ALU.add), ['kq', 'arg'], ['arg'])
                        S.dve(lambda: V.tensor_scalar(out=arg[:], in0=arg[:], scalar1=PI, scalar2=-PI,
                                                      op0=ALU.min, op1=ALU.max), ['arg'], ['arg'])
                        S.act(lambda: A.activation(out=hh[ly][:], in_=arg[:], func=AF.Sin), ['arg'], [('hh', ly)])
                        cur = hh[ly]
                        ckey = ('hh', ly)
                        kdim = 64
                    for blk in range(4):
                        gb = ch * 4 + blk
                        fb = gb % 2
                        p0, p1 = pf[(gb % 2) * 2], pf[(gb % 2) * 2 + 1]
                        S.pe([lambda: P.matmul(p0[:], hh[2][:, blk * 128:(blk + 1) * 128], w4[:, 0:512],
                                               start=True, stop=True),
                              lambda: P.matmul(p1[:], hh[2][:, blk * 128:(blk + 1) * 128], w4[:, 512:1024],
                                               start=True, stop=True)], [('hh', 2), 'fw'], [('pf', gb % 2)])
                        S.act(lambda: A.activation(out=win_[:], in_=dec[:], func=AF.Exp, scale=negt[:, gb:gb + 1]),
                              ['dec', 'negt', 'winw'], ['winw'])
                        S.dve(lambda: V.tensor_tensor(out=fsb[fb][:, 0:512], in0=p0[:], in1=win_[:, 0:512], op=ALU.mult),
                              [('pf', gb % 2), 'winw'], [('fsb', fb, 0)])
                        S.dve(lambda: V.tensor_tensor(out=fsb[fb][:, 512:1024], in0=p1[:], in1=win_[:, 512:1024],
                                                      op=ALU.mult), [('pf', gb % 2), 'winw'], [('fsb', fb, 1)])
                        S.dma('sp', filt_d[li][0, gb * 128:(gb + 1) * 128, :], fsb[fb][:, 0:512],
                              reads=[('fsb', fb, 0)])
                        S.dma('sp', filt_d[li][1, gb * 128:(gb + 1) * 128, :], fsb[fb][:, 512:1024],
                              reads=[('fsb', fb, 1)])
            S.barrier()

        def phase_hyena(l):
            for li in range(2):
                phase_filter(l, li)
                with ExitStack() as st:
                    fft_stage_a(st, li, [filt_d[li][0], filt_d[li][1]], A_d[li], 'fa')
                S.barrier()
                with ExitStack() as st:
                    fft_stage_c(st, li, A_d[li], 'filter', None, 'fc')
                S.barrier()
                s0 = SEQS[li * 2][0]
                s1 = SEQS[li * 2 + 1][0]
                L = LS[li]
                with ExitStack() as st:
                    fft_stage_a(st, li, [u_d[s0:s0 + L, :], u_d[s1:s1 + L, :]], A_d[li], 'ua')
                S.barrier()
                with ExitStack() as st:
                    fft_stage_c(st, li, A_d[li], 'conv', B_d[li], 'uc')
                S.barrier()
                with ExitStack() as st:
                    fft_stage_ah(st, li, B_d[li], [y_d[s0:s0 + L, :], y_d[s1:s1 + L, :]], 'uh')
                S.barrier()
            with ExitStack() as st:
                TT = 512
                gH = sb(st, [128, 512], F32, "gH")
                hb = sb(st, [128, 512], F32, "hb")
                yt = [sb(st, [128, 4, 512], F32, "yt") for _ in range(2)]
                ut = [sb(st, [128, 4, 512], BF16, "ut") for _ in range(2)]
                x0t = [sb(st, [128, 4, 512], BF16, "x0t") for _ in range(2)]
                o1 = sb(st, [128, 512], F32, "o1")
                osq = sb(st, [128, 512], F32, "osq")
                ssq = sb(st, [128, 1], F32, "ssq")
                onb = sb(st, [128, 512], BF16, "onb")
                oTs = [sb(st, [128, 4, TT], BF16, "oTs") for _ in range(2)]
                ptr = ps(st, [128, 512], BF16, "ptr")
                S.dma('sp', gH[:], grp_g[l, 512:1024].partition_broadcast(128), writes=['gH'])
                S.dma('sp', hb[:], hy_bias[l].partition_broadcast(128), writes=['hb'])

                def load(g):
                    b = g % 2
                    sl = slice(g * TT, (g + 1) * TT)
                    S.dma('sp', yt[b][:], y_d[sl, :].rearrange("(b p) c -> p b c", p=128), writes=[('yt', b)])
                    S.dma('sp', ut[b][:], u_d[sl, :].rearrange("(b p) c -> p b c", p=128), writes=[('ut', b)])
                    S.dma('sp', x0t[b][:], x0_d[sl, :].rearrange("(b p) c -> p b c", p=128), writes=[('x0t', b)])

                load(0)
                for g in range(T // TT):
                    b = g % 2
                    if g + 1 < T // TT:
                        load(g + 1)
                    for blk in range(4):
                        S.pool(lambda: G.tensor_tensor(out=o1[:], in0=ut[b][:, blk, :], in1=hb[:], op=ALU.mult),
                               [('ut', b), 'hb', 'o1'], ['o1'])
                        S.dve(lambda: V.tensor_tensor(out=o1[:], in0=o1[:], in1=yt[b][:, blk, :], op=ALU.add),
                              ['o1', ('yt', b)], ['o1'])
                        S.dve(lambda: V.tensor_tensor(out=o1[:], in0=o1[:], in1=x0t[b][:, blk, :], op=ALU.mult),
                              ['o1', ('x0t', b)], ['o1'])
                        S.pool(lambda: G.tensor_tensor(out=osq[:], in0=o1[:], in1=o1[:], op=ALU.mult), ['o1'], ['osq'])
                        S.dve(lambda: V.reduce_sum(out=ssq[:], in_=osq[:], axis=AX.X), ['osq'], ['ssq'])
                        S.dve(lambda: V.tensor_scalar(out=ssq[:], in0=ssq[:], scalar1=1.0 / 512, scalar2=RMS_EPS,
                                                      op0=ALU.mult, op1=ALU.add), ['ssq'], ['ssq'])
                        S.act(lambda: A.activation(out=ssq[:], in_=ssq[:], func=AF.Ln), ['ssq'], ['ssq'])
                        S.act(lambda: A.activation(out=ssq[:], in_=ssq[:], func=AF.Exp, scale=-0.5), ['ssq'], ['ssq'])
                        S.dve(lambda: V.scalar_tensor_tensor(out=onb[:], in0=o1[:], scalar=ssq[:, 0:1], in1=gH[:],
                                                             op0=ALU.mult, op1=ALU.mult),
                              ['o1', 'ssq', 'gH', 'onb'], ['onb'])
                        S.pe([(lambda c=c: P.transpose(ptr[:, c * 128:(c + 1) * 128], onb[:, c * 128:(c + 1) * 128],
                                                       ident_b[:])) for c in range(4)], ['onb'], ['ptr'])
                        S.act(lambda: A.copy(out=oTs[b][:, :, blk * 128:(blk + 1) * 128],
                                             in_=ptr[:].rearrange("p (c t) -> p c t", c=4)), ['ptr'], [('oTs', b)])
                    S.dma('sp', oT_v[:, 4:8, g * TT:(g + 1) * TT], oTs[b][:], reads=[('oTs', b)])
            S.barrier()

        phase_tin()
        for l in range(RUN_LAYERS):
            if STOP_AFTER == ('tin', l):
                break
            phase_mod(l)
            if STOP_AFTER == ('mod', l):
                break
            phase_ffn(l, 0)
            if STOP_AFTER == ('ffn1', l):
                break
            phase_m1(l)
            phase_attn(l)
            phase_hyena(l)
            phase_wout(l)
            if STOP_AFTER == ('mix', l):
                break
            phase_ffn(l, 1)
        phase_tout()
        S.finish()
        print("sched: instr counts", S.cnt, S.dcnt, "waits", S.nwait)
        global LAST_MARKS
        LAST_MARKS = S.marks
    return nc, consts


def kernel(x_prompt, x_sample, c_prompt, c_sample, ada_w, ada_b, ffn1_wi, ffn1_wo, ffn2_wi, ffn2_wo,
           ln_g, ln_b, w_in, w_out, sink, grp_norm_g, hy_conv_w, hy_conv_b, hy_w1, hy_b1, hy_w2, hy_b2,
           hy_w3, hy_b3, hy_w4, hy_freq, hy_decay, hy_bias):
    nc, consts = build()
    f = lambda a: np.ascontiguousarray(np.asarray(a, dtype=np.float32))
    f0 = f
    if RUN_LAYERS < DEPTH:
        f = lambda a: f0(np.asarray(a)[:RUN_LAYERS])
    shared = {
        "ada_w": f(ada_w), "ada_b": f(ada_b), "ffn1_wi": f(ffn1_wi), "ffn1_wo": f(ffn1_wo),
        "ffn2_wi": f(ffn2_wi), "ffn2_wo": f(ffn2_wo), "ln_g": f(ln_g), "ln_b": f(ln_b),
        "w_in": f(w_in), "w_out": f(w_out), "sink": f(sink), "grp_norm_g": f(grp_norm_g),
        "hy_conv_w": f(hy_conv_w), "hy_conv_b": f(hy_conv_b), "hy_w1": f(hy_w1), "hy_b1": f(hy_b1),
        "hy_w2": f(hy_w2), "hy_b2": f(hy_b2), "hy_w3": f(hy_w3), "hy_b3": f(hy_b3), "hy_w4": f(hy_w4),
        "hy_freq": f(hy_freq), "hy_decay": f(hy_decay), "hy_bias": f(hy_bias),
    }
    for k, v in consts.items():
        shared["c_" + k] = v
    xp = f0(x_prompt)
    xs = f0(x_sample)
    cp = f0(c_prompt)
    cs_ = f0(c_sample)
    in_maps = []
    for c in range(NCORES):
        m = dict(shared)
        m["xin"] = np.concatenate([xp[2 * c].reshape(4096, D), xp[2 * c + 1].reshape(4096, D),
                                   xs[2 * c].reshape(2048, D), xs[2 * c + 1].reshape(2048, D)], axis=0)
        m["cin"] = np.concatenate([cp[2 * c:2 * c + 2], cs_[2 * c:2 * c + 2]], axis=0)
        in_maps.append(m)
    res = run_bass_kernel_spmd(nc, in_maps, core_ids=list(range(NCORES)))
    yp = np.empty((16, 4096, D), np.float32)
    ys = np.empty((16, 2048, D), np.float32)
    for c in range(NCORES):
        y = res.results[c]["yout"]
        yp[2 * c] = y[0:4096]
        yp[2 * c + 1] = y[4096:8192]
        ys[2 * c] = y[8192:10240]
        ys[2 * c + 1] = y[10240:12288]
    return yp, ys
```

```python
import math
from contextlib import ExitStack
import numpy as np
import ml_dtypes
import concourse.bass as bass
import concourse.mybir as mybir
from concourse.bass_utils import run_bass_kernel_spmd

F32 = mybir.dt.float32
BF16 = mybir.dt.bfloat16
AF = mybir.ActivationFunctionType
ALU = mybir.AluOpType
AX = mybir.AxisListType

D = 1024
DEPTH = 4
DFF = 2816
NIN = 2304
ALPHA = float((2 * DEPTH) ** 0.25)
LN_EPS = 1e-5 / (ALPHA * ALPHA)
RMS_EPS = 1e-6
NCORES = 8
SEQS = [(0, 4096), (4096, 4096), (8192, 2048), (10240, 2048)]
T = 12288
LS = [4096, 2048]
PI = math.pi

RUN_LAYERS = DEPTH
LAST_MARKS = None
FFN_DEBUG = 4
STOP_AFTER = None


def seq_of(tok):
    for i, (s0, ln) in enumerate(SEQS):
        if s0 <= tok < s0 + ln:
            return i, s0, ln
    raise ValueError


class Sched:
    NDMA = 8

    def __init__(self, nc, stack):
        self.nc = nc
        self.E = {'pe': nc.tensor, 'act': nc.scalar, 'dve': nc.vector, 'pool': nc.gpsimd, 'sp': nc.sync}
        self.sem = {e: stack.enter_context(nc.semaphore(f"s_{e}")) for e in ('pe', 'act', 'dve', 'pool')}
        self.cnt = {e: 0 for e in self.sem}
        self.nd = {'sp': 3, 'pool': 2}
        self.dsem = {q: [stack.enter_context(nc.semaphore(f"d_{q}{i}")) for i in range(self.nd[q])]
                     for q in ('sp', 'pool')}
        self.dcnt = {q: 0 for q in self.dsem}
        self.known = {e: {} for e in self.E}
        self.res = {}
        self.nwait = 0

    PS_NAMES = frozenset(['pt', 'pm', 'pst', 'gu', 'pf', 'psq', 'psr', 'psv', 'psz', 'ptr', 'pS', 'pO', 'pa', 'px',
                          'pb', 'py'])

    def _isp(self, k):
        if isinstance(k, tuple):
            return any(isinstance(x, str) and x in self.PS_NAMES for x in k)
        return k in self.PS_NAMES

    def _split(self, reads, writes):
        r2 = [k for k in reads if not self._isp(k)]
        w2 = list(writes) + [k for k in reads if self._isp(k)]
        return r2, w2

    def _wait(self, e, tok):
        if tok is None:
            return
        sem, val = tok
        k = self.known[e]
        sid = id(sem)
        if k.get(sid, 0) >= val:
            return
        self.E[e].wait_ge(sem, val)
        self.nwait += 1
        k[sid] = val

    def deps(self, e, reads, writes):
        for r in reads:
            st = self.res.get(r)
            if st is not None:
                self._wait(e, st[0])
        for w in writes:
            st = self.res.get(w)
            if st is not None:
                self._wait(e, st[0])
                for t in st[1]:
                    self._wait(e, t)

    def commit(self, tok, reads, writes):
        for r in reads:
            st = self.res.setdefault(r, [None, []])
            st[1].append(tok)
            if len(st[1]) > 10:
                d = {}
                for s, v in st[1]:
                    if id(s) not in d or d[id(s)][1] < v:
                        d[id(s)] = (s, v)
                st[1] = list(d.values())
        for w in writes:
            self.res[w] = [tok, []]

    def op(self, e, fn, reads=(), writes=()):
        reads, writes = self._split(reads, writes)
        self.deps(e, reads, writes)
        ins = fn()
        self.cnt[e] += 1
        ins.then_inc(self.sem[e], 1)
        tok = (self.sem[e], self.cnt[e])
        if e == 'pe':
            self.known[e][id(self.sem[e])] = self.cnt[e]
        self.commit(tok, reads, writes)
        return tok

    def pe(self, fns, reads=(), writes=()):
        reads, writes = self._split(reads, writes)
        self.deps('pe', reads, writes)
        ins = None
        for fn in fns:
            ins = fn()
        self.cnt['pe'] += 1
        ins.then_inc(self.sem['pe'], 1)
        tok = (self.sem['pe'], self.cnt['pe'])
        self.known['pe'][id(self.sem['pe'])] = self.cnt['pe']
        self.commit(tok, reads, writes)
        return tok

    def act(self, fn, r=(), w=()):
        return self.op('act', fn, r, w)

    def dve(self, fn, r=(), w=()):
        return self.op('dve', fn, r, w)

    def pool(self, fn, r=(), w=()):
        return self.op('pool', fn, r, w)

    def dma(self, q, out, in_, reads=(), writes=()):
        self.deps(q, reads, writes)
        j = self.dcnt[q]
        self.dcnt[q] += 1
        sem = self.dsem[q][j % self.nd[q]]
        rnd = j // self.nd[q]
        if rnd > 0:
            self._wait(q, (sem, 16 * rnd))
        self.E[q].dma_start(out=out, in_=in_).then_inc(sem, 16)
        tok = (sem, 16 * (rnd + 1))
        self.commit(tok, reads, writes)
        return tok

    def _all_tokens(self):
        toks = []
        for q in self.dsem:
            j = self.dcnt[q]
            for i in range(min(j, self.nd[q])):
                last = j - 1 - i
                toks.append((self.dsem[q][last % self.nd[q]], 16 * (last // self.nd[q] + 1)))
        for x in self.sem:
            if self.cnt[x] > 0:
                toks.append((self.sem[x], self.cnt[x]))
        return toks

    def barrier(self):
        import inspect
        if not hasattr(self, 'marks'):
            self.marks = []
        fr = inspect.stack()[1]
        self.marks.append((fr.function, fr.lineno, dict(self.cnt)))
        toks = self._all_tokens()
        for e in self.E:
            for t in toks:
                self._wait(e, t)
        self.res = {}

    def finish(self):
        for t in self._all_tokens():
            self._wait('sp', t)


def fft_consts(L):
    N = 2 * L
    N1 = N // 128
    H1 = N1 // 2
    nk1 = H1 + 1
    n1 = np.arange(H1)[:, None]
    k1 = np.arange(nk1)[None, :]
    ang = 2 * np.pi * n1 * k1 / N1
    FA = np.concatenate([np.cos(ang), -np.sin(ang)], axis=1)
    w = np.full(nk1, 2.0)
    w[0] = 1
    w[-1] = 1
    GA = np.concatenate([(w[None, :] * np.cos(ang)).T, (-w[None, :] * np.sin(ang)).T], axis=0)
    n2 = np.arange(128)[:, None]
    k2 = np.arange(128)[None, :]
    Ms = []
    for kk in range(nk1):
        th = 2 * np.pi * ((n2 * (kk + N1 * k2)) % N) / N
        Mre = np.cos(th)
        Mim = -np.sin(th)
        Ms.append(np.stack([Mre, Mim, -Mim, Mre.T, Mim.T, -Mim.T], axis=1))
    M = np.stack(Ms, axis=0)
    return FA, GA, M


def host_consts():
    bf = ml_dtypes.bfloat16
    c = {}
    c["ident_f"] = np.eye(128, dtype=np.float32)
    c["ident_b"] = np.eye(128, dtype=np.float32).astype(bf)
    c["ones_b"] = np.full((128, 128), 1.0 / D, dtype=np.float32).astype(bf)
    rt = np.zeros((64, 16), np.float32)
    for i in range(8):
        rt[i + 8, i] = -1.0
        rt[i, i + 8] = 1.0
    c["rt"] = rt.astype(bf)
    inv = 500000.0 ** (-np.arange(0, 16, 2, dtype=np.float64) / 16)
    ang = np.arange(4096, dtype=np.float64)[None, :] * np.concatenate([inv, inv])[:, None]
    c["cos_t"] = np.cos(ang).astype(np.float32)
    c["sin_t"] = np.sin(ang).astype(np.float32)
    s = np.arange(128)[:, None]
    q = np.arange(128)[None, :]
    c["maskp"] = (s >= q).astype(np.float32).astype(bf)
    c["maskn"] = (s <= q).astype(np.float32).astype(bf)
    for i, L in enumerate(LS):
        FA, GA, M = fft_consts(L)
        c[f"fa{i}"] = FA.astype(np.float32).astype(bf)
        c[f"ga{i}"] = GA.astype(np.float32).astype(bf)
        c[f"mm{i}"] = M.astype(np.float32).astype(bf)
        t = np.linspace(0.0, 1.0, L, dtype=np.float32).astype(np.float64)
        wv = 2.0 * np.pi * np.arange(L, dtype=np.float64) / L
        fb = np.linspace(1e-4, 15, 16, dtype=np.float32).astype(np.float64)[None]
        z = np.concatenate([t[:, None], np.cos(fb * wv[:, None]), -np.sin(fb * wv[:, None])], axis=-1)
        c[f"zt{i}"] = np.ascontiguousarray(z.T).astype(np.float32)
        c[f"negt{i}"] = np.ascontiguousarray((-t).reshape(L // 128, 128).T).astype(np.float32)
    return c


def build():
    nc = bass.Bass("TRN2", target_bir_lowering=False)
    DEPTH = RUN_LAYERS
    consts = host_consts()

    def din(name, shape, dt=F32):
        return nc.dram_tensor(name, list(shape), dt, kind="ExternalInput").ap()

    def dscr(name, shape, dt=F32):
        return nc.dram_tensor(name, list(shape), dt).ap()

    xin = din("xin", [T, D])
    cin = din("cin", [4, D])
    ada_w = din("ada_w", [DEPTH, D, 9 * D])
    ada_b = din("ada_b", [DEPTH, 9 * D])
    ffn_wi = [din("ffn1_wi", [DEPTH, D, 2 * DFF]), din("ffn2_wi", [DEPTH, D, 2 * DFF])]
    ffn_wo = [din("ffn1_wo", [DEPTH, DFF, D]), din("ffn2_wo", [DEPTH, DFF, D])]
    ln_g = din("ln_g", [DEPTH, 3, D])
    ln_b = din("ln_b", [DEPTH, 3, D])
    w_in = din("w_in", [DEPTH, D, NIN])
    w_out = din("w_out", [DEPTH, D, D])
    sink = din("sink", [DEPTH, 8])
    grp_g = din("grp_norm_g", [DEPTH, D])
    hy_cw = din("hy_conv_w", [DEPTH, 3, 1536])
    hy_cb = din("hy_conv_b", [DEPTH, 1536])
    hy_w1 = din("hy_w1", [DEPTH, 33, 64])
    hy_b1 = din("hy_b1", [DEPTH, 64])
    hy_w2 = din("hy_w2", [DEPTH, 64, 64])
    hy_b2 = din("hy_b2", [DEPTH, 64])
    hy_w3 = din("hy_w3", [DEPTH, 64, 64])
    hy_b3 = din("hy_b3", [DEPTH, 64])
    hy_w4 = din("hy_w4", [DEPTH, 64, 1024])
    hy_freq = din("hy_freq", [DEPTH, 64])
    hy_decay = din("hy_decay", [DEPTH, 2, 512])
    hy_bias = din("hy_bias", [DEPTH, 512])
    cd = {}
    for k, v in consts.items():
        cd[k] = din("c_" + k, v.shape, BF16 if v.dtype == ml_dtypes.bfloat16 else F32)
    yout = nc.dram_tensor("yout", [T, D], F32, kind="ExternalOutput").ap()

    xT_d = dscr("xT_d", [D, T])
    q_d = dscr("q_d", [10, 64, T], BF16)
    v_d = dscr("v_d", [T, 130], BF16)
    u_d = dscr("u_d", [T, 512], BF16)
    x0_d = dscr("x0_d", [T, 512], BF16)
    y_d = dscr("y_d", [T, 512], F32)
    oT_d = dscr("oT_d", [D, T], BF16)
    filt_d = [dscr(f"filt_d{i}", [2, L, 512], BF16) for i, L in enumerate(LS)]
    NK1 = [L // 128 + 1 for L in LS]
    A_d = [dscr(f"A_d{i}", [2, 2 * NK1[i], 128, 512], BF16) for i in range(2)]
    B_d = [dscr(f"B_d{i}", [2, 2 * NK1[i], 128, 512], BF16) for i in range(2)]
    Kf_d = [dscr(f"Kf_d{i}", [NK1[i], 128, 2, 512], F32) for i in range(2)]

    xT_v = xT_d.rearrange("(k p) t -> p k t", p=128)
    oT_v = oT_d.rearrange("(k p) t -> p k t", p=128)

    top = ExitStack()
    with top:
        S = Sched(nc, top)
        uid = [0]

        def sb(st, shape, dt, name=None):
            uid[0] += 1
            return st.enter_context(nc.sbuf_tensor(f"{name or 't'}_{uid[0]}", list(shape), dt))

        def ps(st, shape, dt=F32, name=None):
            uid[0] += 1
            return st.enter_context(nc.psum_tensor(f"{name or 'p'}_{uid[0]}", list(shape), dt))

        V, G, A, P = nc.vector, nc.gpsimd, nc.scalar, nc.tensor

        ident_f = sb(top, [128, 128], F32, "identf")
        ident_b = sb(top, [128, 128], BF16, "identb")
        ones_b = sb(top, [128, 128], BF16, "onesb")
        rt_sb = sb(top, [64, 16], BF16, "rt")
        maskp = sb(top, [128, 128], BF16, "maskp")
        maskn = sb(top, [128, 128], BF16, "maskn")
        scT = sb(top, [128, 8, 4], F32, "scT")
        scT_b = sb(top, [128, 8, 4], BF16, "scTb")
        modsb = sb(top, [128, 72, 4], F32, "modsb")
        adab = sb(top, [128, DEPTH, 72], F32, "adab")
        lng = sb(top, [128, DEPTH * 3, 8], F32, "lng")
        nlng = sb(top, [128, DEPTH * 3, 8], F32, "nlng")
        lnb = sb(top, [128, DEPTH * 3, 8], F32, "lnb")
        cw = sb(top, [128, DEPTH * 3, 12], F32, "cw")
        cbs = sb(top, [128, DEPTH, 12], F32, "cbs")
        nsink = sb(top, [128, DEPTH * 8], F32, "nsink")
        eps_ln = sb(top, [128, 1], F32, "epsln")
        S.dve(lambda: V.memset(eps_ln[:], LN_EPS), [], ['epsln'])

        with nc.allow_non_contiguous_dma(reason="tiny one-time parameter loads"):
            S.dma('sp', ident_f[:], cd["ident_f"], writes=['c0'])
            S.dma('sp', ident_b[:], cd["ident_b"], writes=['c1'])
            S.dma('sp', ones_b[:], cd["ones_b"], writes=['c2'])
            S.dma('sp', rt_sb[:], cd["rt"], writes=['c3'])
            S.dma('sp', maskp[:], cd["maskp"], writes=['c4'])
            S.dma('sp', maskn[:], cd["maskn"], writes=['c5'])
            for s_ in range(4):
                S.dma('sp', scT[:, :, s_], cin[s_].rearrange("(k p) -> p k", p=128), writes=[('scT', s_)])
            for l_ in range(DEPTH):
                S.dma('sp', adab[:, l_, :], ada_b[l_].rearrange("(j p) -> p j", p=128), writes=[('adab', l_)])
                S.dma('sp', cbs[:, l_, :], hy_cb[l_].rearrange("(c p) -> p c", p=128), writes=[('cbs', l_)])
                for m_ in range(3):
                    S.dma('sp', lng[:, l_ * 3 + m_, :], ln_g[l_, m_].rearrange("(k p) -> p k", p=128),
                          writes=[('lng', l_, m_)])
                    S.dma('sp', lnb[:, l_ * 3 + m_, :], ln_b[l_, m_].rearrange("(k p) -> p k", p=128),
                          writes=[('lnb', l_, m_)])
                    S.dma('sp', cw[:, l_ * 3 + m_, :], hy_cw[l_, m_].rearrange("(c p) -> p c", p=128),
                          writes=[('cw', l_, m_)])
            S.dma('sp', nsink[:], sink.rearrange("l h -> (l h)").partition_broadcast(128), writes=['nsink'])
        S.barrier()
        S.act(lambda: A.activation(out=scT_b[:], in_=scT[:], func=AF.Silu), ['scT'], ['scTb'])
        S.dve(lambda: V.tensor_scalar(out=nlng[:], in0=lng[:], scalar1=-1.0, scalar2=None, op0=ALU.mult),
              ['lng'], ['nlng'])
        S.dve(lambda: V.tensor_scalar(out=nsink[:], in0=nsink[:], scalar1=-1.0, scalar2=None, op0=ALU.mult),
              ['nsink'], ['nsink'])
        S.barrier()

        def phase_tin():
            with ExitStack() as st:
                xtok = [sb(st, [128, 4, D], F32, "xtok") for _ in range(2)]
                xo = [sb(st, [128, 8, 512], F32, "xo") for _ in range(2)]
                pt = [ps(st, [128, 512], F32, "pt") for _ in range(3)]
                n = 0
                for g in range(T // 512):
                    b = g % 2
                    S.dma('sp', xtok[b][:], xin[g * 512:(g + 1) * 512, :].rearrange("(j p) d -> p j d", p=128),
                          writes=[('xtok', b)])
                    for k in range(8):
                        pb = n % 3
                        n += 1
                        S.pe([(lambda j=j: P.transpose(pt[pb][:, j * 128:(j + 1) * 128],
                                                        xtok[b][:, j, k * 128:(k + 1) * 128], ident_f[:]))
                              for j in range(4)], [('xtok', b)], [('pt', pb)])
                        if k % 2 == 0:
                            S.act(lambda: A.copy(out=xo[b][:, k, :], in_=pt[pb][:]), [('pt', pb)], [('xo', b, k)])
                        else:
                            S.dve(lambda: V.tensor_copy(out=xo[b][:, k, :], in_=pt[pb][:]), [('pt', pb)], [('xo', b, k)])
                    S.dma('sp', xT_v[:, :, g * 512:(g + 1) * 512], xo[b][:],
                          reads=[('xo', b, k) for k in range(8)])
            S.barrier()

        def phase_tout():
            with ExitStack() as st:
                xo = [sb(st, [128, 8, 512], F32, "xo") for _ in range(2)]
                ytok = [sb(st, [128, 4, D], F32, "ytok") for _ in range(2)]
                pt = [ps(st, [128, 512], F32, "pt") for _ in range(4)]
                n = 0
                for g in range(T // 512):
                    b = g % 2
                    S.dma('sp', xo[b][:], xT_v[:, :, g * 512:(g + 1) * 512], writes=[('xo', b)])
                    for j in range(4):
                        for hf in range(2):
                            pb = n % 4
                            n += 1
                            S.pe([(lambda kk=kk: P.transpose(pt[pb][:, kk * 128:(kk + 1) * 128],
                                                              xo[b][:, hf * 4 + kk, j * 128:(j + 1) * 128], ident_f[:]))
                                  for kk in range(4)], [('xo', b)], [('pt', pb)])
                            if hf == 0:
                                S.act(lambda: A.copy(out=ytok[b][:, j, 0:512], in_=pt[pb][:]),
                                      [('pt', pb)], [('ytok', b, j, 0)])
                            else:
                                S.dve(lambda: V.tensor_copy(out=ytok[b][:, j, 512:1024], in_=pt[pb][:]),
                                      [('pt', pb)], [('ytok', b, j, 1)])
                    S.dma('sp', yout[g * 512:(g + 1) * 512, :].rearrange("(j p) d -> p j d", p=128), ytok[b][:],
                          reads=[('ytok', b, j, h) for j in range(4) for h in range(2)])
            S.barrier()

        def phase_mod(l):
            with ExitStack() as st:
                wt = [sb(st, [128, 8, 1152], BF16, "adaw") for _ in range(2)]
                pm = ps(st, [128, 72, 4], F32, "pm")
                src = ada_w[l].rearrange("(k p) f -> p k f", p=128)
                for cg in range(8):
                    b = cg % 2
                    S.dma('pool', wt[b][:], src[:, :, cg * 1152:(cg + 1) * 1152], writes=[('adaw', b)])
                    for jj in range(9):
                        j = cg * 9 + jj
                        S.pe([(lambda k=k: P.matmul(pm[:, j, :], wt[b][:, k, jj * 128:(jj + 1) * 128], scT_b[:, k, :],
                                                    start=(k == 0), stop=(k == 7))) for k in range(8)],
                             [('adaw', b), 'scTb'], ['pm'])
                allpm = ['pm']
                S.dve(lambda: V.tensor_tensor(out=modsb[:], in0=pm[:],
                                              in1=adab[:, l, :].unsqueeze(2).to_broadcast([128, 72, 4]), op=ALU.add),
                      allpm + ['modsb'], ['modsb'])
                for m, cf in ((1, None), (4, None), (7, None), (2, 0.5 / ALPHA), (5, 1.0 / ALPHA), (8, 0.5 / ALPHA)):
                    sl = modsb[:, m * 8:(m + 1) * 8, :]
                    if cf is None:
                        S.dve(lambda: V.tensor_scalar(out=sl, in0=sl, scalar1=1.0, scalar2=None, op0=ALU.add),
                              ['modsb'], ['modsb'])
                    else:
                        S.dve(lambda: V.tensor_scalar(out=sl, in0=sl, scalar1=1.0, scalar2=cf, op0=ALU.add,
                                                      op1=ALU.mult), ['modsb'], ['modsb'])
            S.barrier()

        def ln_tail(st_objs, l, m, b, xt, rb, rq, pst, tmps, TT, rbk='rb', rqk='rq'):
            mean_sb, var, mr = tmps[0:3]
            S.pe([(lambda k=k: P.matmul(pst[:, 0:TT], ones_b[:], rb[:, k, :], start=(k == 0), stop=(k == 7)))
                  for k in range(8)] +
                 [(lambda k=k: P.matmul(pst[:, TT:2 * TT], ones_b[:], rq[:, k, :], start=(k == 0), stop=(k == 7)))
                  for k in range(8)], [rbk, rqk], ['pst'])
            S.dve(lambda: V.tensor_copy(out=mean_sb[:], in_=pst[:, 0:TT]), ['pst'], ['mean'])
            S.dve(lambda: V.tensor_tensor(out=var[:], in0=mean_sb[:], in1=mean_sb[:], op=ALU.mult), ['mean'], ['var'])
            S.dve(lambda: V.tensor_tensor(out=var[:], in0=pst[:, TT:2 * TT], in1=var[:], op=ALU.subtract),
                  ['pst', 'var'], ['var'])
            S.act(lambda: A.activation(out=var[:], in_=var[:], func=AF.Sqrt, bias=eps_ln[:, 0:1], scale=1.0),
                  ['var'], ['var'])
            S.dve(lambda: V.reciprocal(out=var[:], in_=var[:]), ['var'], ['var'])
            S.dve(lambda: V.tensor_tensor(out=mr[:], in0=mean_sb[:], in1=var[:], op=ALU.mult), ['mean', 'var'], ['mr'])
            li = l * 3 + m
            for dk in range(8):
                x = xt[b][:, dk, :]
                S.dve(lambda: V.scalar_tensor_tensor(out=x, in0=x, scalar=lng[:, li, dk:dk + 1], in1=var[:],
                                                     op0=ALU.mult, op1=ALU.mult), [('xt', b, dk), 'var'], [('xt', b, dk)])
                tq = tmps[3 + dk % 2]
                S.pool(lambda: G.tensor_scalar(out=tq[:], in0=mr[:], scalar1=nlng[:, li, dk:dk + 1],
                                               scalar2=lnb[:, li, dk:dk + 1], op0=ALU.mult, op1=ALU.add),
                       ['mr'], [('tq', dk % 2)])
                S.pool(lambda: G.tensor_tensor(out=x, in0=x, in1=tq[:], op=ALU.add), [('xt', b, dk), ('tq', dk % 2)],
                       [('xt', b, dk)])

        def phase_ffn(l, which):
            TT = 256
            NT = T // TT
            jS, jB, jG = (1, 0, 2) if which == 0 else (7, 6, 8)
            lnm = 0 if which == 0 else 2
            with ExitStack() as st:
                wi = sb(st, [128, 8, 2 * DFF], BF16, "wi")
                wo = sb(st, [128, 22, D], BF16, "wo")
                xt = [sb(st, [128, 8, TT], F32, "xt") for _ in range(3)]
                hT = [sb(st, [128, 8, TT], BF16, "hT") for _ in range(2)]
                aT = sb(st, [128, 22, TT], BF16, "aT")
                rb = sb(st, [128, 8, TT], BF16, "rb")
                rq = sb(st, [128, 8, TT], BF16, "rq")
                sg = [sb(st, [128, TT], BF16, "sg") for _ in range(2)]
                tmps = [sb(st, [128, TT], F32, "lnt") for _ in range(5)]
                gu = [ps(st, [128, 512], F32, "gu") for _ in range(3)]
                pf = [ps(st, [128, 512], F32, "pf") for _ in range(4)]
                pst = ps(st, [128, 512], F32, "pst")
                wsrc = ffn_wi[which][l].rearrange("(k p) f -> p k f", p=128)
                osrc = ffn_wo[which][l].rearrange("(c p) d -> p c d", p=128)
                for cg in range(11):
                    S.dma('pool', wi[:, :, cg * 256:(cg + 1) * 256], wsrc[:, :, cg * 256:(cg + 1) * 256],
                          writes=[('wig', cg)])
                    S.dma('pool', wi[:, :, DFF + cg * 256:DFF + (cg + 1) * 256],
                          wsrc[:, :, DFF + cg * 256:DFF + (cg + 1) * 256], writes=[('wiu', cg)])
                for cg in range(11):
                    S.dma('pool', wo[:, cg * 2:cg * 2 + 2, :], osrc[:, cg * 2:cg * 2 + 2, :], writes=[('wo', cg)])

                def load(t):
                    S.dma('sp', xt[t % 3][:], xT_v[:, :, t * TT:(t + 1) * TT], writes=[('xt', t % 3, k_) for k_ in range(8)])

                def modulate(t):
                    b = t % 2
                    xb = t % 3
                    s = seq_of(t * TT)[0]
                    for k in range(8):
                        S.dve(lambda: V.tensor_scalar(out=hT[b][:, k, :], in0=xt[xb][:, k, :],
                                                      scalar1=modsb[:, jS * 8 + k, s:s + 1],
                                                      scalar2=modsb[:, jB * 8 + k, s:s + 1],
                                                      op0=ALU.mult, op1=ALU.add), [('xt', xb, k)], [('hT', b)])

                def gu_part(t):
                    b = t % 2
                    for c in range(22):
                        g = gu[c % 3]
                        S.pe([(lambda k=k: P.matmul(g[:, 0:TT], wi[:, k, c * 128:(c + 1) * 128], hT[b][:, k, :],
                                                    start=(k == 0), stop=(k == 7))) for k in range(8)] +
                             [(lambda k=k: P.matmul(g[:, TT:2 * TT], wi[:, k, DFF + c * 128:DFF + (c + 1) * 128],
                                                    hT[b][:, k, :], start=(k == 0), stop=(k == 7))) for k in range(8)],
                             [('hT', b), ('wig', c // 2), ('wiu', c // 2)], [('gu', c % 3)])
                        S.act(lambda: A.activation(out=sg[c % 2][:], in_=g[:, 0:TT], func=AF.Silu),
                              [('gu', c % 3)], [('sg', c % 2)])
                        S.dve(lambda: V.tensor_tensor(out=aT[:, c, :], in0=g[:, TT:2 * TT], in1=sg[c % 2][:],
                                                      op=ALU.mult), [('gu', c % 3), ('sg', c % 2)], [('aT', c)])

                def down_part(t):
                    b = t % 3
                    s = seq_of(t * TT)[0]
                    for dp in range(4):
                        bank = pf[dp]
                        S.pe([(lambda c=c, h_=h_: P.matmul(bank[:, h_ * TT:(h_ + 1) * TT],
                                                           wo[:, c, (2 * dp + h_) * 128:(2 * dp + h_ + 1) * 128],
                                                           aT[:, c, :], start=(c == 0), stop=(c == 21)))
                              for h_ in range(2) for c in range(22)],
                             [('aT', c) for c in range(22)] + [('wo', cg) for cg in range(11)], [('pf', dp)])
                        for h_ in range(2):
                            dk = 2 * dp + h_
                            pp = bank[:, h_ * TT:(h_ + 1) * TT]
                            x = xt[b][:, dk, :]
                            S.dve(lambda: V.scalar_tensor_tensor(out=x, in0=pp, scalar=modsb[:, jG * 8 + dk, s:s + 1],
                                                                 in1=x, op0=ALU.mult, op1=ALU.add),
                                  [('pf', dp), ('xt', b, dk)], [('xt', b, dk)])
                            S.pool(lambda: G.tensor_copy(out=rb[:, dk, :], in_=x), [('xt', b, dk)], ['rb'])
                            S.pool(lambda: G.tensor_tensor(out=rq[:, dk, :], in0=x, in1=x, op=ALU.mult),
                                   [('xt', b, dk)], ['rq'])

                def tail_part(t):
                    b = t % 3
                    ln_tail(None, l, lnm, b, xt, rb, rq, pst, tmps, TT)
                    S.dma('sp', xT_v[:, :, t * TT:(t + 1) * TT], xt[b][:], reads=[('xt', b, k_) for k_ in range(8)])

                for t in range(min(3, NT)):
                    load(t)
                modulate(0)
                gu_part(0)
                if NT > 1:
                    modulate(1)
                down_part(0)
                for t in range(NT):
                    if t + 1 < NT:
                        gu_part(t + 1)
                    if t + 2 < NT:
                        modulate(t + 2)
                    tail_part(t)
                    if t + 3 < NT:
                        load(t + 3)
                    if t + 1 < NT:
                        down_part(t + 1)
            S.barrier()

        def phase_wout(l):
            TT = 256
            NT = T // TT
            with ExitStack() as st:
                wo = sb(st, [128, 8, D], BF16, "wout")
                xt = [sb(st, [128, 8, TT], F32, "xt") for _ in range(3)]
                oT = [sb(st, [128, 8, TT], BF16, "oT") for _ in range(3)]
                rb = [sb(st, [128, 8, TT], BF16, "rb") for _ in range(2)]
                rq = [sb(st, [128, 8, TT], BF16, "rq") for _ in range(2)]
                tmps = [sb(st, [128, TT], F32, "lnt") for _ in range(5)]
                pf = [ps(st, [128, 512], F32, "pf") for _ in range(4)]
                pst = ps(st, [128, 512], F32, "pst")
                S.dma('pool', wo[:], w_out[l].rearrange("(k p) d -> p k d", p=128), writes=['wout'])

                def load(t):
                    S.dma('sp', xt[t % 3][:], xT_v[:, :, t * TT:(t + 1) * TT], writes=[('xt', t % 3, k_) for k_ in range(8)])
                    S.dma('sp', oT[t % 3][:], oT_v[:, :, t * TT:(t + 1) * TT], writes=[('oT', t % 3)])

                def mm(t):
                    b = t % 3
                    b2 = t % 2
                    s = seq_of(t * TT)[0]
                    for dp in range(4):
                        bank = pf[dp]
                        S.pe([(lambda k=k, h_=h_: P.matmul(bank[:, h_ * TT:(h_ + 1) * TT],
                                                           wo[:, k, (2 * dp + h_) * 128:(2 * dp + h_ + 1) * 128],
                                                           oT[b][:, k, :], start=(k == 0), stop=(k == 7)))
                              for h_ in range(2) for k in range(8)], [('oT', b), 'wout'], [('pf', dp)])
                        for h_ in range(2):
                            dk = 2 * dp + h_
                            pp = bank[:, h_ * TT:(h_ + 1) * TT]
                            x = xt[b][:, dk, :]
                            S.dve(lambda: V.scalar_tensor_tensor(out=x, in0=pp, scalar=modsb[:, 5 * 8 + dk, s:s + 1],
                                                                 in1=x, op0=ALU.mult, op1=ALU.add),
                                  [('pf', dp), ('xt', b, dk)], [('xt', b, dk)])
                            S.pool(lambda: G.tensor_copy(out=rb[b2][:, dk, :], in_=x), [('xt', b, dk)], [('rb', b2)])
                            S.pool(lambda: G.tensor_tensor(out=rq[b2][:, dk, :], in0=x, in1=x, op=ALU.mult),
                                   [('xt', b, dk)], [('rq', b2)])

                def tail(t):
                    b = t % 3
                    b2 = t % 2
                    ln_tail(None, l, 1, b, xt, rb[b2], rq[b2], pst, tmps, TT, rbk=('rb', b2), rqk=('rq', b2))
                    S.dma('sp', xT_v[:, :, t * TT:(t + 1) * TT], xt[b][:], reads=[('xt', b, k_) for k_ in range(8)])

                for t in range(min(3, NT)):
                    load(t)
                mm(0)
                for t in range(NT):
                    if t + 1 < NT:
                        mm(t + 1)
                    tail(t)
                    if t + 3 < NT:
                        load(t + 3)
            S.barrier()

        def phase_m1(l):
            TT = 512
            NT = T // TT
            with ExitStack() as st:
                win = sb(st, [128, 8, NIN], BF16, "win")
                xt = [sb(st, [128, 8, TT + 2], F32, "xt") for _ in range(2)]
                hT = [sb(st, [128, 8, TT + 2], BF16, "hT") for _ in range(2)]
                qb = [sb(st, [64, TT], BF16, "qb") for _ in range(2)]
                t1 = sb(st, [16, TT], F32, "t1")
                t2 = sb(st, [16, TT], F32, "t2")
                cs = [sb(st, [16, 2, TT], F32, "cs") for _ in range(2)]
                vx = [sb(st, [128, 4, 2, 65], BF16, "vx") for _ in range(2)]
                zs = [sb(st, [128, TT + 2], F32, "zs") for _ in range(2)]
                zc = [sb(st, [128, TT], F32, "zc") for _ in range(2)]
                x0b = sb(st, [128, 4, TT], BF16, "x0b")
                ub = sb(st, [128, 4, TT], BF16, "ub")
                tok0b = [sb(st, [128, 4, 512], BF16, "tok0") for _ in range(2)]
                psq = [ps(st, [64, 512], F32, "psq")] * 2
                psr = ps(st, [16, 512], F32, "psr")
                psv = ps(st, [128, 512], F32, "psv")
                psz = [ps(st, [128, 512], F32, "psz") for _ in range(2)]
                pszh = [ps(st, [128, 16], F32, "pszh") for _ in range(2)]
                ptr = ps(st, [128, 512], BF16, "ptr")
                S.dma('pool', win[:, :, 0:768], w_in[l].rearrange("(k p) f -> p k f", p=128)[:, :, 0:768],
                      writes=['win0'])
                for i in range(3):
                    S.dma('pool', win[:, :, 768 + i * 512:768 + (i + 1) * 512],
                          w_in[l].rearrange("(k p) f -> p k f", p=128)[:, :, 768 + i * 512:768 + (i + 1) * 512],
                          writes=[('win', i)])
                for b in range(2):
                    S.pool(lambda: G.memset(vx[b][:], 1.0), [], [('vx', b)])
                wkeys = ['win0'] + [('win', i) for i in range(3)]

                def load(g):
                    b = g % 2
                    tok0 = g * TT
                    s, s0, ln = seq_of(tok0)
                    lo = 1 if tok0 == s0 else 0
                    hi = 1 if tok0 + TT == s0 + ln else 0
                    S.dma('sp', xt[b][:, :, lo:TT + 2 - hi], xT_v[:, :, tok0 - 1 + lo:tok0 + TT + 1 - hi],
                          writes=[('xt', b)])
                    S.dma('sp', cs[b][:, 0, :], cd["cos_t"][:, tok0 - s0:tok0 - s0 + TT], writes=[('cs', b, 0)])
                    S.dma('sp', cs[b][:, 1, :], cd["sin_t"][:, tok0 - s0:tok0 - s0 + TT], writes=[('cs', b, 1)])

                def modulate(g):
                    b = g % 2
                    tok0 = g * TT
                    s, s0, ln = seq_of(tok0)
                    lo = 1 if tok0 == s0 else 0
                    hi = 1 if tok0 + TT == s0 + ln else 0
                    for k in range(8):
                        S.dve(lambda: V.tensor_scalar(out=hT[b][:, k, lo:TT + 2 - hi], in0=xt[b][:, k, lo:TT + 2 - hi],
                                                      scalar1=modsb[:, 4 * 8 + k, s:s + 1],
                                                      scalar2=modsb[:, 3 * 8 + k, s:s + 1],
                                                      op0=ALU.mult, op1=ALU.add), [('xt', b)], [('hT', b)])
                    if lo:
                        S.dve(lambda: V.memset(hT[b][:, :, 0:1], 0.0), [], [('hT', b)])
                    if hi:
                        S.dve(lambda: V.memset(hT[b][:, :, TT + 1:TT + 2], 0.0), [], [('hT', b)])

                load(0)
                modulate(0)
                nq = 0
                nz = 0
                for g in range(NT):
                    b = g % 2
                    tok0 = g * TT
                    if g + 1 < NT:
                        load(g + 1)
                    for hd in range(10):
                        qi = nq % 2
                        nq += 1
                        S.pe([(lambda k=k: P.matmul(psq[qi][:], win[:, k, hd * 64:(hd + 1) * 64], hT[b][:, k, 1:TT + 1],
                                                    start=(k == 0), stop=(k == 7))) for k in range(8)],
                             [('hT', b), 'win0'], ['psq'])
                        S.act(lambda: A.copy(out=qb[qi][:], in_=psq[qi][:]), ['psq'], [('qb', qi)])
                        S.pe([lambda: P.matmul(psr[:], rt_sb[:], qb[qi][:], start=True, stop=True)],
                             [('qb', qi)], ['psr'])
                        S.dve(lambda: V.tensor_tensor(out=t1[:], in0=psq[qi][0:16, :], in1=cs[b][:, 0, :], op=ALU.mult),
                              ['psq', ('cs', b, 0)], ['t1'])
                        S.dve(lambda: V.tensor_tensor(out=t2[:], in0=psr[:], in1=cs[b][:, 1, :], op=ALU.mult),
                              ['psr', ('cs', b, 1)], ['t2'])
                        S.pool(lambda: G.tensor_tensor(out=qb[qi][0:16, :], in0=t1[:], in1=t2[:], op=ALU.add),
                               ['t1', 't2', ('qb', qi)], [('qb', qi)])
                        S.dma('sp', q_d[hd, :, tok0:tok0 + TT], qb[qi][:], reads=[('qb', qi)])
                    for blk in range(4):
                        S.pe([(lambda k=k: P.matmul(psv[:, blk * 128:(blk + 1) * 128],
                                                    hT[b][:, k, 1 + blk * 128:1 + (blk + 1) * 128],
                                                    win[:, k, 640:768], start=(k == 0), stop=(k == 7)))
                              for k in range(8)], [('hT', b), 'win0'], ['psv'])
                    S.act(lambda: A.copy(out=vx[b][:, :, :, 0:64],
                                         in_=psv[:].rearrange("p (b k e) -> p b k e", b=4, k=2)),
                          ['psv', ('vx', b)], [('vx', b)])
                    S.dma('sp', v_d[tok0:tok0 + TT, :].rearrange("(b p) e -> p b e", p=128),
                          vx[b][:].rearrange("p b k e -> p b (k e)"), reads=[('vx', b)])
                    for i in range(4):
                        for part in range(3):
                            cz = part * 4 + i
                            zi = nz % 2
                            nz += 1
                            c0 = 768 + cz * 128
                            S.pe([(lambda k=k: P.matmul(psz[zi][:], win[:, k, c0:c0 + 128], hT[b][:, k, 1:TT + 1],
                                                        start=(k == 0), stop=(k == 7))) for k in range(8)] +
                                 [(lambda k=k: P.matmul(pszh[zi][:, 0:2], win[:, k, c0:c0 + 128],
                                                        hT[b][:, k, 0:TT + 2:TT + 1],
                                                        start=(k == 0), stop=(k == 7))) for k in range(8)],
                                 [('hT', b)] + wkeys, [('psz', zi)])
                            S.act(lambda: A.copy(out=zs[zi][:, 1:TT + 1], in_=psz[zi][:]), [('psz', zi)], [('zs', zi)])
                            S.act(lambda: A.copy(out=zs[zi][:, 0:TT + 2:TT + 1], in_=pszh[zi][:, 0:2]),
                                  [('psz', zi)], [('zs', zi)])
                            zz = zc[part % 2] if part > 0 else zc[0]
                            zkey = ('zc', part % 2 if part > 0 else 0)
                            S.pool(lambda: G.tensor_scalar(out=zz[:], in0=zs[zi][:, 1:TT + 1],
                                                           scalar1=cw[:, l * 3 + 1, cz:cz + 1],
                                                           scalar2=cbs[:, l, cz:cz + 1], op0=ALU.mult, op1=ALU.add),
                                   [('zs', zi)], [zkey])
                            S.dve(lambda: V.scalar_tensor_tensor(out=zz[:], in0=zs[zi][:, 0:TT],
                                                                  scalar=cw[:, l * 3 + 0, cz:cz + 1], in1=zz[:],
                                                                  op0=ALU.mult, op1=ALU.add), [('zs', zi), zkey], [zkey])
                            if part == 0:
                                S.dve(lambda: V.scalar_tensor_tensor(out=x0b[:, i, :], in0=zs[zi][:, 2:TT + 2],
                                                                     scalar=cw[:, l * 3 + 2, cz:cz + 1], in1=zz[:],
                                                                     op0=ALU.mult, op1=ALU.add),
                                      [('zs', zi), zkey], [('x0b', i)])
                            else:
                                S.dve(lambda: V.scalar_tensor_tensor(out=zz[:], in0=zs[zi][:, 2:TT + 2],
                                                                     scalar=cw[:, l * 3 + 2, cz:cz + 1], in1=zz[:],
                                                                     op0=ALU.mult, op1=ALU.add),
                                      [('zs', zi), zkey], [zkey])
                        S.pool(lambda: G.tensor_tensor(out=ub[:, i, :], in0=zc[0][:], in1=zc[1][:], op=ALU.mult),
                               [('zc', 0), ('zc', 1)], [('ub', i)])
                    for ti, (srcb, dst, key) in enumerate(((x0b, x0_d, 'x0b'), (ub, u_d, 'ub'))):
                        tb = tok0b[ti]
                        for blk in range(4):
                            S.pe([(lambda i=i: P.transpose(ptr[:, i * 128:(i + 1) * 128],
                                                           srcb[:, i, blk * 128:(blk + 1) * 128], ident_b[:]))
                                  for i in range(4)], [(key, i) for i in range(4)], ['ptr'])
                            S.act(lambda: A.copy(out=tb[:, blk, :], in_=ptr[:]), ['ptr'], [('tokb', ti)])
                        S.dma('sp', dst[tok0:tok0 + TT, :].rearrange("(b p) c -> p b c", p=128), tb[:],
                              reads=[('tokb', ti)])
                    if g + 1 < NT:
                        modulate(g + 1)
            S.barrier()

        def phase_attn(l):
            TT = 512
            NT = T // TT
            with ExitStack() as st:
                qT = [sb(st, [64, 8, TT], BF16, "qT") for _ in range(2)]
                kT = [sb(st, [64, 2, TT + 256], BF16, "kT") for _ in range(2)]
                vv = [sb(st, [128, 6, 130], BF16, "vv") for _ in range(2)]
                pT = [sb(st, [128, 3, 128], BF16, "pT") for _ in range(3)]
                gA = sb(st, [128, 512], F32, "gA")
                den = sb(st, [128, 8], F32, "den")
                osb = sb(st, [128, 8, 64], F32, "osb")
                osq = sb(st, [128, 512], F32, "osq")
                ssq = sb(st, [128, 1], F32, "ssq")
                onb = sb(st, [128, 512], BF16, "onb")
                oTs = [sb(st, [128, 4, TT], BF16, "oTs") for _ in range(2)]
                pS = [ps(st, [128, 3, 128], F32, "pS") for _ in range(3)]
                pO = [ps(st, [128, 4, 65], F32, "pO") for _ in range(4)]
                ptr = ps(st, [128, 512], BF16, "ptr")
                S.dma('sp', gA[:], grp_g[l, 0:512].partition_broadcast(128), writes=['gA'])

                def load(g):
                    b = g % 2
                    tok0 = g * TT
                    s, s0, ln = seq_of(tok0)
                    lo = 128 if tok0 == s0 else 0
                    hi = 128 if tok0 + TT == s0 + ln else 0
                    S.dma('sp', qT[b][:], q_d[0:8, :, tok0:tok0 + TT].rearrange("h p t -> p h t"), writes=[('qT', b)])
                    S.dma('sp', kT[b][:, :, lo:TT + 256 - hi],
                          q_d[8:10, :, tok0 - 128 + lo:tok0 + TT + 128 - hi].rearrange("h p t -> p h t"),
                          writes=[('kT', b)])
                    S.dma('sp', vv[b][:, lo // 128:6 - hi // 128, :],
                          v_d[tok0 - 128 + lo:tok0 + TT + 128 - hi, :].rearrange("(b p) e -> p b e", p=128),
                          writes=[('vv', b)])

                def emit_norm(b, qb_, po, pok, tok0, do_store):
                    hk = [pok + (0,), pok + (1,)]
                    for hh in range(2):
                        S.dve(lambda: V.tensor_scalar(out=den[:, hh * 4:(hh + 1) * 4], in0=po[hh][:, :, 64],
                                                      scalar1=1.0, scalar2=None, op0=ALU.add), hk + ['den'], ['den'])
                    S.dve(lambda: V.reciprocal(out=den[:], in_=den[:]), ['den'], ['den'])
                    for hh in range(2):
                        S.dve(lambda: V.tensor_tensor(out=osb[:, hh * 4:(hh + 1) * 4, :], in0=po[hh][:, :, 0:64],
                                                      in1=den[:, hh * 4:(hh + 1) * 4].unsqueeze(2).to_broadcast(
                                                          [128, 4, 64]), op=ALU.mult),
                              hk + ['den', 'osb'], ['osb'])
                    of = osb[:].rearrange("p h e -> p (h e)")
                    S.pool(lambda: G.tensor_tensor(out=osq[:], in0=of, in1=of, op=ALU.mult), ['osb'], ['osq'])
                    S.dve(lambda: V.reduce_sum(out=ssq[:], in_=osq[:], axis=AX.X), ['osq'], ['ssq'])
                    S.dve(lambda: V.tensor_scalar(out=ssq[:], in0=ssq[:], scalar1=1.0 / 512, scalar2=RMS_EPS,
                                                  op0=ALU.mult, op1=ALU.add), ['ssq'], ['ssq'])
                    S.act(lambda: A.activation(out=ssq[:], in_=ssq[:], func=AF.Ln), ['ssq'], ['ssq'])
                    S.act(lambda: A.activation(out=ssq[:], in_=ssq[:], func=AF.Exp, scale=-0.5), ['ssq'], ['ssq'])
                    S.dve(lambda: V.scalar_tensor_tensor(out=onb[:], in0=of, scalar=ssq[:, 0:1], in1=gA[:],
                                                         op0=ALU.mult, op1=ALU.mult),
                          ['osb', 'ssq', 'gA', 'onb'], ['onb'])
                    S.pe([(lambda c=c: P.transpose(ptr[:, c * 128:(c + 1) * 128], onb[:, c * 128:(c + 1) * 128],
                                                   ident_b[:])) for c in range(4)], ['onb'], ['ptr'])
                    S.act(lambda: A.copy(out=oTs[b][:, :, qb_ * 128:(qb_ + 1) * 128],
                                         in_=ptr[:].rearrange("p (c t) -> p c t", c=4)), ['ptr'], [('oTs', b)])
                    if do_store:
                        S.dma('sp', oT_v[:, 0:4, tok0:tok0 + TT], oTs[b][:], reads=[('oTs', b)])

                jobs = []
                npi = 0
                npo = 0
                for g in range(NT):
                    b = g % 2
                    tok0 = g * TT
                    s, s0, ln = seq_of(tok0)
                    for qb_ in range(4):
                        gpos = tok0 + qb_ * 128
                        jlo = 1 if gpos == s0 else 0
                        jhi = 2 if gpos + 128 == s0 + ln else 3
                        po = (pO[(npo % 2) * 2], pO[(npo % 2) * 2 + 1])
                        pok = ('pO', npo % 2)
                        npo += 1
                        for h in range(8):
                            jobs.append(dict(g=g, b=b, tok0=tok0, qb=qb_, h=h, kv=h // 4, pi=npi % 3, jlo=jlo, jhi=jhi,
                                             po=po, pok=pok))
                            npi += 1

                def scores(j):
                    b, qb_, h, kv, pi, jlo, jhi = j['b'], j['qb'], j['h'], j['kv'], j['pi'], j['jlo'], j['jhi']
                    S.pe([(lambda jj=jj: P.matmul(pS[pi][:, jj, :],
                                                  kT[b][:, kv, (qb_ + jj) * 128:(qb_ + jj + 1) * 128],
                                                  qT[b][:, h, qb_ * 128:(qb_ + 1) * 128], start=True, stop=True))
                          for jj in range(jlo, jhi)], [('qT', b), ('kT', b)], [('pS', pi)])

                def softmax_pv(j):
                    b, qb_, h, kv, pi, jlo, jhi = j['b'], j['qb'], j['h'], j['kv'], j['pi'], j['jlo'], j['jhi']
                    po, pok = j['po'], j['pok']
                    S.act(lambda: A.activation(out=pT[pi][:, jlo:jhi, :], in_=pS[pi][:, jlo:jhi, :], func=AF.Exp,
                                               bias=nsink[:, l * 8 + h:l * 8 + h + 1], scale=0.125),
                          [('pS', pi)], [('pT', pi)])
                    if jlo == 0:
                        S.pool(lambda: G.tensor_tensor(out=pT[pi][:, 0, :], in0=pT[pi][:, 0, :], in1=maskp[:],
                                                       op=ALU.mult), [('pT', pi)], [('pT', pi)])
                    if jhi == 3:
                        S.pool(lambda: G.tensor_tensor(out=pT[pi][:, 2, :], in0=pT[pi][:, 2, :], in1=maskn[:],
                                                       op=ALU.mult), [('pT', pi)], [('pT', pi)])
                    S.pe([(lambda jj=jj: P.matmul(po[h // 4][:, h % 4, :], pT[pi][:, jj, :],
                                                  vv[b][:, qb_ + jj, kv * 65:(kv + 1) * 65],
                                                  start=(jj == jlo), stop=(jj == jhi - 1)))
                          for jj in range(jlo, jhi)], [('pT', pi), ('vv', b)], [pok + (h // 4,)])

                load(0)
                if NT > 1:
                    load(1)
                pending = None
                scores(jobs[0])
                for i, j in enumerate(jobs):
                    if i + 1 < len(jobs):
                        scores(jobs[i + 1])
                    softmax_pv(j)
                    if j['h'] == 7:
                        if pending is not None:
                            emit_norm(*pending)
                        pending = (j['b'], j['qb'], j['po'], j['pok'], j['tok0'], j['qb'] == 3)
                        if j['qb'] == 3 and j['g'] + 2 < NT:
                            load(j['g'] + 2)
                if pending is not None:
                    emit_norm(*pending)
            S.barrier()

        def fft_stage_a(st, li, srcs, dstA, tag):
            L = LS[li]
            H1 = L // 128
            nk = 2 * NK1[li]
            fa = sb(st, [H1, nk], BF16, "fa")
            S.dma('sp', fa[:], cd[f"fa{li}"], writes=[(tag, 'fa')])
            xc = [sb(st, [H1, 16, 512], BF16, "xc") for _ in range(2)]
            asb = [sb(st, [nk, 16, 512], BF16, "asb") for _ in range(2)]
            pa = [ps(st, [nk, 512], F32, "pa") for _ in range(3)]
            n = 0
            it = 0
            for sq in range(2):
                sv = srcs[sq].rearrange("(a b) c -> a b c", b=128)
                for ch in range(8):
                    bb = it % 2
                    it += 1
                    S.dma('sp', xc[bb][:], sv[:, ch * 16:(ch + 1) * 16, :], writes=[(tag, 'xc', bb)])
                    for j in range(16):
                        pi = n % 3
                        n += 1
                        S.pe([lambda: P.matmul(pa[pi][:], fa[:], xc[bb][:, j, :], start=True, stop=True)],
                             [(tag, 'fa'), (tag, 'xc', bb)], [(tag, 'pa', pi)])
                        if j % 2 == 0:
                            S.act(lambda: A.copy(out=asb[bb][:, j, :], in_=pa[pi][:]), [(tag, 'pa', pi)],
                                  [(tag, 'asb', bb, j)])
                        else:
                            S.dve(lambda: V.tensor_copy(out=asb[bb][:, j, :], in_=pa[pi][:]), [(tag, 'pa', pi)],
                                  [(tag, 'asb', bb, j)])
                    S.dma('sp', dstA[sq, :, ch * 16:(ch + 1) * 16, :], asb[bb][:],
                          reads=[(tag, 'asb', bb, j) for j in range(16)])

        def fft_stage_c(st, li, srcA, mode, dstB, tag):
            nk1 = NK1[li]
            mk = [sb(st, [128, 6, 128], BF16, "mk") for _ in range(2)]
            ak = [sb(st, [128, 2, 512], BF16, "ak") for _ in range(2)]
            kf = [sb(st, [128, 2, 512], F32, "kf") for _ in range(2)]
            tt = [sb(st, [128, 512], F32, "tt") for _ in range(8)]
            yb = [sb(st, [128, 2, 512], BF16, "yb") for _ in range(2)]
            bsb = [sb(st, [128, 2, 512], BF16, "bsb") for _ in range(2)]
            px = [ps(st, [128, 512], F32, "px") for _ in range(4)]
            pb = [ps(st, [128, 512], F32, "pb") for _ in range(4)]
            jobs = [(k1, sq) for k1 in range(nk1) for sq in range(2)]

            def xpart(i):
                k1, sq = jobs[i]
                mb = k1 % 2
                ab = i % 2
                if sq == 0:
                    S.dma('sp', mk[mb][:], cd[f"mm{li}"][k1], writes=[(tag, 'mk', mb)])
                    if mode == 'conv':
                        S.dma('sp', kf[mb][:], Kf_d[li][k1], writes=[(tag, 'kf', mb)])
                av = srcA[sq].rearrange("(r k) n c -> n r k c", r=2)
                S.dma('sp', ak[ab][:], av[:, :, k1, :], writes=[(tag, 'ak', ab)])
                pr, pi_ = px[ab * 2], px[ab * 2 + 1]
                S.pe([lambda: P.matmul(pr[:], mk[mb][:, 0, :], ak[ab][:, 0, :], start=True, stop=False),
                      lambda: P.matmul(pr[:], mk[mb][:, 2, :], ak[ab][:, 1, :], start=False, stop=True),
                      lambda: P.matmul(pi_[:], mk[mb][:, 1, :], ak[ab][:, 0, :], start=True, stop=False),
                      lambda: P.matmul(pi_[:], mk[mb][:, 0, :], ak[ab][:, 1, :], start=False, stop=True)],
                     [(tag, 'mk', mb), (tag, 'ak', ab)], [(tag, 'px', ab)])

            def ypart(i):
                k1, sq = jobs[i]
                mb = k1 % 2
                ab = i % 2
                pr, pi_ = px[ab * 2], px[ab * 2 + 1]
                kk = kf[mb]
                if mode == 'filter':
                    if sq == 0:
                        S.act(lambda: A.copy(out=kk[:, 0, :], in_=pr[:]), [(tag, 'px', ab)], [(tag, 'kf', mb)])
                        S.dve(lambda: V.tensor_copy(out=kk[:, 1, :], in_=pi_[:]), [(tag, 'px', ab)],
                              [(tag, 'kf', mb)])
                    else:
                        S.dve(lambda: V.tensor_tensor(out=kk[:, 0, :], in0=pr[:], in1=kk[:, 0, :], op=ALU.add),
                              [(tag, 'px', ab), (tag, 'kf', mb)], [(tag, 'kf', mb)])
                        S.dve(lambda: V.scalar_tensor_tensor(out=kk[:, 1, :], in0=pi_[:], scalar=-1.0,
                                                             in1=kk[:, 1, :], op0=ALU.mult, op1=ALU.add),
                              [(tag, 'px', ab), (tag, 'kf', mb)], [(tag, 'kf', mb)])
                        S.dma('sp', Kf_d[li][k1], kk[:], reads=[(tag, 'kf', mb)])
                    return
                S.dve(lambda: V.tensor_tensor(out=tt[ab * 4 + 0][:], in0=pr[:], in1=kk[:, 0, :], op=ALU.mult),
                      [(tag, 'px', ab), (tag, 'kf', mb)], [(tag, 'tt', ab * 4 + 0)])
                S.dve(lambda: V.tensor_tensor(out=tt[ab * 4 + 1][:], in0=pi_[:], in1=kk[:, 1, :], op=ALU.mult),
                      [(tag, 'px', ab), (tag, 'kf', mb)], [(tag, 'tt', ab * 4 + 1)])
                S.pool(lambda: G.tensor_tensor(out=yb[ab][:, 0, :], in0=tt[ab * 4 + 0][:], in1=tt[ab * 4 + 1][:],
                                               op=ALU.subtract),
                       [(tag, 'tt', ab * 4 + 0), (tag, 'tt', ab * 4 + 1)], [(tag, 'yb', ab)])
                S.dve(lambda: V.tensor_tensor(out=tt[ab * 4 + 2][:], in0=pr[:], in1=kk[:, 1, :], op=ALU.mult),
                      [(tag, 'px', ab), (tag, 'kf', mb)], [(tag, 'tt', ab * 4 + 2)])
                S.dve(lambda: V.tensor_tensor(out=tt[ab * 4 + 3][:], in0=pi_[:], in1=kk[:, 0, :], op=ALU.mult),
                      [(tag, 'px', ab), (tag, 'kf', mb)], [(tag, 'tt', ab * 4 + 3)])
                S.pool(lambda: G.tensor_tensor(out=yb[ab][:, 1, :], in0=tt[ab * 4 + 2][:], in1=tt[ab * 4 + 3][:],
                                               op=ALU.add),
                       [(tag, 'tt', ab * 4 + 2), (tag, 'tt', ab * 4 + 3)], [(tag, 'yb', ab)])
                br, bi = pb[ab * 2], pb[ab * 2 + 1]
                S.pe([lambda: P.matmul(br[:], mk[mb][:, 3, :], yb[ab][:, 0, :], start=True, stop=False),
                      lambda: P.matmul(br[:], mk[mb][:, 4, :], yb[ab][:, 1, :], start=False, stop=True),
                      lambda: P.matmul(bi[:], mk[mb][:, 3, :], yb[ab][:, 1, :], start=True, stop=False),
                      lambda: P.matmul(bi[:], mk[mb][:, 5, :], yb[ab][:, 0, :], start=False, stop=True)],
                     [(tag, 'mk', mb), (tag, 'yb', ab)], [(tag, 'pb', ab)])
                S.act(lambda: A.copy(out=bsb[ab][:, 0, :], in_=br[:]), [(tag, 'pb', ab)], [(tag, 'bsb', ab, 0)])
                S.act(lambda: A.copy(out=bsb[ab][:, 1, :], in_=bi[:]), [(tag, 'pb', ab)], [(tag, 'bsb', ab, 1)])
                bv = dstB[sq].rearrange("(r k) n c -> n r k c", r=2)
                S.dma('sp', bv[:, :, k1, :], bsb[ab][:], reads=[(tag, 'bsb', ab, 0), (tag, 'bsb', ab, 1)])

            xpart(0)
            for i in range(len(jobs)):
                if i + 1 < len(jobs):
                    xpart(i + 1)
                ypart(i)

        def fft_stage_ah(st, li, srcB, dsts, tag):
            L = LS[li]
            H1 = L // 128
            nk = 2 * NK1[li]
            ga = sb(st, [nk, H1], BF16, "ga")
            S.dma('sp', ga[:], cd[f"ga{li}"], writes=[(tag, 'ga')])
            bc = [sb(st, [nk, 16, 512], BF16, "bc") for _ in range(2)]
            ysb = [sb(st, [H1, 16, 512], F32, "ysb") for _ in range(2)]
            py = [ps(st, [H1, 512], F32, "py") for _ in range(3)]
            n = 0
            it = 0
            for sq in range(2):
                dv = dsts[sq].rearrange("(a b) c -> a b c", b=128)
                for ch in range(8):
                    bb = it % 2
                    it += 1
                    S.dma('sp', bc[bb][:], srcB[sq, :, ch * 16:(ch + 1) * 16, :], writes=[(tag, 'bc', bb)])
                    for j in range(16):
                        pi = n % 3
                        n += 1
                        S.pe([lambda: P.matmul(py[pi][:], ga[:], bc[bb][:, j, :], start=True, stop=True)],
                             [(tag, 'ga'), (tag, 'bc', bb)], [(tag, 'py', pi)])
                        if j % 2 == 0:
                            S.act(lambda: A.mul(out=ysb[bb][:, j, :], in_=py[pi][:], mul=1.0 / (2 * L)),
                                  [(tag, 'py', pi)], [(tag, 'ysb', bb, j)])
                        else:
                            S.dve(lambda: V.tensor_scalar(out=ysb[bb][:, j, :], in0=py[pi][:], scalar1=1.0 / (2 * L),
                                                          scalar2=None, op0=ALU.mult),
                                  [(tag, 'py', pi)], [(tag, 'ysb', bb, j)])
                    S.dma('sp', dv[:, ch * 16:(ch + 1) * 16, :], ysb[bb][:],
                          reads=[(tag, 'ysb', bb, j) for j in range(16)])

        def phase_filter(l, li):
            L = LS[li]
            with ExitStack() as st:
                w1 = sb(st, [33, 64], F32, "w1")
                w2 = sb(st, [64, 64], F32, "w2")
                w3 = sb(st, [64, 64], F32, "w3")
                w4 = sb(st, [64, 1024], F32, "w4")
                fr = sb(st, [64, 1], F32, "fr")
                bb_ = sb(st, [64, 3], F32, "bb")
                dec = sb(st, [128, 1024], F32, "dec")
                negt = sb(st, [128, L // 128], F32, "negt")
                zt = [sb(st, [33, 512], F32, "zt") for _ in range(2)]
                hh = [sb(st, [64, 512], F32, "hh") for _ in range(3)]
                arg = sb(st, [64, 512], F32, "arg")
                kq = sb(st, [64, 512], F32, "kq")
                win_ = sb(st, [128, 1024], F32, "winw")
                fsb = [sb(st, [128, 1024], BF16, "fsb") for _ in range(2)]
                pm = [ps(st, [64, 512], F32, "pm") for _ in range(2)]
                pf = [ps(st, [128, 512], F32, "pf") for _ in range(4)]
                with nc.allow_non_contiguous_dma(reason="tiny filter params"):
                    S.dma('sp', w1[:], hy_w1[l], writes=['fw'])
                    S.dma('sp', w2[:], hy_w2[l], writes=['fw'])
                    S.dma('sp', w3[:], hy_w3[l], writes=['fw'])
                    S.dma('sp', w4[:], hy_w4[l], writes=['fw'])
                    S.dma('sp', fr[:], hy_freq[l].rearrange("(p o) -> p o", o=1), writes=['fr'])
                    S.dma('sp', bb_[:, 0:1], hy_b1[l].rearrange("(p o) -> p o", o=1), writes=['fb'])
                    S.dma('sp', bb_[:, 1:2], hy_b2[l].rearrange("(p o) -> p o", o=1), writes=['fb'])
                    S.dma('sp', bb_[:, 2:3], hy_b3[l].rearrange("(p o) -> p o", o=1), writes=['fb'])
                    S.dma('sp', dec[:], hy_decay[l].rearrange("a c -> (a c)").partition_broadcast(128), writes=['dec'])
                    S.dma('sp', negt[:], cd[f"negt{li}"], writes=['negt'])
                S.barrier()
                S.dve(lambda: V.tensor_scalar(out=bb_[:], in0=bb_[:], scalar1=fr[:, 0:1], scalar2=None, op0=ALU.mult),
                      ['fb', 'fr'], ['fb'])
                S.dve(lambda: V.scalar_tensor_tensor(out=dec[:], in0=dec[:], scalar=-1.0, in1=dec[:], op0=ALU.mult,
                                                     op1=ALU.max), ['dec'], ['dec'])
                ws = [w1, w2, w3]
                np_ = 0
                for ch in range(L // 512):
                    zb = ch % 2
                    S.dma('sp', zt[zb][:], cd[f"zt{li}"][:, ch * 512:(ch + 1) * 512], writes=[('zt', zb)])
                    cur = zt[zb]
                    ckey = ('zt', zb)
                    kdim = 33
                    for ly in range(3):
                        pi = np_ % 2
                        np_ += 1
                        S.pe([lambda: P.matmul(pm[pi][:], ws[ly][0:kdim, :], cur[0:kdim, :], start=True, stop=True)],
                             [ckey, 'fw'], [('pm', pi)])
                        S.dve(lambda: V.tensor_scalar(out=arg[:], in0=pm[pi][:], scalar1=fr[:, 0:1],
                                                      scalar2=bb_[:, ly:ly + 1], op0=ALU.mult, op1=ALU.add),
                              [('pm', pi), 'fb', 'fr'], ['arg'])
                        S.dve(lambda: V.tensor_scalar(out=kq[:], in0=arg[:], scalar1=1.0 / (2 * PI), scalar2=12582912.0,
                                                      op0=ALU.mult, op1=ALU.add), ['arg'], ['kq'])
                        S.dve(lambda: V.tensor_scalar(out=kq[:], in0=kq[:], scalar1=-12582912.0, scalar2=None,
                                                      op0=ALU.add), ['kq'], ['kq'])
                        S.dve(lambda: V.scalar_tensor_tensor(out=arg[:], in0=kq[:], scalar=-2 * PI, in1=arg[:],
                                                             op0=ALU.mult, op1=ALU.add), ['kq', 'arg'], ['arg'])
                        S.dve(lambda: V.tensor_scalar(out=arg[:], in0=arg[:], scalar1=PI, scalar2=-PI,
                                                      op0=ALU.min, op1=ALU.max), ['arg'], ['arg'])
                        S.act(lambda: A.activation(out=hh[ly][:], in_=arg[:], func=AF.Sin), ['arg'], [('hh', ly)])
                        cur = hh[ly]
                        ckey = ('hh', ly)
                        kdim = 64
                    for blk in range(4):
                        gb = ch * 4 + blk
                        fb = gb % 2
                        p0, p1 = pf[(gb % 2) * 2], pf[(gb % 2) * 2 + 1]
                        S.pe([lambda: P.matmul(p0[:], hh[2][:, blk * 128:(blk + 1) * 128], w4[:, 0:512],
                                               start=True, stop=True),
                              lambda: P.matmul(p1[:], hh[2][:, blk * 128:(blk + 1) * 128], w4[:, 512:1024],
                                               start=True, stop=True)], [('hh', 2), 'fw'], [('pf', gb % 2)])
                        S.act(lambda: A.activation(out=win_[:], in_=dec[:], func=AF.Exp, scale=negt[:, gb:gb + 1]),
                              ['dec', 'negt', 'winw'], ['winw'])
                        S.dve(lambda: V.tensor_tensor(out=fsb[fb][:, 0:512], in0=p0[:], in1=win_[:, 0:512], op=ALU.mult),
                              [('pf', gb % 2), 'winw'], [('fsb', fb, 0)])
                        S.dve(lambda: V.tensor_tensor(out=fsb[fb][:, 512:1024], in0=p1[:], in1=win_[:, 512:1024],
                                                      op=ALU.mult), [('pf', gb % 2), 'winw'], [('fsb', fb, 1)])
                        S.dma('sp', filt_d[li][0, gb * 128:(gb + 1) * 128, :], fsb[fb][:, 0:512],
                              reads=[('fsb', fb, 0)])
                        S.dma('sp', filt_d[li][1, gb * 128:(gb + 1) * 128, :], fsb[fb][:, 512:1024],
                              reads=[('fsb', fb, 1)])
            S.barrier()

        def phase_hyena(l):
            for li in range(2):
                phase_filter(l, li)
                with ExitStack() as st:
                    fft_stage_a(st, li, [filt_d[li][0], filt_d[li][1]], A_d[li], 'fa')
                S.barrier()
                with ExitStack() as st:
                    fft_stage_c(st, li, A_d[li], 'filter', None, 'fc')
                S.barrier()
                s0 = SEQS[li * 2][0]
                s1 = SEQS[li * 2 + 1][0]
                L = LS[li]
                with ExitStack() as st:
                    fft_stage_a(st, li, [u_d[s0:s0 + L, :], u_d[s1:s1 + L, :]], A_d[li], 'ua')
                S.barrier()
                with ExitStack() as st:
                    fft_stage_c(st, li, A_d[li], 'conv', B_d[li], 'uc')
                S.barrier()
                with ExitStack() as st:
                    fft_stage_ah(st, li, B_d[li], [y_d[s0:s0 + L, :], y_d[s1:s1 + L, :]], 'uh')
                S.barrier()
            with ExitStack() as st:
                TT = 512
                gH = sb(st, [128, 512], F32, "gH")
                hb = sb(st, [128, 512], F32, "hb")
                yt = [sb(st, [128, 4, 512], F32, "yt") for _ in range(2)]
                ut = [sb(st, [128, 4, 512], BF16, "ut") for _ in range(2)]
                x0t = [sb(st, [128, 4, 512], BF16, "x0t") for _ in range(2)]
                o1 = sb(st, [128, 512], F32, "o1")
                osq = sb(st, [128, 512], F32, "osq")
                ssq = sb(st, [128, 1], F32, "ssq")
                onb = sb(st, [128, 512], BF16, "onb")
                oTs = [sb(st, [128, 4, TT], BF16, "oTs") for _ in range(2)]
                ptr = ps(st, [128, 512], BF16, "ptr")
                S.dma('sp', gH[:], grp_g[l, 512:1024].partition_broadcast(128), writes=['gH'])
                S.dma('sp', hb[:], hy_bias[l].partition_broadcast(128), writes=['hb'])

                def load(g):
                    b = g % 2
                    sl = slice(g * TT, (g + 1) * TT)
                    S.dma('sp', yt[b][:], y_d[sl, :].rearrange("(b p) c -> p b c", p=128), writes=[('yt', b)])
                    S.dma('sp', ut[b][:], u_d[sl, :].rearrange("(b p) c -> p b c", p=128), writes=[('ut', b)])
                    S.dma('sp', x0t[b][:], x0_d[sl, :].rearrange("(b p) c -> p b c", p=128), writes=[('x0t', b)])

                load(0)
                for g in range(T // TT):
                    b = g % 2
                    if g + 1 < T // TT:
                        load(g + 1)
                    for blk in range(4):
                        S.pool(lambda: G.tensor_tensor(out=o1[:], in0=ut[b][:, blk, :], in1=hb[:], op=ALU.mult),
                               [('ut', b), 'hb', 'o1'], ['o1'])
                        S.dve(lambda: V.tensor_tensor(out=o1[:], in0=o1[:], in1=yt[b][:, blk, :], op=ALU.add),
                              ['o1', ('yt', b)], ['o1'])
                        S.dve(lambda: V.tensor_tensor(out=o1[:], in0=o1[:], in1=x0t[b][:, blk, :], op=ALU.mult),
                              ['o1', ('x0t', b)], ['o1'])
                        S.pool(lambda: G.tensor_tensor(out=osq[:], in0=o1[:], in1=o1[:], op=ALU.mult), ['o1'], ['osq'])
                        S.dve(lambda: V.reduce_sum(out=ssq[:], in_=osq[:], axis=AX.X), ['osq'], ['ssq'])
                        S.dve(lambda: V.tensor_scalar(out=ssq[:], in0=ssq[:], scalar1=1.0 / 512, scalar2=RMS_EPS,
                                                      op0=ALU.mult, op1=ALU.add), ['ssq'], ['ssq'])
                        S.act(lambda: A.activation(out=ssq[:], in_=ssq[:], func=AF.Ln), ['ssq'], ['ssq'])
                        S.act(lambda: A.activation(out=ssq[:], in_=ssq[:], func=AF.Exp, scale=-0.5), ['ssq'], ['ssq'])
                        S.dve(lambda: V.scalar_tensor_tensor(out=onb[:], in0=o1[:], scalar=ssq[:, 0:1], in1=gH[:],
                                                             op0=ALU.mult, op1=ALU.mult),
                              ['o1', 'ssq', 'gH', 'onb'], ['onb'])
                        S.pe([(lambda c=c: P.transpose(ptr[:, c * 128:(c + 1) * 128], onb[:, c * 128:(c + 1) * 128],
                                                       ident_b[:])) for c in range(4)], ['onb'], ['ptr'])
                        S.act(lambda: A.copy(out=oTs[b][:, :, blk * 128:(blk + 1) * 128],
                                             in_=ptr[:].rearrange("p (c t) -> p c t", c=4)), ['ptr'], [('oTs', b)])
                    S.dma('sp', oT_v[:, 4:8, g * TT:(g + 1) * TT], oTs[b][:], reads=[('oTs', b)])
            S.barrier()

        phase_tin()
        for l in range(RUN_LAYERS):
            if STOP_AFTER == ('tin', l):
                break
            phase_mod(l)
            if STOP_AFTER == ('mod', l):
                break
            phase_ffn(l, 0)
            if STOP_AFTER == ('ffn1', l):
                break
            phase_m1(l)
            phase_attn(l)
            phase_hyena(l)
            phase_wout(l)
            if STOP_AFTER == ('mix', l):
                break
            phase_ffn(l, 1)
        phase_tout()
        S.finish()
        print("sched: instr counts", S.cnt, S.dcnt, "waits", S.nwait)
        global LAST_MARKS
        LAST_MARKS = S.marks
    return nc, consts


def kernel(x_prompt, x_sample, c_prompt, c_sample, ada_w, ada_b, ffn1_wi, ffn1_wo, ffn2_wi, ffn2_wo,
           ln_g, ln_b, w_in, w_out, sink, grp_norm_g, hy_conv_w, hy_conv_b, hy_w1, hy_b1, hy_w2, hy_b2,
           hy_w3, hy_b3, hy_w4, hy_freq, hy_decay, hy_bias):
    nc, consts = build()
    f = lambda a: np.ascontiguousarray(np.asarray(a, dtype=np.float32))
    f0 = f
    if RUN_LAYERS < DEPTH:
        f = lambda a: f0(np.asarray(a)[:RUN_LAYERS])
    shared = {
        "ada_w": f(ada_w), "ada_b": f(ada_b), "ffn1_wi": f(ffn1_wi), "ffn1_wo": f(ffn1_wo),
        "ffn2_wi": f(ffn2_wi), "ffn2_wo": f(ffn2_wo), "ln_g": f(ln_g), "ln_b": f(ln_b),
        "w_in": f(w_in), "w_out": f(w_out), "sink": f(sink), "grp_norm_g": f(grp_norm_g),
        "hy_conv_w": f(hy_conv_w), "hy_conv_b": f(hy_conv_b), "hy_w1": f(hy_w1), "hy_b1": f(hy_b1),
        "hy_w2": f(hy_w2), "hy_b2": f(hy_b2), "hy_w3": f(hy_w3), "hy_b3": f(hy_b3), "hy_w4": f(hy_w4),
        "hy_freq": f(hy_freq), "hy_decay": f(hy_decay), "hy_bias": f(hy_bias),
    }
    for k, v in consts.items():
        shared["c_" + k] = v
    xp = f0(x_prompt)
    xs = f0(x_sample)
    cp = f0(c_prompt)
    cs_ = f0(c_sample)
    in_maps = []
    for c in range(NCORES):
        m = dict(shared)
        m["xin"] = np.concatenate([xp[2 * c].reshape(4096, D), xp[2 * c + 1].reshape(4096, D),
                                   xs[2 * c].reshape(2048, D), xs[2 * c + 1].reshape(2048, D)], axis=0)
        m["cin"] = np.concatenate([cp[2 * c:2 * c + 2], cs_[2 * c:2 * c + 2]], axis=0)
        in_maps.append(m)
    res = run_bass_kernel_spmd(nc, in_maps, core_ids=list(range(NCORES)))
    yp = np.empty((16, 4096, D), np.float32)
    ys = np.empty((16, 2048, D), np.float32)
    for c in range(NCORES):
        y = res.results[c]["yout"]
        yp[2 * c] = y[0:4096]
        yp[2 * c + 1] = y[4096:8192]
        ys[2 * c] = y[8192:10240]
        ys[2 * c + 1] = y[10240:12288]
    return yp, ys
```

```python
import math
from contextlib import ExitStack
import numpy as np
import ml_dtypes
import concourse.bass as bass
import concourse.mybir as mybir
from concourse.bass_utils import run_bass_kernel_spmd

F32 = mybir.dt.float32
BF16 = mybir.dt.bfloat16
AF = mybir.ActivationFunctionType
ALU = mybir.AluOpType
AX = mybir.AxisListType

D = 1024
DEPTH = 4
DFF = 2816
NIN = 2304
ALPHA = float((2 * DEPTH) ** 0.25)
LN_EPS = 1e-5 / (ALPHA * ALPHA)
RMS_EPS = 1e-6
NCORES = 8
SEQS = [(0, 4096), (4096, 4096), (8192, 2048), (10240, 2048)]
T = 12288
LS = [4096, 2048]
PI = math.pi

RUN_LAYERS = DEPTH
LAST_MARKS = None
FFN_DEBUG = 4
STOP_AFTER = None


def seq_of(tok):
    for i, (s0, ln) in enumerate(SEQS):
        if s0 <= tok < s0 + ln:
            return i, s0, ln
    raise ValueError


class Sched:
    NDMA = 8

    def __init__(self, nc, stack):
        self.nc = nc
        self.E = {'pe': nc.tensor, 'act': nc.scalar, 'dve': nc.vector, 'pool': nc.gpsimd, 'sp': nc.sync}
        self.sem = {e: stack.enter_context(nc.semaphore(f"s_{e}")) for e in ('pe', 'act', 'dve', 'pool')}
        self.cnt = {e: 0 for e in self.sem}
        self.nd = {'sp': 3, 'pool': 2}
        self.dsem = {q: [stack.enter_context(nc.semaphore(f"d_{q}{i}")) for i in range(self.nd[q])]
                     for q in ('sp', 'pool')}
        self.dcnt = {q: 0 for q in self.dsem}
        self.known = {e: {} for e in self.E}
        self.res = {}
        self.nwait = 0

    PS_NAMES = frozenset(['pt', 'pm', 'pst', 'gu', 'pf', 'psq', 'psr', 'psv', 'psz', 'ptr', 'pS', 'pO', 'pa', 'px',
                          'pb', 'py'])

    def _isp(self, k):
        if isinstance(k, tuple):
            return any(isinstance(x, str) and x in self.PS_NAMES for x in k)
        return k in self.PS_NAMES

    def _split(self, reads, writes):
        r2 = [k for k in reads if not self._isp(k)]
        w2 = list(writes) + [k for k in reads if self._isp(k)]
        return r2, w2

    def _wait(self, e, tok):
        if tok is None:
            return
        sem, val = tok
        k = self.known[e]
        sid = id(sem)
        if k.get(sid, 0) >= val:
            return
        self.E[e].wait_ge(sem, val)
        self.nwait += 1
        k[sid] = val

    def deps(self, e, reads, writes):
        for r in reads:
            st = self.res.get(r)
            if st is not None:
                self._wait(e, st[0])
        for w in writes:
            st = self.res.get(w)
            if st is not None:
                self._wait(e, st[0])
                for t in st[1]:
                    self._wait(e, t)

    def commit(self, tok, reads, writes):
        for r in reads:
            st = self.res.setdefault(r, [None, []])
            st[1].append(tok)
            if len(st[1]) > 10:
                d = {}
                for s, v in st[1]:
                    if id(s) not in d or d[id(s)][1] < v:
                        d[id(s)] = (s, v)
                st[1] = list(d.values())
        for w in writes:
            self.res[w] = [tok, []]

    def op(self, e, fn, reads=(), writes=()):
        reads, writes = self._split(reads, writes)
        self.deps(e, reads, writes)
        ins = fn()
        self.cnt[e] += 1
        ins.then_inc(self.sem[e], 1)
        tok = (self.sem[e], self.cnt[e])
        if e == 'pe':
            self.known[e][id(self.sem[e])] = self.cnt[e]
        self.commit(tok, reads, writes)
        return tok

    def pe(self, fns, reads=(), writes=()):
        reads, writes = self._split(reads, writes)
        self.deps('pe', reads, writes)
        ins = None
        for fn in fns:
            ins = fn()
        self.cnt['pe'] += 1
        ins.then_inc(self.sem['pe'], 1)
        tok = (self.sem['pe'], self.cnt['pe'])
        self.known['pe'][id(self.sem['pe'])] = self.cnt['pe']
        self.commit(tok, reads, writes)
        return tok

    def act(self, fn, r=(), w=()):
        return self.op('act', fn, r, w)

    def dve(self, fn, r=(), w=()):
        return self.op('dve', fn, r, w)

    def pool(self, fn, r=(), w=()):
        return self.op('pool', fn, r, w)

    def dma(self, q, out, in_, reads=(), writes=()):
        self.deps(q, reads, writes)
        j = self.dcnt[q]
        self.dcnt[q] += 1
        sem = self.dsem[q][j % self.nd[q]]
        rnd = j // self.nd[q]
        if rnd > 0:
            self._wait(q, (sem, 16 * rnd))
        self.E[q].dma_start(out=out, in_=in_).then_inc(sem, 16)
        tok = (sem, 16 * (rnd + 1))
        self.commit(tok, reads, writes)
        return tok

    def _all_tokens(self):
        toks = []
        for q in self.dsem:
            j = self.dcnt[q]
            for i in range(min(j, self.nd[q])):
                last = j - 1 - i
                toks.append((self.dsem[q][last % self.nd[q]], 16 * (last // self.nd[q] + 1)))
        for x in self.sem:
            if self.cnt[x] > 0:
                toks.append((self.sem[x], self.cnt[x]))
        return toks

    def barrier(self):
        import inspect
        if not hasattr(self, 'marks'):
            self.marks = []
        fr = inspect.stack()[1]
        self.marks.append((fr.function, fr.lineno, dict(self.cnt)))
        toks = self._all_tokens()
        for e in self.E:
            for t in toks:
                self._wait(e, t)
        self.res = {}

    def finish(self):
        for t in self._all_tokens():
            self._wait('sp', t)


def fft_consts(L):
    N = 2 * L
    N1 = N // 128
    H1 = N1 // 2
    nk1 = H1 + 1
    n1 = np.arange(H1)[:, None]
    k1 = np.arange(nk1)[None, :]
    ang = 2 * np.pi * n1 * k1 / N1
    FA = np.concatenate([np.cos(ang), -np.sin(ang)], axis=1)
    w = np.full(nk1, 2.0)
    w[0] = 1
    w[-1] = 1
    GA = np.concatenate([(w[None, :] * np.cos(ang)).T, (-w[None, :] * np.sin(ang)).T], axis=0)
    n2 = np.arange(128)[:, None]
    k2 = np.arange(128)[None, :]
    Ms = []
    for kk in range(nk1):
        th = 2 * np.pi * ((n2 * (kk + N1 * k2)) % N) / N
        Mre = np.cos(th)
        Mim = -np.sin(th)
        Ms.append(np.stack([Mre, Mim, -Mim, Mre.T, Mim.T, -Mim.T], axis=1))
    M = np.stack(Ms, axis=0)
    return FA, GA, M


def host_consts():
    bf = ml_dtypes.bfloat16
    c = {}
    c["ident_f"] = np.eye(128, dtype=np.float32)
    c["ident_b"] = np.eye(128, dtype=np.float32).astype(bf)
    c["ones_b"] = np.full((128, 128), 1.0 / D, dtype=np.float32).astype(bf)
    rt = np.zeros((64, 16), np.float32)
    for i in range(8):
        rt[i + 8, i] = -1.0
        rt[i, i + 8] = 1.0
    c["rt"] = rt.astype(bf)
    inv = 500000.0 ** (-np.arange(0, 16, 2, dtype=np.float64) / 16)
    ang = np.arange(4096, dtype=np.float64)[None, :] * np.concatenate([inv, inv])[:, None]
    c["cos_t"] = np.cos(ang).astype(np.float32)
    c["sin_t"] = np.sin(ang).astype(np.float32)
    s = np.arange(128)[:, None]
    q = np.arange(128)[None, :]
    c["maskp"] = (s >= q).astype(np.float32).astype(bf)
    c["maskn"] = (s <= q).astype(np.float32).astype(bf)
    for i, L in enumerate(LS):
        FA, GA, M = fft_consts(L)
        c[f"fa{i}"] = FA.astype(np.float32).astype(bf)
        c[f"ga{i}"] = GA.astype(np.float32).astype(bf)
        c[f"mm{i}"] = M.astype(np.float32).astype(bf)
        t = np.linspace(0.0, 1.0, L, dtype=np.float32).astype(np.float64)
        wv = 2.0 * np.pi * np.arange(L, dtype=np.float64) / L
        fb = np.linspace(1e-4, 15, 16, dtype=np.float32).astype(np.float64)[None]
        z = np.concatenate([t[:, None], np.cos(fb * wv[:, None]), -np.sin(fb * wv[:, None])], axis=-1)
        c[f"zt{i}"] = np.ascontiguousarray(z.T).astype(np.float32)
        c[f"negt{i}"] = np.ascontiguousarray((-t).reshape(L // 128, 128).T).astype(np.float32)
    return c


def build():
    nc = bass.Bass("TRN2", target_bir_lowering=False)
    DEPTH = RUN_LAYERS
    consts = host_consts()

    def din(name, shape, dt=F32):
        return nc.dram_tensor(name, list(shape), dt, kind="ExternalInput").ap()

    def dscr(name, shape, dt=F32):
        return nc.dram_tensor(name, list(shape), dt).ap()

    xin = din("xin", [T, D])
    cin = din("cin", [4, D])
    ada_w = din("ada_w", [DEPTH, D, 9 * D])
    ada_b = din("ada_b", [DEPTH, 9 * D])
    ffn_wi = [din("ffn1_wi", [DEPTH, D, 2 * DFF]), din("ffn2_wi", [DEPTH, D, 2 * DFF])]
    ffn_wo = [din("ffn1_wo", [DEPTH, DFF, D]), din("ffn2_wo", [DEPTH, DFF, D])]
    ln_g = din("ln_g", [DEPTH, 3, D])
    ln_b = din("ln_b", [DEPTH, 3, D])
    w_in = din("w_in", [DEPTH, D, NIN])
    w_out = din("w_out", [DEPTH, D, D])
    sink = din("sink", [DEPTH, 8])
    grp_g = din("grp_norm_g", [DEPTH, D])
    hy_cw = din("hy_conv_w", [DEPTH, 3, 1536])
    hy_cb = din("hy_conv_b", [DEPTH, 1536])
    hy_w1 = din("hy_w1", [DEPTH, 33, 64])
    hy_b1 = din("hy_b1", [DEPTH, 64])
    hy_w2 = din("hy_w2", [DEPTH, 64, 64])
    hy_b2 = din("hy_b2", [DEPTH, 64])
    hy_w3 = din("hy_w3", [DEPTH, 64, 64])
    hy_b3 = din("hy_b3", [DEPTH, 64])
    hy_w4 = din("hy_w4", [DEPTH, 64, 1024])
    hy_freq = din("hy_freq", [DEPTH, 64])
    hy_decay = din("hy_decay", [DEPTH, 2, 512])
    hy_bias = din("hy_bias", [DEPTH, 512])
    cd = {}
    for k, v in consts.items():
        cd[k] = din("c_" + k, v.shape, BF16 if v.dtype == ml_dtypes.bfloat16 else F32)
    yout = nc.dram_tensor("yout", [T, D], F32, kind="ExternalOutput").ap()

    xT_d = dscr("xT_d", [D, T])
    q_d = dscr("q_d", [10, 64, T], BF16)
    v_d = dscr("v_d", [T, 130], BF16)
    u_d = dscr("u_d", [T, 512], BF16)
    x0_d = dscr("x0_d", [T, 512], BF16)
    y_d = dscr("y_d", [T, 512], F32)
    oT_d = dscr("oT_d", [D, T], BF16)
    filt_d = [dscr(f"filt_d{i}", [2, L, 512], BF16) for i, L in enumerate(LS)]
    NK1 = [L // 128 + 1 for L in LS]
    A_d = [dscr(f"A_d{i}", [2, 2 * NK1[i], 128, 512], BF16) for i in range(2)]
    B_d = [dscr(f"B_d{i}", [2, 2 * NK1[i], 128, 512], BF16) for i in range(2)]
    Kf_d = [dscr(f"Kf_d{i}", [NK1[i], 128, 2, 512], F32) for i in range(2)]

    xT_v = xT_d.rearrange("(k p) t -> p k t", p=128)
    oT_v = oT_d.rearrange("(k p) t -> p k t", p=128)

    top = ExitStack()
    with top:
        S = Sched(nc, top)
        uid = [0]

        def sb(st, shape, dt, name=None):
            uid[0] += 1
            return st.enter_context(nc.sbuf_tensor(f"{name or 't'}_{uid[0]}", list(shape), dt))

        def ps(st, shape, dt=F32, name=None):
            uid[0] += 1
            return st.enter_context(nc.psum_tensor(f"{name or 'p'}_{uid[0]}", list(shape), dt))

        V, G, A, P = nc.vector, nc.gpsimd, nc.scalar, nc.tensor

        ident_f = sb(top, [128, 128], F32, "identf")
        ident_b = sb(top, [128, 128], BF16, "identb")
        ones_b = sb(top, [128, 128], BF16, "onesb")
        rt_sb = sb(top, [64, 16], BF16, "rt")
        maskp = sb(top, [128, 128], BF16, "maskp")
        maskn = sb(top, [128, 128], BF16, "maskn")
        scT = sb(top, [128, 8, 4], F32, "scT")
        scT_b = sb(top, [128, 8, 4], BF16, "scTb")
        modsb = sb(top, [128, 72, 4], F32, "modsb")
        adab = sb(top, [128, DEPTH, 72], F32, "adab")
        lng = sb(top, [128, DEPTH * 3, 8], F32, "lng")
        nlng = sb(top, [128, DEPTH * 3, 8], F32, "nlng")
        lnb = sb(top, [128, DEPTH * 3, 8], F32, "lnb")
        cw = sb(top, [128, DEPTH * 3, 12], F32, "cw")
        cbs = sb(top, [128, DEPTH, 12], F32, "cbs")
        nsink = sb(top, [128, DEPTH * 8], F32, "nsink")
        eps_ln = sb(top, [128, 1], F32, "epsln")
        S.dve(lambda: V.memset(eps_ln[:], LN_EPS), [], ['epsln'])

        with nc.allow_non_contiguous_dma(reason="tiny one-time parameter loads"):
            S.dma('sp', ident_f[:], cd["ident_f"], writes=['c0'])
            S.dma('sp', ident_b[:], cd["ident_b"], writes=['c1'])
            S.dma('sp', ones_b[:], cd["ones_b"], writes=['c2'])
            S.dma('sp', rt_sb[:], cd["rt"], writes=['c3'])
            S.dma('sp', maskp[:], cd["maskp"], writes=['c4'])
            S.dma('sp', maskn[:], cd["maskn"], writes=['c5'])
            for s_ in range(4):
                S.dma('sp', scT[:, :, s_], cin[s_].rearrange("(k p) -> p k", p=128), writes=[('scT', s_)])
            for l_ in range(DEPTH):
                S.dma('sp', adab[:, l_, :], ada_b[l_].rearrange("(j p) -> p j", p=128), writes=[('adab', l_)])
                S.dma('sp', cbs[:, l_, :], hy_cb[l_].rearrange("(c p) -> p c", p=128), writes=[('cbs', l_)])
                for m_ in range(3):
                    S.dma('sp', lng[:, l_ * 3 + m_, :], ln_g[l_, m_].rearrange("(k p) -> p k", p=128),
                          writes=[('lng', l_, m_)])
                    S.dma('sp', lnb[:, l_ * 3 + m_, :], ln_b[l_, m_].rearrange("(k p) -> p k", p=128),
                          writes=[('lnb', l_, m_)])
                    S.dma('sp', cw[:, l_ * 3 + m_, :], hy_cw[l_, m_].rearrange("(c p) -> p c", p=128),
                          writes=[('cw', l_, m_)])
            S.dma('sp', nsink[:], sink.rearrange("l h -> (l h)").partition_broadcast(128), writes=['nsink'])
        S.barrier()
        S.act(lambda: A.activation(out=scT_b[:], in_=scT[:], func=AF.Silu), ['scT'], ['scTb'])
        S.dve(lambda: V.tensor_scalar(out=nlng[:], in0=lng[:], scalar1=-1.0, scalar2=None, op0=ALU.mult),
              ['lng'], ['nlng'])
        S.dve(lambda: V.tensor_scalar(out=nsink[:], in0=nsink[:], scalar1=-1.0, scalar2=None, op0=ALU.mult),
              ['nsink'], ['nsink'])
        S.barrier()

        def phase_tin():
            with ExitStack() as st:
                xtok = [sb(st, [128, 4, D], F32, "xtok") for _ in range(2)]
                xo = [sb(st, [128, 8, 512], F32, "xo") for _ in range(2)]
                pt = [ps(st, [128, 512], F32, "pt") for _ in range(3)]
                n = 0
                for g in range(T // 512):
                    b = g % 2
                    S.dma('sp', xtok[b][:], xin[g * 512:(g + 1) * 512, :].rearrange("(j p) d -> p j d", p=128),
                          writes=[('xtok', b)])
                    for k in range(8):
                        pb = n % 3
                        n += 1
                        S.pe([(lambda j=j: P.transpose(pt[pb][:, j * 128:(j + 1) * 128],
                                                        xtok[b][:, j, k * 128:(k + 1) * 128], ident_f[:]))
                              for j in range(4)], [('xtok', b)], [('pt', pb)])
                        if k % 2 == 0:
                            S.act(lambda: A.copy(out=xo[b][:, k, :], in_=pt[pb][:]), [('pt', pb)], [('xo', b, k)])
                        else:
                            S.dve(lambda: V.tensor_copy(out=xo[b][:, k, :], in_=pt[pb][:]), [('pt', pb)], [('xo', b, k)])
                    S.dma('sp', xT_v[:, :, g * 512:(g + 1) * 512], xo[b][:],
                          reads=[('xo', b, k) for k in range(8)])
            S.barrier()

        def phase_tout():
            with ExitStack() as st:
                xo = [sb(st, [128, 8, 512], F32, "xo") for _ in range(2)]
                ytok = [sb(st, [128, 4, D], F32, "ytok") for _ in range(2)]
                pt = [ps(st, [128, 512], F32, "pt") for _ in range(4)]
                n = 0
                for g in range(T // 512):
                    b = g % 2
                    S.dma('sp', xo[b][:], xT_v[:, :, g * 512:(g + 1) * 512], writes=[('xo', b)])
                    for j in range(4):
                        for hf in range(2):
                            pb = n % 4
                            n += 1
                            S.pe([(lambda kk=kk: P.transpose(pt[pb][:, kk * 128:(kk + 1) * 128],
                                                              xo[b][:, hf * 4 + kk, j * 128:(j + 1) * 128], ident_f[:]))
                                  for kk in range(4)], [('xo', b)], [('pt', pb)])
                            if hf == 0:
                                S.act(lambda: A.copy(out=ytok[b][:, j, 0:512], in_=pt[pb][:]),
                                      [('pt', pb)], [('ytok', b, j, 0)])
                            else:
                                S.dve(lambda: V.tensor_copy(out=ytok[b][:, j, 512:1024], in_=pt[pb][:]),
                                      [('pt', pb)], [('ytok', b, j, 1)])
                    S.dma('sp', yout[g * 512:(g + 1) * 512, :].rearrange("(j p) d -> p j d", p=128), ytok[b][:],
                          reads=[('ytok', b, j, h) for j in range(4) for h in range(2)])
            S.barrier()

        def phase_mod(l):
            with ExitStack() as st:
                wt = [sb(st, [128, 8, 1152], BF16, "adaw") for _ in range(2)]
                pm = ps(st, [128, 72, 4], F32, "pm")
                src = ada_w[l].rearrange("(k p) f -> p k f", p=128)
                for cg in range(8):
                    b = cg % 2
                    S.dma('pool', wt[b][:], src[:, :, cg * 1152:(cg + 1) * 1152], writes=[('adaw', b)])
                    for jj in range(9):
                        j = cg * 9 + jj
                        S.pe([(lambda k=k: P.matmul(pm[:, j, :], wt[b][:, k, jj * 128:(jj + 1) * 128], scT_b[:, k, :],
                                                    start=(k == 0), stop=(k == 7))) for k in range(8)],
                             [('adaw', b), 'scTb'], ['pm'])
                allpm = ['pm']
                S.dve(lambda: V.tensor_tensor(out=modsb[:], in0=pm[:],
                                              in1=adab[:, l, :].unsqueeze(2).to_broadcast([128, 72, 4]), op=ALU.add),
                      allpm + ['modsb'], ['modsb'])
                for m, cf in ((1, None), (4, None), (7, None), (2, 0.5 / ALPHA), (5, 1.0 / ALPHA), (8, 0.5 / ALPHA)):
                    sl = modsb[:, m * 8:(m + 1) * 8, :]
                    if cf is None:
                        S.dve(lambda: V.tensor_scalar(out=sl, in0=sl, scalar1=1.0, scalar2=None, op0=ALU.add),
                              ['modsb'], ['modsb'])
                    else:
                        S.dve(lambda: V.tensor_scalar(out=sl, in0=sl, scalar1=1.0, scalar2=cf, op0=ALU.add,
                                                      op1=ALU.mult), ['modsb'], ['modsb'])
            S.barrier()

        def ln_tail(st_objs, l, m, b, xt, rb, rq, pst, tmps, TT, rbk='rb', rqk='rq'):
            mean_sb, var, mr = tmps[0:3]
            S.pe([(lambda k=k: P.matmul(pst[:, 0:TT], ones_b[:], rb[:, k, :], start=(k == 0), stop=(k == 7)))
                  for k in range(8)] +
                 [(lambda k=k: P.matmul(pst[:, TT:2 * TT], ones_b[:], rq[:, k, :], start=(k == 0), stop=(k == 7)))
                  for k in range(8)], [rbk, rqk], ['pst'])
            S.dve(lambda: V.tensor_copy(out=mean_sb[:], in_=pst[:, 0:TT]), ['pst'], ['mean'])
            S.dve(lambda: V.tensor_tensor(out=var[:], in0=mean_sb[:], in1=mean_sb[:], op=ALU.mult), ['mean'], ['var'])
            S.dve(lambda: V.tensor_tensor(out=var[:], in0=pst[:, TT:2 * TT], in1=var[:], op=ALU.subtract),
                  ['pst', 'var'], ['var'])
            S.act(lambda: A.activation(out=var[:], in_=var[:], func=AF.Sqrt, bias=eps_ln[:, 0:1], scale=1.0),
                  ['var'], ['var'])
            S.dve(lambda: V.reciprocal(out=var[:], in_=var[:]), ['var'], ['var'])
            S.dve(lambda: V.tensor_tensor(out=mr[:], in0=mean_sb[:], in1=var[:], op=ALU.mult), ['mean', 'var'], ['mr'])
            li = l * 3 + m
            for dk in range(8):
                x = xt[b][:, dk, :]
                S.dve(lambda: V.scalar_tensor_tensor(out=x, in0=x, scalar=lng[:, li, dk:dk + 1], in1=var[:],
                                                     op0=ALU.mult, op1=ALU.mult), [('xt', b, dk), 'var'], [('xt', b, dk)])
                tq = tmps[3 + dk % 2]
                S.pool(lambda: G.tensor_scalar(out=tq[:], in0=mr[:], scalar1=nlng[:, li, dk:dk + 1],
                                               scalar2=lnb[:, li, dk:dk + 1], op0=ALU.mult, op1=ALU.add),
                       ['mr'], [('tq', dk % 2)])
                S.pool(lambda: G.tensor_tensor(out=x, in0=x, in1=tq[:], op=ALU.add), [('xt', b, dk), ('tq', dk % 2)],
                       [('xt', b, dk)])

        def phase_ffn(l, which):
            TT = 256
            NT = T // TT
            jS, jB, jG = (1, 0, 2) if which == 0 else (7, 6, 8)
            lnm = 0 if which == 0 else 2
            with ExitStack() as st:
                wi = sb(st, [128, 8, 2 * DFF], BF16, "wi")
                wo = sb(st, [128, 22, D], BF16, "wo")
                xt = [sb(st, [128, 8, TT], F32, "xt") for _ in range(3)]
                hT = [sb(st, [128, 8, TT], BF16, "hT") for _ in range(2)]
                aT = sb(st, [128, 22, TT], BF16, "aT")
                rb = sb(st, [128, 8, TT], BF16, "rb")
                rq = sb(st, [128, 8, TT], BF16, "rq")
                sg = [sb(st, [128, TT], BF16, "sg") for _ in range(2)]
                tmps = [sb(st, [128, TT], F32, "lnt") for _ in range(5)]
                gu = [ps(st, [128, 512], F32, "gu") for _ in range(3)]
                pf = [ps(st, [128, 512], F32, "pf") for _ in range(4)]
                pst = ps(st, [128, 512], F32, "pst")
                wsrc = ffn_wi[which][l].rearrange("(k p) f -> p k f", p=128)
                osrc = ffn_wo[which][l].rearrange("(c p) d -> p c d", p=128)
                for cg in range(11):
                    S.dma('pool', wi[:, :, cg * 256:(cg + 1) * 256], wsrc[:, :, cg * 256:(cg + 1) * 256],
                          writes=[('wig', cg)])
                    S.dma('pool', wi[:, :, DFF + cg * 256:DFF + (cg + 1) * 256],
                          wsrc[:, :, DFF + cg * 256:DFF + (cg + 1) * 256], writes=[('wiu', cg)])
                for cg in range(11):
                    S.dma('pool', wo[:, cg * 2:cg * 2 + 2, :], osrc[:, cg * 2:cg * 2 + 2, :], writes=[('wo', cg)])

                def load(t):
                    S.dma('sp', xt[t % 3][:], xT_v[:, :, t * TT:(t + 1) * TT], writes=[('xt', t % 3, k_) for k_ in range(8)])

                def modulate(t):
                    b = t % 2
                    xb = t % 3
                    s = seq_of(t * TT)[0]
                    for k in range(8):
                        S.dve(lambda: V.tensor_scalar(out=hT[b][:, k, :], in0=xt[xb][:, k, :],
                                                      scalar1=modsb[:, jS * 8 + k, s:s + 1],
                                                      scalar2=modsb[:, jB * 8 + k, s:s + 1],
                                                      op0=ALU.mult, op1=ALU.add), [('xt', xb, k)], [('hT', b)])

                def gu_part(t):
                    b = t % 2
                    for c in range(22):
                        g = gu[c % 3]
                        S.pe([(lambda k=k: P.matmul(g[:, 0:TT], wi[:, k, c * 128:(c + 1) * 128], hT[b][:, k, :],
                                                    start=(k == 0), stop=(k == 7))) for k in range(8)] +
                             [(lambda k=k: P.matmul(g[:, TT:2 * TT], wi[:, k, DFF + c * 128:DFF + (c + 1) * 128],
                                                    hT[b][:, k, :], start=(k == 0), stop=(k == 7))) for k in range(8)],
                             [('hT', b), ('wig', c // 2), ('wiu', c // 2)], [('gu', c % 3)])
                        S.act(lambda: A.activation(out=sg[c % 2][:], in_=g[:, 0:TT], func=AF.Silu),
                              [('gu', c % 3)], [('sg', c % 2)])
                        S.dve(lambda: V.tensor_tensor(out=aT[:, c, :], in0=g[:, TT:2 * TT], in1=sg[c % 2][:],
                                                      op=ALU.mult), [('gu', c % 3), ('sg', c % 2)], [('aT', c)])

                def down_part(t):
                    b = t % 3
                    s = seq_of(t * TT)[0]
                    for dp in range(4):
                        bank = pf[dp]
                        S.pe([(lambda c=c, h_=h_: P.matmul(bank[:, h_ * TT:(h_ + 1) * TT],
                                                           wo[:, c, (2 * dp + h_) * 128:(2 * dp + h_ + 1) * 128],
                                                           aT[:, c, :], start=(c == 0), stop=(c == 21)))
                              for h_ in range(2) for c in range(22)],
                             [('aT', c) for c in range(22)] + [('wo', cg) for cg in range(11)], [('pf', dp)])
                        for h_ in range(2):
                            dk = 2 * dp + h_
                            pp = bank[:, h_ * TT:(h_ + 1) * TT]
                            x = xt[b][:, dk, :]
                            S.dve(lambda: V.scalar_tensor_tensor(out=x, in0=pp, scalar=modsb[:, jG * 8 + dk, s:s + 1],
                                                                 in1=x, op0=ALU.mult, op1=ALU.add),
                                  [('pf', dp), ('xt', b, dk)], [('xt', b, dk)])
                            S.pool(lambda: G.tensor_copy(out=rb[:, dk, :], in_=x), [('xt', b, dk)], ['rb'])
                            S.pool(lambda: G.tensor_tensor(out=rq[:, dk, :], in0=x, in1=x, op=ALU.mult),
                                   [('xt', b, dk)], ['rq'])

                def tail_part(t):
                    b = t % 3
                    ln_tail(None, l, lnm, b, xt, rb, rq, pst, tmps, TT)
                    S.dma('sp', xT_v[:, :, t * TT:(t + 1) * TT], xt[b][:], reads=[('xt', b, k_) for k_ in range(8)])

                for t in range(min(3, NT)):
                    load(t)
                modulate(0)
                gu_part(0)
                if NT > 1:
                    modulate(1)
                down_part(0)
                for t in range(NT):
                    if t + 1 < NT:
                        gu_part(t + 1)
                    if t + 2 < NT:
                        modulate(t + 2)
                    tail_part(t)
                    if t + 3 < NT:
                        load(t + 3)
                    if t + 1 < NT:
                        down_part(t + 1)
            S.barrier()

        def phase_wout(l):
            TT = 256
            NT = T // TT
            with ExitStack() as st:
                wo = sb(st, [128, 8, D], BF16, "wout")
                xt = [sb(st, [128, 8, TT], F32, "xt") for _ in range(3)]
                oT = [sb(st, [128, 8, TT], BF16, "oT") for _ in range(3)]
                rb = [sb(st, [128, 8, TT], BF16, "rb") for _ in range(2)]
                rq = [sb(st, [128, 8, TT], BF16, "rq") for _ in range(2)]
                tmps = [sb(st, [128, TT], F32, "lnt") for _ in range(5)]
                pf = [ps(st, [128, 512], F32, "pf") for _ in range(4)]
                pst = ps(st, [128, 512], F32, "pst")
                S.dma('pool', wo[:], w_out[l].rearrange("(k p) d -> p k d", p=128), writes=['wout'])

                def load(t):
                    S.dma('sp', xt[t % 3][:], xT_v[:, :, t * TT:(t + 1) * TT], writes=[('xt', t % 3, k_) for k_ in range(8)])
                    S.dma('sp', oT[t % 3][:], oT_v[:, :, t * TT:(t + 1) * TT], writes=[('oT', t % 3)])

                def mm(t):
                    b = t % 3
                    b2 = t % 2
                    s = seq_of(t * TT)[0]
                    for dp in range(4):
                        bank = pf[dp]
                        S.pe([(lambda k=k, h_=h_: P.matmul(bank[:, h_ * TT:(h_ + 1) * TT],
                                                           wo[:, k, (2 * dp + h_) * 128:(2 * dp + h_ + 1) * 128],
                                                           oT[b][:, k, :], start=(k == 0), stop=(k == 7)))
                              for h_ in range(2) for k in range(8)], [('oT', b), 'wout'], [('pf', dp)])
                        for h_ in range(2):
                            dk = 2 * dp + h_
                            pp = bank[:, h_ * TT:(h_ + 1) * TT]
                            x = xt[b][:, dk, :]
                            S.dve(lambda: V.scalar_tensor_tensor(out=x, in0=pp, scalar=modsb[:, 5 * 8 + dk, s:s + 1],
                                                                 in1=x, op0=ALU.mult, op1=ALU.add),
                                  [('pf', dp), ('xt', b, dk)], [('xt', b, dk)])
                            S.pool(lambda: G.tensor_copy(out=rb[b2][:, dk, :], in_=x), [('xt', b, dk)], [('rb', b2)])
                            S.pool(lambda: G.tensor_tensor(out=rq[b2][:, dk, :], in0=x, in1=x, op=ALU.mult),
                                   [('xt', b, dk)], [('rq', b2)])

                def tail(t):
                    b = t % 3
                    b2 = t % 2
                    ln_tail(None, l, 1, b, xt, rb[b2], rq[b2], pst, tmps, TT, rbk=('rb', b2), rqk=('rq', b2))
                    S.dma('sp', xT_v[:, :, t * TT:(t + 1) * TT], xt[b][:], reads=[('xt', b, k_) for k_ in range(8)])

                for t in range(min(3, NT)):
                    load(t)
                mm(0)
                for t in range(NT):
                    if t + 1 < NT:
                        mm(t + 1)
                    tail(t)
                    if t + 3 < NT:
                        load(t + 3)
            S.barrier()

        def phase_m1(l):
            TT = 512
            NT = T // TT
            with ExitStack() as st:
                win = sb(st, [128, 8, NIN], BF16, "win")
                xt = [sb(st, [128, 8, TT + 2], F32, "xt") for _ in range(2)]
                hT = [sb(st, [128, 8, TT + 2], BF16, "hT") for _ in range(2)]
                qb = [sb(st, [64, TT], BF16, "qb") for _ in range(2)]
                t1 = sb(st, [16, TT], F32, "t1")
                t2 = sb(st, [16, TT], F32, "t2")
                cs = [sb(st, [16, 2, TT], F32, "cs") for _ in range(2)]
                vx = [sb(st, [128, 4, 2, 65], BF16, "vx") for _ in range(2)]
                zs = [sb(st, [128, TT + 2], F32, "zs") for _ in range(2)]
                zc = [sb(st, [128, TT], F32, "zc") for _ in range(2)]
                x0b = sb(st, [128, 4, TT], BF16, "x0b")
                ub = sb(st, [128, 4, TT], BF16, "ub")
                tok0b = [sb(st, [128, 4, 512], BF16, "tok0") for _ in range(2)]
                psq = [ps(st, [64, 512], F32, "psq")] * 2
                psr = ps(st, [16, 512], F32, "psr")
                psv = ps(st, [128, 512], F32, "psv")
                psz = [ps(st, [128, 512], F32, "psz") for _ in range(2)]
                pszh = [ps(st, [128, 16], F32, "pszh") for _ in range(2)]
                ptr = ps(st, [128, 512], BF16, "ptr")
                S.dma('pool', win[:, :, 0:768], w_in[l].rearrange("(k p) f -> p k f", p=128)[:, :, 0:768],
                      writes=['win0'])
                for i in range(3):
                    S.dma('pool', win[:, :, 768 + i * 512:768 + (i + 1) * 512],
                          w_in[l].rearrange("(k p) f -> p k f", p=128)[:, :, 768 + i * 512:768 + (i + 1) * 512],
                          writes=[('win', i)])
                for b in range(2):
                    S.pool(lambda: G.memset(vx[b][:], 1.0), [], [('vx', b)])
                wkeys = ['win0'] + [('win', i) for i in range(3)]

                def load(g):
                    b = g % 2
                    tok0 = g * TT
                    s, s0, ln = seq_of(tok0)
                    lo = 1 if tok0 == s0 else 0
                    hi = 1 if tok0 + TT == s0 + ln else 0
                    S.dma('sp', xt[b][:, :, lo:TT + 2 - hi], xT_v[:, :, tok0 - 1 + lo:tok0 + TT + 1 - hi],
                          writes=[('xt', b)])
                    S.dma('sp', cs[b][:, 0, :], cd["cos_t"][:, tok0 - s0:tok0 - s0 + TT], writes=[('cs', b, 0)])
                    S.dma('sp', cs[b][:, 1, :], cd["sin_t"][:, tok0 - s0:tok0 - s0 + TT], writes=[('cs', b, 1)])

                def modulate(g):
                    b = g % 2
                    tok0 = g * TT
                    s, s0, ln = seq_of(tok0)
                    lo = 1 if tok0 == s0 else 0
                    hi = 1 if tok0 + TT == s0 + ln else 0
                    for k in range(8):
                        S.dve(lambda: V.tensor_scalar(out=hT[b][:, k, lo:TT + 2 - hi], in0=xt[b][:, k, lo:TT + 2 - hi],
                                                      scalar1=modsb[:, 4 * 8 + k, s:s + 1],
                                                      scalar2=modsb[:, 3 * 8 + k, s:s + 1],
                                                      op0=ALU.mult, op1=ALU.add), [('xt', b)], [('hT', b)])
                    if lo:
                        S.dve(lambda: V.memset(hT[b][:, :, 0:1], 0.0), [], [('hT', b)])
                    if hi:
                        S.dve(lambda: V.memset(hT[b][:, :, TT + 1:TT + 2], 0.0), [], [('hT', b)])

                load(0)
                modulate(0)
                nq = 0
                nz = 0
                for g in range(NT):
                    b = g % 2
                    tok0 = g * TT
                    if g + 1 < NT:
                        load(g + 1)
                    for hd in range(10):
                        qi = nq % 2
                        nq += 1
                        S.pe([(lambda k=k: P.matmul(psq[qi][:], win[:, k, hd * 64:(hd + 1) * 64], hT[b][:, k, 1:TT + 1],
                                                    start=(k == 0), stop=(k == 7))) for k in range(8)],
                             [('hT', b), 'win0'], ['psq'])
                        S.act(lambda: A.copy(out=qb[qi][:], in_=psq[qi][:]), ['psq'], [('qb', qi)])
                        S.pe([lambda: P.matmul(psr[:], rt_sb[:], qb[qi][:], start=True, stop=True)],
                             [('qb', qi)], ['psr'])
                        S.dve(lambda: V.tensor_tensor(out=t1[:], in0=psq[qi][0:16, :], in1=cs[b][:, 0, :], op=ALU.mult),
                              ['psq', ('cs', b, 0)], ['t1'])
                        S.dve(lambda: V.tensor_tensor(out=t2[:], in0=psr[:], in1=cs[b][:, 1, :], op=ALU.mult),
                              ['psr', ('cs', b, 1)], ['t2'])
                        S.pool(lambda: G.tensor_tensor(out=qb[qi][0:16, :], in0=t1[:], in1=t2[:], op=ALU.add),
                               ['t1', 't2', ('qb', qi)], [('qb', qi)])
                        S.dma('sp', q_d[hd, :, tok0:tok0 + TT], qb[qi][:], reads=[('qb', qi)])
                    for blk in range(4):
                        S.pe([(lambda k=k: P.matmul(psv[:, blk * 128:(blk + 1) * 128],
                                                    hT[b][:, k, 1 + blk * 128:1 + (blk + 1) * 128],
                                                    win[:, k, 640:768], start=(k == 0), stop=(k == 7)))
                              for k in range(8)], [('hT', b), 'win0'], ['psv'])
                    S.act(lambda: A.copy(out=vx[b][:, :, :, 0:64],
                                         in_=psv[:].rearrange("p (b k e) -> p b k e", b=4, k=2)),
                          ['psv', ('vx', b)], [('vx', b)])
                    S.dma('sp', v_d[tok0:tok0 + TT, :].rearrange("(b p) e -> p b e", p=128),
                          vx[b][:].rearrange("p b k e -> p b (k e)"), reads=[('vx', b)])
                    for i in range(4):
                        for part in range(3):
                            cz = part * 4 + i
                            zi = nz % 2
                            nz += 1
                            c0 = 768 + cz * 128
                            S.pe([(lambda k=k: P.matmul(psz[zi][:], win[:, k, c0:c0 + 128], hT[b][:, k, 1:TT + 1],
                                                        start=(k == 0), stop=(k == 7))) for k in range(8)] +
                                 [(lambda k=k: P.matmul(pszh[zi][:, 0:2], win[:, k, c0:c0 + 128],
                                                        hT[b][:, k, 0:TT + 2:TT + 1],
                                                        start=(k == 0), stop=(k == 7))) for k in range(8)],
                                 [('hT', b)] + wkeys, [('psz', zi)])
                            S.act(lambda: A.copy(out=zs[zi][:, 1:TT + 1], in_=psz[zi][:]), [('psz', zi)], [('zs', zi)])
                            S.act(lambda: A.copy(out=zs[zi][:, 0:TT + 2:TT + 1], in_=pszh[zi][:, 0:2]),
                                  [('psz', zi)], [('zs', zi)])
                            zz = zc[part % 2] if part > 0 else zc[0]
                            zkey = ('zc', part % 2 if part > 0 else 0)
                            S.pool(lambda: G.tensor_scalar(out=zz[:], in0=zs[zi][:, 1:TT + 1],
                                                           scalar1=cw[:, l * 3 + 1, cz:cz + 1],
                                                           scalar2=cbs[:, l, cz:cz + 1], op0=ALU.mult, op1=ALU.add),
                                   [('zs', zi)], [zkey])
                            S.dve(lambda: V.scalar_tensor_tensor(out=zz[:], in0=zs[zi][:, 0:TT],
                                                                  scalar=cw[:, l * 3 + 0, cz:cz + 1], in1=zz[:],
                                                                  op0=ALU.mult, op1=ALU.add), [('zs', zi), zkey], [zkey])
                            if part == 0:
                                S.dve(lambda: V.scalar_tensor_tensor(out=x0b[:, i, :], in0=zs[zi][:, 2:TT + 2],
                                                                     scalar=cw[:, l * 3 + 2, cz:cz + 1], in1=zz[:],
                                                                     op0=ALU.mult, op1=ALU.add),
                                      [('zs', zi), zkey], [('x0b', i)])
                            else:
                                S.dve(lambda: V.scalar_tensor_tensor(out=zz[:], in0=zs[zi][:, 2:TT + 2],
                                                                     scalar=cw[:, l * 3 + 2, cz:cz + 1], in1=zz[:],
                                                                     op0=ALU.mult, op1=ALU.add),
                                      [('zs', zi), zkey], [zkey])
                        S.pool(lambda: G.tensor_tensor(out=ub[:, i, :], in0=zc[0][:], in1=zc[1][:], op=ALU.mult),
                               [('zc', 0), ('zc', 1)], [('ub', i)])
                    for ti, (srcb, dst, key) in enumerate(((x0b, x0_d, 'x0b'), (ub, u_d, 'ub'))):
                        tb = tok0b[ti]
                        for blk in range(4):
                            S.pe([(lambda i=i: P.transpose(ptr[:, i * 128:(i + 1) * 128],
                                                           srcb[:, i, blk * 128:(blk + 1) * 128], ident_b[:]))
                                  for i in range(4)], [(key, i) for i in range(4)], ['ptr'])
                            S.act(lambda: A.copy(out=tb[:, blk, :], in_=ptr[:]), ['ptr'], [('tokb', ti)])
                        S.dma('sp', dst[tok0:tok0 + TT, :].rearrange("(b p) c -> p b c", p=128), tb[:],
                              reads=[('tokb', ti)])
                    if g + 1 < NT:
                        modulate(g + 1)
            S.barrier()

        def phase_attn(l):
            TT = 512
            NT = T // TT
            with ExitStack() as st:
                qT = [sb(st, [64, 8, TT], BF16, "qT") for _ in range(2)]
                kT = [sb(st, [64, 2, TT + 256], BF16, "kT") for _ in range(2)]
                vv = [sb(st, [128, 6, 130], BF16, "vv") for _ in range(2)]
                pT = [sb(st, [128, 3, 128], BF16, "pT") for _ in range(3)]
                gA = sb(st, [128, 512], F32, "gA")
                den = sb(st, [128, 8], F32, "den")
                osb = sb(st, [128, 8, 64], F32, "osb")
                osq = sb(st, [128, 512], F32, "osq")
                ssq = sb(st, [128, 1], F32, "ssq")
                onb = sb(st, [128, 512], BF16, "onb")
                oTs = [sb(st, [128, 4, TT], BF16, "oTs") for _ in range(2)]
                pS = [ps(st, [128, 3, 128], F32, "pS") for _ in range(3)]
                pO = [ps(st, [128, 4, 65], F32, "pO") for _ in range(4)]
                ptr = ps(st, [128, 512], BF16, "ptr")
                S.dma('sp', gA[:], grp_g[l, 0:512].partition_broadcast(128), writes=['gA'])

                def load(g):
                    b = g % 2
                    tok0 = g * TT
                    s, s0, ln = seq_of(tok0)
                    lo = 128 if tok0 == s0 else 0
                    hi = 128 if tok0 + TT == s0 + ln else 0
                    S.dma('sp', qT[b][:], q_d[0:8, :, tok0:tok0 + TT].rearrange("h p t -> p h t"), writes=[('qT', b)])
                    S.dma('sp', kT[b][:, :, lo:TT + 256 - hi],
                          q_d[8:10, :, tok0 - 128 + lo:tok0 + TT + 128 - hi].rearrange("h p t -> p h t"),
                          writes=[('kT', b)])
                    S.dma('sp', vv[b][:, lo // 128:6 - hi // 128, :],
                          v_d[tok0 - 128 + lo:tok0 + TT + 128 - hi, :].rearrange("(b p) e -> p b e", p=128),
                          writes=[('vv', b)])

                def emit_norm(b, qb_, po, pok, tok0, do_store):
                    hk = [pok + (0,), pok + (1,)]
                    for hh in range(2):
                        S.dve(lambda: V.tensor_scalar(out=den[:, hh * 4:(hh + 1) * 4], in0=po[hh][:, :, 64],
                                                      scalar1=1.0, scalar2=None, op0=ALU.add), hk + ['den'], ['den'])
                    S.dve(lambda: V.reciprocal(out=den[:], in_=den[:]), ['den'], ['den'])
                    for hh in range(2):
                        S.dve(lambda: V.tensor_tensor(out=osb[:, hh * 4:(hh + 1) * 4, :], in0=po[hh][:, :, 0:64],
                                                      in1=den[:, hh * 4:(hh + 1) * 4].unsqueeze(2).to_broadcast(
                                                          [128, 4, 64]), op=ALU.mult),
                              hk + ['den', 'osb'], ['osb'])
                    of = osb[:].rearrange("p h e -> p (h e)")
                    S.pool(lambda: G.tensor_tensor(out=osq[:], in0=of, in1=of, op=ALU.mult), ['osb'], ['osq'])
                    S.dve(lambda: V.reduce_sum(out=ssq[:], in_=osq[:], axis=AX.X), ['osq'], ['ssq'])
                    S.dve(lambda: V.tensor_scalar(out=ssq[:], in0=ssq[:], scalar1=1.0 / 512, scalar2=RMS_EPS,
                                                  op0=ALU.mult, op1=ALU.add), ['ssq'], ['ssq'])
                    S.act(lambda: A.activation(out=ssq[:], in_=ssq[:], func=AF.Ln), ['ssq'], ['ssq'])
                    S.act(lambda: A.activation(out=ssq[:], in_=ssq[:], func=AF.Exp, scale=-0.5), ['ssq'], ['ssq'])
                    S.dve(lambda: V.scalar_tensor_tensor(out=onb[:], in0=of, scalar=ssq[:, 0:1], in1=gA[:],
                                                         op0=ALU.mult, op1=ALU.mult),
                          ['osb', 'ssq', 'gA', 'onb'], ['onb'])
                    S.pe([(lambda c=c: P.transpose(ptr[:, c * 128:(c + 1) * 128], onb[:, c * 128:(c + 1) * 128],
                                                   ident_b[:])) for c in range(4)], ['onb'], ['ptr'])
                    S.act(lambda: A.copy(out=oTs[b][:, :, qb_ * 128:(qb_ + 1) * 128],
                                         in_=ptr[:].rearrange("p (c t) -> p c t", c=4)), ['ptr'], [('oTs', b)])
                    if do_store:
                        S.dma('sp', oT_v[:, 0:4, tok0:tok0 + TT], oTs[b][:], reads=[('oTs', b)])

                jobs = []
                npi = 0
                npo = 0
                for g in range(NT):
                    b = g % 2
                    tok0 = g * TT
                    s, s0, ln = seq_of(tok0)
                    for qb_ in range(4):
                        gpos = tok0 + qb_ * 128
                        jlo = 1 if gpos == s0 else 0
                        jhi = 2 if gpos + 128 == s0 + ln else 3
                        po = (pO[(npo % 2) * 2], pO[(npo % 2) * 2 + 1])
                        pok = ('pO', npo % 2)
                        npo += 1
                        for h in range(8):
                            jobs.append(dict(g=g, b=b, tok0=tok0, qb=qb_, h=h, kv=h // 4, pi=npi % 3, jlo=jlo, jhi=jhi,
                                             po=po, pok=pok))
                            npi += 1

                def scores(j):
                    b, qb_, h, kv, pi, jlo, jhi = j['b'], j['qb'], j['h'], j['kv'], j['pi'], j['jlo'], j['jhi']
                    S.pe([(lambda jj=jj: P.matmul(pS[pi][:, jj, :],
                                                  kT[b][:, kv, (qb_ + jj) * 128:(qb_ + jj + 1) * 128],
                                                  qT[b][:, h, qb_ * 128:(qb_ + 1) * 128], start=True, stop=True))
                          for jj in range(jlo, jhi)], [('qT', b), ('kT', b)], [('pS', pi)])

                def softmax_pv(j):
                    b, qb_, h, kv, pi, jlo, jhi = j['b'], j['qb'], j['h'], j['kv'], j['pi'], j['jlo'], j['jhi']
                    po, pok = j['po'], j['pok']
                    S.act(lambda: A.activation(out=pT[pi][:, jlo:jhi, :], in_=pS[pi][:, jlo:jhi, :], func=AF.Exp,
                                               bias=nsink[:, l * 8 + h:l * 8 + h + 1], scale=0.125),
                          [('pS', pi)], [('pT', pi)])
                    if jlo == 0:
                        S.pool(lambda: G.tensor_tensor(out=pT[pi][:, 0, :], in0=pT[pi][:, 0, :], in1=maskp[:],
                                                       op=ALU.mult), [('pT', pi)], [('pT', pi)])
                    if jhi == 3:
                        S.pool(lambda: G.tensor_tensor(out=pT[pi][:, 2, :], in0=pT[pi][:, 2, :], in1=maskn[:],
                                                       op=ALU.mult), [('pT', pi)], [('pT', pi)])
                    S.pe([(lambda jj=jj: P.matmul(po[h // 4][:, h % 4, :], pT[pi][:, jj, :],
                                                  vv[b][:, qb_ + jj, kv * 65:(kv + 1) * 65],
                                                  start=(jj == jlo), stop=(jj == jhi - 1)))
                          for jj in range(jlo, jhi)], [('pT', pi), ('vv', b)], [pok + (h // 4,)])

                load(0)
                if NT > 1:
                    load(1)
                pending = None
                scores(jobs[0])
                for i, j in enumerate(jobs):
                    if i + 1 < len(jobs):
                        scores(jobs[i + 1])
                    softmax_pv(j)
                    if j['h'] == 7:
                        if pending is not None:
                            emit_norm(*pending)
                        pending = (j['b'], j['qb'], j['po'], j['pok'], j['tok0'], j['qb'] == 3)
                        if j['qb'] == 3 and j['g'] + 2 < NT:
                            load(j['g'] + 2)
                if pending is not None:
                    emit_norm(*pending)
            S.barrier()

        def fft_stage_a(st, li, srcs, dstA, tag):
            L = LS[li]
            H1 = L // 128
            nk = 2 * NK1[li]
            fa = sb(st, [H1, nk], BF16, "fa")
            S.dma('sp', fa[:], cd[f"fa{li}"], writes=[(tag, 'fa')])
            xc = [sb(st, [H1, 16, 512], BF16, "xc") for _ in range(2)]
            asb = [sb(st, [nk, 16, 512], BF16, "asb") for _ in range(2)]
            pa = [ps(st, [nk, 512], F32, "pa") for _ in range(3)]
            n = 0
            it = 0
            for sq in range(2):
                sv = srcs[sq].rearrange("(a b) c -> a b c", b=128)
                for ch in range(8):
                    bb = it % 2
                    it += 1
                    S.dma('sp', xc[bb][:], sv[:, ch * 16:(ch + 1) * 16, :], writes=[(tag, 'xc', bb)])
                    for j in range(16):
                        pi = n % 3
                        n += 1
                        S.pe([lambda: P.matmul(pa[pi][:], fa[:], xc[bb][:, j, :], start=True, stop=True)],
                             [(tag, 'fa'), (tag, 'xc', bb)], [(tag, 'pa', pi)])
                        if j % 2 == 0:
                            S.act(lambda: A.copy(out=asb[bb][:, j, :], in_=pa[pi][:]), [(tag, 'pa', pi)],
                                  [(tag, 'asb', bb, j)])
                        else:
                            S.dve(lambda: V.tensor_copy(out=asb[bb][:, j, :], in_=pa[pi][:]), [(tag, 'pa', pi)],
                                  [(tag, 'asb', bb, j)])
                    S.dma('sp', dstA[sq, :, ch * 16:(ch + 1) * 16, :], asb[bb][:],
                          reads=[(tag, 'asb', bb, j) for j in range(16)])

        def fft_stage_c(st, li, srcA, mode, dstB, tag):
            nk1 = NK1[li]
            mk = [sb(st, [128, 6, 128], BF16, "mk") for _ in range(2)]
            ak = [sb(st, [128, 2, 512], BF16, "ak") for _ in range(2)]
            kf = [sb(st, [128, 2, 512], F32, "kf") for _ in range(2)]
            tt = [sb(st, [128, 512], F32, "tt") for _ in range(8)]
            yb = [sb(st, [128, 2, 512], BF16, "yb") for _ in range(2)]
            bsb = [sb(st, [128, 2, 512], BF16, "bsb") for _ in range(2)]
            px = [ps(st, [128, 512], F32, "px") for _ in range(4)]
            pb = [ps(st, [128, 512], F32, "pb") for _ in range(4)]
            jobs = [(k1, sq) for k1 in range(nk1) for sq in range(2)]

            def xpart(i):
                k1, sq = jobs[i]
                mb = k1 % 2
                ab = i % 2
                if sq == 0:
                    S.dma('sp', mk[mb][:], cd[f"mm{li}"][k1], writes=[(tag, 'mk', mb)])
                    if mode == 'conv':
                        S.dma('sp', kf[mb][:], Kf_d[li][k1], writes=[(tag, 'kf', mb)])
                av = srcA[sq].rearrange("(r k) n c -> n r k c", r=2)
                S.dma('sp', ak[ab][:], av[:, :, k1, :], writes=[(tag, 'ak', ab)])
                pr, pi_ = px[ab * 2], px[ab * 2 + 1]
                S.pe([lambda: P.matmul(pr[:], mk[mb][:, 0, :], ak[ab][:, 0, :], start=True, stop=False),
                      lambda: P.matmul(pr[:], mk[mb][:, 2, :], ak[ab][:, 1, :], start=False, stop=True),
                      lambda: P.matmul(pi_[:], mk[mb][:, 1, :], ak[ab][:, 0, :], start=True, stop=False),
                      lambda: P.matmul(pi_[:], mk[mb][:, 0, :], ak[ab][:, 1, :], start=False, stop=True)],
                     [(tag, 'mk', mb), (tag, 'ak', ab)], [(tag, 'px', ab)])

            def ypart(i):
                k1, sq = jobs[i]
                mb = k1 % 2
                ab = i % 2
                pr, pi_ = px[ab * 2], px[ab * 2 + 1]
                kk = kf[mb]
                if mode == 'filter':
                    if sq == 0:
                        S.act(lambda: A.copy(out=kk[:, 0, :], in_=pr[:]), [(tag, 'px', ab)], [(tag, 'kf', mb)])
                        S.dve(lambda: V.tensor_copy(out=kk[:, 1, :], in_=pi_[:]), [(tag, 'px', ab)],
                              [(tag, 'kf', mb)])
                    else:
                        S.dve(lambda: V.tensor_tensor(out=kk[:, 0, :], in0=pr[:], in1=kk[:, 0, :], op=ALU.add),
                              [(tag, 'px', ab), (tag, 'kf', mb)], [(tag, 'kf', mb)])
                        S.dve(lambda: V.scalar_tensor_tensor(out=kk[:, 1, :], in0=pi_[:], scalar=-1.0,
                                                             in1=kk[:, 1, :], op0=ALU.mult, op1=ALU.add),
                              [(tag, 'px', ab), (tag, 'kf', mb)], [(tag, 'kf', mb)])
                        S.dma('sp', Kf_d[li][k1], kk[:], reads=[(tag, 'kf', mb)])
                    return
                S.dve(lambda: V.tensor_tensor(out=tt[ab * 4 + 0][:], in0=pr[:], in1=kk[:, 0, :], op=ALU.mult),
                      [(tag, 'px', ab), (tag, 'kf', mb)], [(tag, 'tt', ab * 4 + 0)])
                S.dve(lambda: V.tensor_tensor(out=tt[ab * 4 + 1][:], in0=pi_[:], in1=kk[:, 1, :], op=ALU.mult),
                      [(tag, 'px', ab), (tag, 'kf', mb)], [(tag, 'tt', ab * 4 + 1)])
                S.pool(lambda: G.tensor_tensor(out=yb[ab][:, 0, :], in0=tt[ab * 4 + 0][:], in1=tt[ab * 4 + 1][:],
                                               op=ALU.subtract),
                       [(tag, 'tt', ab * 4 + 0), (tag, 'tt', ab * 4 + 1)], [(tag, 'yb', ab)])
                S.dve(lambda: V.tensor_tensor(out=tt[ab * 4 + 2][:], in0=pr[:], in1=kk[:, 1, :], op=ALU.mult),
                      [(tag, 'px', ab), (tag, 'kf', mb)], [(tag, 'tt', ab * 4 + 2)])
                S.dve(lambda: V.tensor_tensor(out=tt[ab * 4 + 3][:], in0=pi_[:], in1=kk[:, 0, :], op=ALU.mult),
                      [(tag, 'px', ab), (tag, 'kf', mb)], [(tag, 'tt', ab * 4 + 3)])
                S.pool(lambda: G.tensor_tensor(out=yb[ab][:, 1, :], in0=tt[ab * 4 + 2][:], in1=tt[ab * 4 + 3][:],
                                               op=ALU.add),
                       [(tag, 'tt', ab * 4 + 2), (tag, 'tt', ab * 4 + 3)], [(tag, 'yb', ab)])
                br, bi = pb[ab * 2], pb[ab * 2 + 1]
                S.pe([lambda: P.matmul(br[:], mk[mb][:, 3, :], yb[ab][:, 0, :], start=True, stop=False),
                      lambda: P.matmul(br[:], mk[mb][:, 4, :], yb[ab][:, 1, :], start=False, stop=True),
                      lambda: P.matmul(bi[:], mk[mb][:, 3, :], yb[ab][:, 1, :], start=True, stop=False),
                      lambda: P.matmul(bi[:], mk[mb][:, 5, :], yb[ab][:, 0, :], start=False, stop=True)],
                     [(tag, 'mk', mb), (tag, 'yb', ab)], [(tag, 'pb', ab)])
                S.act(lambda: A.copy(out=bsb[ab][:, 0, :], in_=br[:]), [(tag, 'pb', ab)], [(tag, 'bsb', ab, 0)])
                S.act(lambda: A.copy(out=bsb[ab][:, 1, :], in_=bi[:]), [(tag, 'pb', ab)], [(tag, 'bsb', ab, 1)])
                bv = dstB[sq].rearrange("(r k) n c -> n r k c", r=2)
                S.dma('sp', bv[:, :, k1, :], bsb[ab][:], reads=[(tag, 'bsb', ab, 0), (tag, 'bsb', ab, 1)])

            xpart(0)
            for i in range(len(jobs)):
                if i + 1 < len(jobs):
                    xpart(i + 1)
                ypart(i)

        def fft_stage_ah(st, li, srcB, dsts, tag):
            L = LS[li]
            H1 = L // 128
            nk = 2 * NK1[li]
            ga = sb(st, [nk, H1], BF16, "ga")
            S.dma('sp', ga[:], cd[f"ga{li}"], writes=[(tag, 'ga')])
            bc = [sb(st, [nk, 16, 512], BF16, "bc") for _ in range(2)]
            ysb = [sb(st, [H1, 16, 512], F32, "ysb") for _ in range(2)]
            py = [ps(st, [H1, 512], F32, "py") for _ in range(3)]
            n = 0
            it = 0
            for sq in range(2):
                dv = dsts[sq].rearrange("(a b) c -> a b c", b=128)
                for ch in range(8):
                    bb = it % 2
                    it += 1
                    S.dma('sp', bc[bb][:], srcB[sq, :, ch * 16:(ch + 1) * 16, :], writes=[(tag, 'bc', bb)])
                    for j in range(16):
                        pi = n % 3
                        n += 1
                        S.pe([lambda: P.matmul(py[pi][:], ga[:], bc[bb][:, j, :], start=True, stop=True)],
                             [(tag, 'ga'), (tag, 'bc', bb)], [(tag, 'py', pi)])
                        if j % 2 == 0:
                            S.act(lambda: A.mul(out=ysb[bb][:, j, :], in_=py[pi][:], mul=1.0 / (2 * L)),
                                  [(tag, 'py', pi)], [(tag, 'ysb', bb, j)])
                        else:
                            S.dve(lambda: V.tensor_scalar(out=ysb[bb][:, j, :], in0=py[pi][:], scalar1=1.0 / (2 * L),
                                                          scalar2=None, op0=ALU.mult),
                                  [(tag, 'py', pi)], [(tag, 'ysb', bb, j)])
                    S.dma('sp', dv[:, ch * 16:(ch + 1) * 16, :], ysb[bb][:],
                          reads=[(tag, 'ysb', bb, j) for j in range(16)])

        def phase_filter(l, li):
            L = LS[li]
            with ExitStack() as st:
                w1 = sb(st, [33, 64], F32, "w1")
                w2 = sb(st, [64, 64], F32, "w2")
                w3 = sb(st, [64, 64], F32, "w3")
                w4 = sb(st, [64, 1024], F32, "w4")
                fr = sb(st, [64, 1], F32, "fr")
                bb_ = sb(st, [64, 3], F32, "bb")
                dec = sb(st, [128, 1024], F32, "dec")
                negt = sb(st, [128, L // 128], F32, "negt")
                zt = [sb(st, [33, 512], F32, "zt") for _ in range(2)]
                hh = [sb(st, [64, 512], F32, "hh") for _ in range(3)]
                arg = sb(st, [64, 512], F32, "arg")
                kq = sb(st, [64, 512], F32, "kq")
                win_ = sb(st, [128, 1024], F32, "winw")
                fsb = [sb(st, [128, 1024], BF16, "fsb") for _ in range(2)]
                pm = [ps(st, [64, 512], F32, "pm") for _ in range(2)]
                pf = [ps(st, [128, 512], F32, "pf") for _ in range(4)]
                with nc.allow_non_contiguous_dma(reason="tiny filter params"):
                    S.dma('sp', w1[:], hy_w1[l], writes=['fw'])
                    S.dma('sp', w2[:], hy_w2[l], writes=['fw'])
                    S.dma('sp', w3[:], hy_w3[l], writes=['fw'])
                    S.dma('sp', w4[:], hy_w4[l], writes=['fw'])
                    S.dma('sp', fr[:], hy_freq[l].rearrange("(p o) -> p o", o=1), writes=['fr'])
                    S.dma('sp', bb_[:, 0:1], hy_b1[l].rearrange("(p o) -> p o", o=1), writes=['fb'])
                    S.dma('sp', bb_[:, 1:2], hy_b2[l].rearrange("(p o) -> p o", o=1), writes=['fb'])
                    S.dma('sp', bb_[:, 2:3], hy_b3[l].rearrange("(p o) -> p o", o=1), writes=['fb'])
                    S.dma('sp', dec[:], hy_decay[l].rearrange("a c -> (a c)").partition_broadcast(128), writes=['dec'])
                    S.dma('sp', negt[:], cd[f"negt{li}"], writes=['negt'])
                S.barrier()
                S.dve(lambda: V.tensor_scalar(out=bb_[:], in0=bb_[:], scalar1=fr[:, 0:1], scalar2=None, op0=ALU.mult),
                      ['fb', 'fr'], ['fb'])
                S.dve(lambda: V.scalar_tensor_tensor(out=dec[:], in0=dec[:], scalar=-1.0, in1=dec[:], op0=ALU.mult,
                                                     op1=ALU.max), ['dec'], ['dec'])
                ws = [w1, w2, w3]
                np_ = 0
                for ch in range(L // 512):
                    zb = ch % 2
                    S.dma('sp', zt[zb][:], cd[f"zt{li}"][:, ch * 512:(ch + 1) * 512], writes=[('zt', zb)])
                    cur = zt[zb]
                    ckey = ('zt', zb)
                    kdim = 33
                    for ly in range(3):
                        pi = np_ % 2
                        np_ += 1
                        S.pe([lambda: P.matmul(pm[pi][:], ws[ly][0:kdim, :], cur[0:kdim, :], start=True, stop=True)],
                             [ckey, 'fw'], [('pm', pi)])
                        S.dve(lambda: V.tensor_scalar(out=arg[:], in0=pm[pi][:], scalar1=fr[:, 0:1],
                                                      scalar2=bb_[:, ly:ly + 1], op0=ALU.mult, op1=ALU.add),
                              [('pm', pi), 'fb', 'fr'], ['arg'])
                        S.dve(lambda: V.tensor_scalar(out=kq[:], in0=arg[:], scalar1=1.0 / (2 * PI), scalar2=12582912.0,
                                                      op0=ALU.mult, op1=ALU.add), ['arg'], ['kq'])
                        S.dve(lambda: V.tensor_scalar(out=kq[:], in0=kq[:], scalar1=-12582912.0, scalar2=None,
                                                      op0=ALU.add), ['kq'], ['kq'])
                        S.dve(lambda: V.scalar_tensor_tensor(out=arg[:], in0=kq[:], scalar=-2 * PI, in1=arg[:],
                                                             op0=ALU.mult, op1=ALU.add), ['kq', 'arg'], ['arg'])
                        S.dve(lambda: V.tensor_scalar(out=arg[:], in0=arg[:], scalar1=PI, scalar2=-PI,
                                                      op0=ALU.min, op1=ALU.max), ['arg'], ['arg'])
                        S.act(lambda: A.activation(out=hh[ly][:], in_=arg[:], func=AF.Sin), ['arg'], [('hh', ly)])
                        cur = hh[ly]
                        ckey = ('hh', ly)
                        kdim = 64
                    for blk in range(4):
                        gb = ch * 4 + blk
                        fb = gb % 2
                        p0, p1 = pf[(gb % 2) * 2], pf[(gb % 2) * 2 + 1]
                        S.pe([lambda: P.matmul(p0[:], hh[2][:, blk * 128:(blk + 1) * 128], w4[:, 0:512],
                                               start=True, stop=True),
                              lambda: P.matmul(p1[:], hh[2][:, blk * 128:(blk + 1) * 128], w4[:, 512:1024],
                                               start=True, stop=True)], [('hh', 2), 'fw'], [('pf', gb % 2)])
                        S.act(lambda: A.activation(out=win_[:], in_=dec[:], func=AF.Exp, scale=negt[:, gb:gb + 1]),
                              ['dec', 'negt', 'winw'], ['winw'])
                        S.dve(lambda: V.tensor_tensor(out=fsb[fb][:, 0:512], in0=p0[:], in1=win_[:, 0:512], op=ALU.mult),
                              [('pf', gb % 2), 'winw'], [('fsb', fb, 0)])
                        S.dve(lambda: V.tensor_tensor(out=fsb[fb][:, 512:1024], in0=p1[:], in1=win_[:, 512:1024],
                                                      op=ALU.mult), [('pf', gb % 2), 'winw'], [('fsb', fb, 1)])
                        S.dma('sp', filt_d[li][0, gb * 128:(gb + 1) * 128, :], fsb[fb][:, 0:512],
                              reads=[('fsb', fb, 0)])
                        S.dma('sp', filt_d[li][1, gb * 128:(gb + 1) * 128, :], fsb[fb][:, 512:1024],
                              reads=[('fsb', fb, 1)])
            S.barrier()

        def phase_hyena(l):
            for li in range(2):
                phase_filter(l, li)
                with ExitStack() as st:
                    fft_stage_a(st, li, [filt_d[li][0], filt_d[li][1]], A_d[li], 'fa')
                S.barrier()
                with ExitStack() as st:
                    fft_stage_c(st, li, A_d[li], 'filter', None, 'fc')
                S.barrier()
                s0 = SEQS[li * 2][0]
                s1 = SEQS[li * 2 + 1][0]
                L = LS[li]
                with ExitStack() as st:
                    fft_stage_a(st, li, [u_d[s0:s0 + L, :], u_d[s1:s1 + L, :]], A_d[li], 'ua')
                S.barrier()
                with ExitStack() as st:
                    fft_stage_c(st, li, A_d[li], 'conv', B_d[li], 'uc')
                S.barrier()
                with ExitStack() as st:
                    fft_stage_ah(st, li, B_d[li], [y_d[s0:s0 + L, :], y_d[s1:s1 + L, :]], 'uh')
                S.barrier()
            with ExitStack() as st:
                TT = 512
                gH = sb(st, [128, 512], F32, "gH")
                hb = sb(st, [128, 512], F32, "hb")
                yt = [sb(st, [128, 4, 512], F32, "yt") for _ in range(2)]
                ut = [sb(st, [128, 4, 512], BF16, "ut") for _ in range(2)]
                x0t = [sb(st, [128, 4, 512], BF16, "x0t") for _ in range(2)]
                o1_ = [sb(st, [128, 512], F32, "o1") for _ in range(2)]
                osq_ = [sb(st, [128, 512], F32, "osq") for _ in range(2)]
                ssq_ = [sb(st, [128, 1], F32, "ssq") for _ in range(2)]
                onb_ = [sb(st, [128, 512], BF16, "onb") for _ in range(2)]
                oTs = [sb(st, [128, 4, TT], BF16, "oTs") for _ in range(2)]
                ptr = ps(st, [128, 512], BF16, "ptr")
                S.dma('sp', gH[:], grp_g[l, 512:1024].partition_broadcast(128), writes=['gH'])
                S.dma('sp', hb[:], hy_bias[l].partition_broadcast(128), writes=['hb'])

                def load(g):
                    b = g % 2
                    sl = slice(g * TT, (g + 1) * TT)
                    S.dma('sp', yt[b][:], y_d[sl, :].rearrange("(b p) c -> p b c", p=128), writes=[('yt', b)])
                    S.dma('sp', ut[b][:], u_d[sl, :].rearrange("(b p) c -> p b c", p=128), writes=[('ut', b)])
                    S.dma('sp', x0t[b][:], x0_d[sl, :].rearrange("(b p) c -> p b c", p=128), writes=[('x0t', b)])

                load(0)
                for g in range(T // TT):
                    b = g % 2
                    if g + 1 < T // TT:
                        load(g + 1)
                    for blk in range(4):
                        e_ = blk % 2
                        o1, osq, ssq, onb = o1_[e_], osq_[e_], ssq_[e_], onb_[e_]
                        S.pool(lambda: G.tensor_tensor(out=o1[:], in0=ut[b][:, blk, :], in1=hb[:], op=ALU.mult),
                               [('ut', b), 'hb', ('o1', e_)], [('o1', e_)])
                        S.dve(lambda: V.tensor_tensor(out=o1[:], in0=o1[:], in1=yt[b][:, blk, :], op=ALU.add),
                              [('o1', e_), ('yt', b)], [('o1', e_)])
                        S.dve(lambda: V.tensor_tensor(out=o1[:], in0=o1[:], in1=x0t[b][:, blk, :], op=ALU.mult),
                              [('o1', e_), ('x0t', b)], [('o1', e_)])
                        S.pool(lambda: G.tensor_tensor(out=osq[:], in0=o1[:], in1=o1[:], op=ALU.mult), [('o1', e_)], [('osq', e_)])
                        S.dve(lambda: V.reduce_sum(out=ssq[:], in_=osq[:], axis=AX.X), [('osq', e_)], [('ssq', e_)])
                        S.dve(lambda: V.tensor_scalar(out=ssq[:], in0=ssq[:], scalar1=1.0 / 512, scalar2=RMS_EPS,
                                                      op0=ALU.mult, op1=ALU.add), [('ssq', e_)], [('ssq', e_)])
                        S.act(lambda: A.activation(out=ssq[:], in_=ssq[:], func=AF.Ln), [('ssq', e_)], [('ssq', e_)])
                        S.act(lambda: A.activation(out=ssq[:], in_=ssq[:], func=AF.Exp, scale=-0.5), [('ssq', e_)], [('ssq', e_)])
                        S.dve(lambda: V.scalar_tensor_tensor(out=onb[:], in0=o1[:], scalar=ssq[:, 0:1], in1=gH[:],
                                                             op0=ALU.mult, op1=ALU.mult),
                              [('o1', e_), ('ssq', e_), 'gH', ('onb', e_)], [('onb', e_)])
                        S.pe([(lambda c=c: P.transpose(ptr[:, c * 128:(c + 1) * 128], onb[:, c * 128:(c + 1) * 128],
                                                       ident_b[:])) for c in range(4)], [('onb', e_)], ['ptr'])
                        S.act(lambda: A.copy(out=oTs[b][:, :, blk * 128:(blk + 1) * 128],
                                             in_=ptr[:].rearrange("p (c t) -> p c t", c=4)), ['ptr'], [('oTs', b)])
                    S.dma('sp', oT_v[:, 4:8, g * TT:(g + 1) * TT], oTs[b][:], reads=[('oTs', b)])
            S.barrier()

        phase_tin()
        for l in range(RUN_LAYERS):
            if STOP_AFTER == ('tin', l):
                break
            phase_mod(l)
            if STOP_AFTER == ('mod', l):
                break
            phase_ffn(l, 0)
            if STOP_AFTER == ('ffn1', l):
                break
            phase_m1(l)
            phase_attn(l)
            phase_hyena(l)
            phase_wout(l)
            if STOP_AFTER == ('mix', l):
                break
            phase_ffn(l, 1)
        phase_tout()
        S.finish()
        print("sched: instr counts", S.cnt, S.dcnt, "waits", S.nwait)
        global LAST_MARKS
        LAST_MARKS = S.marks
    return nc, consts


def kernel(x_prompt, x_sample, c_prompt, c_sample, ada_w, ada_b, ffn1_wi, ffn1_wo, ffn2_wi, ffn2_wo,
           ln_g, ln_b, w_in, w_out, sink, grp_norm_g, hy_conv_w, hy_conv_b, hy_w1, hy_b1, hy_w2, hy_b2,
           hy_w3, hy_b3, hy_w4, hy_freq, hy_decay, hy_bias):
    nc, consts = build()
    f = lambda a: np.ascontiguousarray(np.asarray(a, dtype=np.float32))
    f0 = f
    if RUN_LAYERS < DEPTH:
        f = lambda a: f0(np.asarray(a)[:RUN_LAYERS])
    shared = {
        "ada_w": f(ada_w), "ada_b": f(ada_b), "ffn1_wi": f(ffn1_wi), "ffn1_wo": f(ffn1_wo),
        "ffn2_wi": f(ffn2_wi), "ffn2_wo": f(ffn2_wo), "ln_g": f(ln_g), "ln_b": f(ln_b),
        "w_in": f(w_in), "w_out": f(w_out), "sink": f(sink), "grp_norm_g": f(grp_norm_g),
        "hy_conv_w": f(hy_conv_w), "hy_conv_b": f(hy_conv_b), "hy_w1": f(hy_w1), "hy_b1": f(hy_b1),
        "hy_w2": f(hy_w2), "hy_b2": f(hy_b2), "hy_w3": f(hy_w3), "hy_b3": f(hy_b3), "hy_w4": f(hy_w4),
        "hy_freq": f(hy_freq), "hy_decay": f(hy_decay), "hy_bias": f(hy_bias),
    }
    for k, v in consts.items():
        shared["c_" + k] = v
    xp = f0(x_prompt)
    xs = f0(x_sample)
    cp = f0(c_prompt)
    cs_ = f0(c_sample)
    in_maps = []
    for c in range(NCORES):
        m = dict(shared)
        m["xin"] = np.concatenate([xp[2 * c].reshape(4096, D), xp[2 * c + 1].reshape(4096, D),
                                   xs[2 * c].reshape(2048, D), xs[2 * c + 1].reshape(2048, D)], axis=0)
        m["cin"] = np.concatenate([cp[2 * c:2 * c + 2], cs_[2 * c:2 * c + 2]], axis=0)
        in_maps.append(m)
    res = run_bass_kernel_spmd(nc, in_maps, core_ids=list(range(NCORES)))
    yp = np.empty((16, 4096, D), np.float32)
    ys = np.empty((16, 2048, D), np.float32)
    for c in range(NCORES):
        y = res.results[c]["yout"]
        yp[2 * c] = y[0:4096]
        yp[2 * c + 1] = y[4096:8192]
        ys[2 * c] = y[8192:10240]
        ys[2 * c + 1] = y[10240:12288]
    return yp, ys
```
